# Optimizing a Trainium2 kernel written in Bass

```python
import math
import jax, jax.numpy as jnp
from jax import lax
import numpy as np

D_MODEL = 1024
BATCH = 32
SEQ = 2048
DEPTH = 2

MLA_HEADS = 8
MLA_NOPE = 64
MLA_ROPE = 32
MLA_QK = MLA_NOPE + MLA_ROPE
MLA_V = 64
Q_LORA = 384
KV_LORA = 256
ROPE_THETA = 10000.0
Q_BLOCK = 128
MLA_WIDTH = MLA_HEADS * MLA_V

HG_HEADS = 4
HG_DK = 128
HG_DV = 128
HG_CHUNK = 32
HG_KW = HG_HEADS * HG_DK
HG_WIDTH = HG_HEADS * HG_DV

D_FF = -(-8 * D_MODEL // (3 * 256)) * 256

NORM_EPS = 1e-6

IN_SPLITS = (Q_LORA, KV_LORA, MLA_ROPE, HG_KW, HG_KW, HG_WIDTH, HG_WIDTH, D_MODEL, D_MODEL)
D_IN = sum(IN_SPLITS)

kernel_name = "hybrid_mla_hgrn2_gated_merge"


def rms_norm(x, g):
    xf = x.astype(jnp.float32)
    y = xf * lax.rsqrt(jnp.mean(xf * xf, axis=-1, keepdims=True) + NORM_EPS)
    return (y * g.astype(jnp.float32)).astype(x.dtype)


def rope_tables(seq):
    pos = jnp.arange(seq, dtype=jnp.float32)
    inv_freq = 1.0 / (ROPE_THETA ** (jnp.arange(0, MLA_ROPE, 2, dtype=jnp.float32) / MLA_ROPE))
    ang = pos[:, None] * inv_freq[None, :]
    return jnp.cos(ang), jnp.sin(ang)


def apply_rope(x, cos, sin):
    half = x.shape[-1] // 2
    xf = x.astype(jnp.float32)
    x1, x2 = xf[..., :half], xf[..., half:]
    return jnp.concatenate([x1 * cos - x2 * sin, x1 * sin + x2 * cos], axis=-1).astype(x.dtype)


def mla_mixer(c_q, c_kv, k_pe, norm_cq, w_uq, norm_ckv, w_ukv, q_norm, k_norm, cos, sin):
    B, T, _ = c_q.shape
    q = (rms_norm(c_q, norm_cq) @ w_uq).reshape(B, T, MLA_HEADS, MLA_QK)
    kv = (rms_norm(c_kv, norm_ckv) @ w_ukv).reshape(B, T, MLA_HEADS, MLA_NOPE + MLA_V)
    k_nope, v = kv[..., :MLA_NOPE], kv[..., MLA_NOPE:]
    k = jnp.concatenate([k_nope, jnp.broadcast_to(k_pe[:, :, None, :], (B, T, MLA_HEADS, MLA_ROPE))], axis=-1)
    q = rms_norm(q, q_norm)
    k = rms_norm(k, k_norm)
    c, s = cos[:, None, :], sin[:, None, :]
    q = jnp.concatenate([q[..., :MLA_NOPE], apply_rope(q[..., MLA_NOPE:], c, s)], axis=-1)
    k = jnp.concatenate([k[..., :MLA_NOPE], apply_rope(k[..., MLA_NOPE:], c, s)], axis=-1)
    scale = MLA_QK ** -0.5
    outs = []
    for blk in range(T // Q_BLOCK):
        s0 = blk * Q_BLOCK
        s1 = s0 + Q_BLOCK
        qb, kb, vb = q[:, s0:s1], k[:, :s1], v[:, :s1]
        logits = jnp.einsum('bqhd,bkhd->bhqk', qb, kb).astype(jnp.float32) * scale
        mask = (s0 + jnp.arange(Q_BLOCK))[:, None] >= jnp.arange(s1)[None, :]
        logits = jnp.where(mask, logits, -jnp.inf)
        p = jax.nn.softmax(logits, axis=-1).astype(v.dtype)
        outs.append(jnp.einsum('bhqk,bkhd->bqhd', p, vb))
    return jnp.concatenate(outs, axis=1).reshape(B, T, MLA_WIDTH)


def hgrn2_mixer(q, f_logit, i, g, lb, out_norm):
    B, T, _ = q.shape
    N, C = T // HG_CHUNK, HG_CHUNK
    lbf = lb.astype(jnp.float32)
    log_f = jnp.logaddexp(jnp.log(lbf), jnp.log1p(-lbf) + jax.nn.log_sigmoid(f_logit.astype(jnp.float32)))
    k = -jnp.expm1(log_f)

    def to_chunks(a, d):
        return a.astype(jnp.float32).reshape(B, N, C, HG_HEADS, d).transpose(1, 0, 3, 2, 4)

    qc = to_chunks(q, HG_DK)
    kc = to_chunks(k, HG_DK)
    vc = to_chunks(i, HG_DV)
    bc = jnp.cumsum(to_chunks(log_f, HG_DK), axis=3)
    causal = jnp.arange(C)[:, None] >= jnp.arange(C)[None, :]

    def step(S, inp):
        qx, kx, vx, bx = inp
        b_last = bx[:, :, -1:, :]
        o_inter = jnp.einsum('bhck,bhkv->bhcv', qx * jnp.exp(bx), S)
        diff = bx[:, :, :, None, :] - bx[:, :, None, :, :]
        decay = jnp.exp(jnp.where(causal[:, :, None], diff, -jnp.inf))
        A = jnp.einsum('bhtk,bhsk,bhtsk->bhts', qx, kx, decay)
        o_intra = jnp.einsum('bhts,bhsv->bhtv', A, vx)
        S_new = jnp.exp(b_last[:, :, 0, :])[..., None] * S + jnp.einsum('bhsk,bhsv->bhkv', kx * jnp.exp(b_last - bx), vx)
        return S_new, o_inter + o_intra

    S0 = jnp.zeros((B, HG_HEADS, HG_DK, HG_DV), jnp.float32)
    _, o = lax.scan(step, S0, (qc, kc, vc, bc))
    o = o.transpose(1, 0, 3, 2, 4).reshape(B, T, HG_HEADS, HG_DV).astype(q.dtype)
    o = rms_norm(o, out_norm) * jax.nn.silu(g.reshape(B, T, HG_HEADS, HG_DV))
    return o.reshape(B, T, HG_WIDTH)


def setup_inputs(seed: int = 0) -> dict:
    key = jax.random.key(seed)
    ks = jax.random.split(key, 20)

    def w(k, shape, fan_in):
        return jax.random.normal(k, shape, jnp.float32) * fan_in ** -0.5

    def gain(k, shape):
        return 1.0 + 0.02 * jax.random.normal(k, shape, jnp.float32)

    L = DEPTH
    return {
        "x": jax.random.normal(ks[0], (BATCH, SEQ, D_MODEL), jnp.float32),
        "norm_mix": gain(ks[1], (L, D_MODEL)),
        "w_in": w(ks[2], (L, D_MODEL, D_IN), D_MODEL),
        "mla_norm_cq": gain(ks[3], (L, Q_LORA)),
        "mla_w_uq": w(ks[4], (L, Q_LORA, MLA_HEADS * MLA_QK), Q_LORA),
        "mla_norm_ckv": gain(ks[5], (L, KV_LORA)),
        "mla_w_ukv": w(ks[6], (L, KV_LORA, MLA_HEADS * (MLA_NOPE + MLA_V)), KV_LORA),
        "mla_q_norm": gain(ks[7], (L, MLA_QK)),
        "mla_k_norm": gain(ks[8], (L, MLA_QK)),
        "hg_lb_logits": 0.5 * jax.random.normal(ks[9], (L, HG_KW), jnp.float32),
        "hg_out_norm": gain(ks[10], (L, HG_DV)),
        "w_proj_a": w(ks[11], (L, MLA_WIDTH, D_MODEL), MLA_WIDTH),
        "w_proj_b": w(ks[12], (L, HG_WIDTH, D_MODEL), HG_WIDTH),
        "w_out": w(ks[13], (L, D_MODEL, D_MODEL), D_MODEL),
        "norm_ffn": gain(ks[14], (L, D_MODEL)),
        "w_gate": w(ks[15], (L, D_MODEL, D_FF), D_MODEL),
        "w_up": w(ks[16], (L, D_MODEL, D_FF), D_MODEL),
        "w_down": w(ks[17], (L, D_FF, D_MODEL), D_FF),
    }


def reference(x, norm_mix, w_in, mla_norm_cq, mla_w_uq, mla_norm_ckv, mla_w_ukv, mla_q_norm, mla_k_norm,
              hg_lb_logits, hg_out_norm, w_proj_a, w_proj_b, w_out, norm_ffn, w_gate, w_up, w_down):
    T = x.shape[1]
    cos, sin = rope_tables(T)
    p = jax.nn.softmax(hg_lb_logits.astype(jnp.float32), axis=0)
    lower_bounds = jnp.maximum(jnp.cumsum(p, axis=0) - p[0:1], 0.0)
    offsets = [int(o) for o in np.cumsum(IN_SPLITS)[:-1]]

    for l in range(DEPTH):
        h = rms_norm(x, norm_mix[l])
        z = h @ w_in[l]
        c_q, c_kv, k_pe, hq, hf, hi, hg, g_a, g_b = jnp.split(z, offsets, axis=-1)
        y_a = mla_mixer(c_q, c_kv, k_pe, mla_norm_cq[l], mla_w_uq[l], mla_norm_ckv[l], mla_w_ukv[l],
                        mla_q_norm[l], mla_k_norm[l], cos, sin) @ w_proj_a[l]
        y_b = hgrn2_mixer(hq, hf, hi, hg, lower_bounds[l], hg_out_norm[l]) @ w_proj_b[l]
        y = jax.nn.sigmoid(g_a) * y_a + jax.nn.sigmoid(g_b) * y_b
        x = x + y @ w_out[l]
        h = rms_norm(x, norm_ffn[l])
        x = x + (jax.nn.silu(h @ w_gate[l]) * (h @ w_up[l])) @ w_down[l]
    return x
```

```python
import math
from contextlib import ExitStack

import numpy as np
import concourse.bass as bass
import concourse.mybir as mybir
from concourse.bass_utils import run_bass_kernel_spmd

F32 = mybir.dt.float32
BF16 = mybir.dt.bfloat16
ALU = mybir.AluOpType
AF = mybir.ActivationFunctionType

D = 1024
KC = 8
DFF = 2816
NF = 22
TB = 512
EPS = 1e-6
NCORES = 8
SLOT = 2048
NSLOT = 2
ENGS = ("pe", "act", "dve", "pool", "sp")


class Buf:
    __slots__ = ("name", "w", "r")

    def __init__(self, name):
        self.name = name
        self.w = None
        self.r = {}


class Sched:
    def __init__(self, nc, stack):
        self.nc = nc
        self.stack = stack
        self.ops = {e: [] for e in ENGS}
        self.n = {e: 0 for e in ENGS}
        self.seen = {e: {} for e in ENGS}
        self.prog = {e: stack.enter_context(nc.semaphore("prog_" + e)) for e in ENGS}
        self.dma_sems = {}
        self.dma_cnt = {}

    def dma_sem(self, key):
        if key not in self.dma_sems:
            self.dma_sems[key] = self.stack.enter_context(self.nc.semaphore("d_" + key))
            self.dma_cnt[key] = 0
        return self.dma_sems[key]

    def _need(self, E, dep, waits):
        if dep[0] == "eng":
            _, E2, idx = dep
            if E2 == E and E == "pe":
                return
            key = ("eng", E2)
            if self.seen[E].get(key, 0) >= idx:
                return
            self.seen[E][key] = idx
            waits.append((self.prog[E2], idx))
        else:
            _, skey, cnt = dep
            key = ("dma", skey)
            if self.seen[E].get(key, 0) >= cnt:
                return
            self.seen[E][key] = cnt
            waits.append((self.dma_sems[skey], cnt))

    def _deps(self, E, reads, writes):
        waits = []
        for b in reads:
            if b.w is not None:
                self._need(E, b.w, waits)
        for b in writes:
            if b.w is not None:
                self._need(E, b.w, waits)
            for k, v in b.r.items():
                self._need(E, (k[0], k[1], v), waits)
        return waits

    def op(self, E, fn, reads=(), writes=()):
        waits = self._deps(E, reads, writes)
        self.n[E] += 1
        idx = self.n[E]
        self.ops[E].append((waits, fn, (self.prog[E], 1)))
        me = ("eng", E, idx)
        for b in reads:
            b.r[("eng", E)] = idx
        for b in writes:
            b.w = me
            b.r = {}
        return idx

    def dma(self, Q, fn, reads=(), writes=(), key=None):
        waits = self._deps(Q, reads, writes)
        sem = self.dma_sem(key)
        self.dma_cnt[key] += 16
        cnt = self.dma_cnt[key]
        self.ops[Q].append((waits, fn, (sem, 16)))
        me = ("dma", key, cnt)
        for b in reads:
            b.r[("dma", key)] = cnt
        for b in writes:
            b.w = me
            b.r = {}
        return cnt

    def final_wait(self, E, bufs):
        waits = []
        for b in bufs:
            if b.w is not None:
                self._need(E, b.w, waits)
            for k, v in b.r.items():
                self._need(E, (k[0], k[1], v), waits)
        self.ops[E].append((waits, None, None))

    def emit(self):
        nc = self.nc
        with nc.Block() as block:
            def run(E, eng):
                for waits, fn, inc in self.ops[E]:
                    for sem, val in waits:
                        eng.wait_ge(sem, val)
                    if fn is not None:
                        fn().then_inc(inc[0], inc[1])

            @block.tensor
            def _(e):
                run("pe", e)

            @block.scalar
            def _(e):
                run("act", e)

            @block.vector
            def _(e):
                run("dve", e)

            @block.gpsimd
            def _(e):
                run("pool", e)

            @block.sync
            def _(e):
                run("sp", e)


def weight_items():
    it = []
    for j in range(3):
        it.append((f"cq{j}", 8, 128))
    for j in range(2):
        it.append((f"ckv{j}", 8, 128))
    it.append(("kpe", 8, 64))
    for j in range(4):
        it.append((f"uk{j}", 2, 128))
    it.append(("uv", 2, 512))
    for h in range(8):
        it.append((f"uq{h}", 3, 128))
    for h in range(4):
        it.append((f"hf{h}", 8, 128))
        it.append((f"hq{h}", 8, 128))
        it.append((f"hi{h}", 8, 128))
        it.append((f"hg{h}", 8, 128))
    for o in range(8):
        it.append((f"pa{o}", 4, 128))
        it.append((f"pb{o}", 4, 128))
        it.append((f"ga{o}", 8, 128))
        it.append((f"gb{o}", 8, 128))
    for o in range(8):
        it.append((f"wo{o}", 8, 128))
    for f in range(NF):
        it.append((f"gate{f}", 8, 128))
        it.append((f"up{f}", 8, 128))
    for o in range(8):
        it.append((f"dna{o}", 11, 128))
        it.append((f"dnb{o}", 11, 128))
    return it


def weight_groups():
    groups = []
    index = {}
    cur_off = 0
    cur_size = 0
    start = 0
    for name, kc, m in weight_items():
        n = kc * m
        if cur_size + n > SLOT:
            groups.append((start, cur_size))
            start += cur_size
            cur_size = 0
        index[name] = (len(groups), cur_size, kc, m)
        cur_size += n
    groups.append((start, cur_size))
    total = start + cur_size
    total = ((total + 8191) // 8192) * 8192
    return groups, index, total


IN_OFF = {}
_o = 0
for _n, _s in (("cq", 384), ("ckv", 256), ("kpe", 32), ("hq", 512), ("hf", 512), ("hi", 512), ("hg", 512),
               ("ga", 1024), ("gb", 1024)):
    IN_OFF[_n] = _o
    _o += _s

NV = 40


def pack_layer_weights(inp, l):
    w_in = inp["w_in"][l]
    cols = {}

    def rng(a, n):
        return list(range(a, a + n))
    for j in range(3):
        cols[f"cq{j}"] = (w_in, rng(IN_OFF["cq"] + j * 128, 128))
    for j in range(2):
        cols[f"ckv{j}"] = (w_in, rng(IN_OFF["ckv"] + j * 128, 128))
    kp = IN_OFF["kpe"]
    cols["kpe"] = (w_in, rng(kp, 32) + rng(kp + 16, 16) + rng(kp, 16))
    ukv = inp["mla_w_ukv"][l]
    for j in range(4):
        cols[f"uk{j}"] = (ukv, rng((2 * j) * 128, 64) + rng((2 * j + 1) * 128, 64))
    vc = []
    for h in range(8):
        vc += rng(h * 128 + 64, 64)
    cols["uv"] = (ukv, vc)
    uq = inp["mla_w_uq"][l]
    for h in range(8):
        cols[f"uq{h}"] = (uq, rng(h * 96, 96) + rng(h * 96 + 80, 16) + rng(h * 96 + 64, 16))
    for h in range(4):
        for nm in ("hf", "hq", "hi", "hg"):
            cols[f"{nm}{h}"] = (w_in, rng(IN_OFF[nm] + h * 128, 128))
    for o in range(8):
        cols[f"pa{o}"] = (inp["w_proj_a"][l], rng(o * 128, 128))
        cols[f"pb{o}"] = (inp["w_proj_b"][l], rng(o * 128, 128))
        cols[f"ga{o}"] = (w_in, rng(IN_OFF["ga"] + o * 128, 128))
        cols[f"gb{o}"] = (w_in, rng(IN_OFF["gb"] + o * 128, 128))
        cols[f"wo{o}"] = (inp["w_out"][l], rng(o * 128, 128))
    for f in range(NF):
        cols[f"gate{f}"] = (inp["w_gate"][l], rng(f * 128, 128))
        cols[f"up{f}"] = (inp["w_up"][l], rng(f * 128, 128))
    wd = inp["w_down"][l]
    for o in range(8):
        cols[f"dna{o}"] = (wd[0:11 * 128], rng(o * 128, 128))
        cols[f"dnb{o}"] = (wd[11 * 128:22 * 128], rng(o * 128, 128))
    groups, index, total = weight_groups()
    out = np.zeros((128, total), np.float32)
    for name, kc, m in weight_items():
        W, cl = cols[name]
        g, off, _, _ = index[name]
        base = groups[g][0] + off
        blk = W[:kc * 128][:, cl].reshape(kc, 128, m).transpose(1, 0, 2).reshape(128, kc * m)
        out[:, base:base + kc * m] = blk
    return out


def pack_vecs(inp, l):
    v = np.zeros((128, NV), np.float32)
    v[:, 0:8] = inp["norm_mix"][l].reshape(8, 128).T
    v[:, 8:16] = inp["norm_ffn"][l].reshape(8, 128).T
    v[:, 16:19] = inp["mla_norm_cq"][l].reshape(3, 128).T
    v[:, 19:21] = inp["mla_norm_ckv"][l].reshape(2, 128).T
    qn = inp["mla_q_norm"][l]
    kn = inp["mla_k_norm"][l]
    v[0:96, 21] = qn
    v[96:112, 21] = qn[80:96]
    v[112:128, 21] = qn[64:80]
    v[0:64, 22] = kn[0:64]
    v[64:128, 22] = kn[0:64]
    v[64:96, 23] = kn[64:96]
    v[96:112, 23] = kn[80:96]
    v[112:128, 23] = kn[64:80]
    v[:, 24] = inp["hg_out_norm"][l]
    for ll in range(inp["hg_lb_logits"].shape[0]):
        v[:, 25 + 4 * ll:29 + 4 * ll] = inp["hg_lb_logits"][ll].reshape(4, 128).T
    return v


def const_tables(T):
    pos = np.arange(T, dtype=np.float32)
    inv_freq = (1.0 / (np.float32(10000.0) ** (np.arange(0, 32, 2, dtype=np.float32) / np.float32(32)))).astype(np.float32)
    ang = (pos[:, None] * inv_freq[None, :]).astype(np.float32)
    cos = np.cos(ang).astype(np.float32).T
    sin = np.sin(ang).astype(np.float32).T
    cs = np.ones((128, T), np.float32)
    cs[64:80] = cos
    cs[80:96] = cos
    cs[96:112] = -sin
    cs[112:128] = sin
    k = np.arange(128)
    tri_att = (k[None, :] >= k[:, None]).astype(np.float32)
    ident = np.eye(128, dtype=np.float32)
    s64 = k % 64
    t64 = np.arange(64)
    tri_h = (t64[None, :] >= s64[:, None]).astype(np.float32)
    tri_h4 = np.tile(tri_h, (1, 4))
    scanmask = np.ones((128, TB), np.float32)
    scanmask[:, ::64] = 0.0
    misc = np.concatenate([tri_att, ident, tri_h4, scanmask], axis=1).astype(np.float32)
    return cs, misc


def build_program(NSEQ, T, L, dbg=False):
    NBLK = T // TB
    NT = T // 128
    groups, windex, WTOT = weight_groups()
    NG = len(groups)
    nc = bass.Bass("TRN2", target_bir_lowering=False)
    xT_d = nc.dram_tensor("xT", [NSEQ, D, T], F32, kind="ExternalInput").ap()
    NWS = 1
    WPC = WTOT // NWS
    wp_d = [[nc.dram_tensor(f"wpack{l}_{i}", [128, WPC], F32, kind="ExternalInput").ap() for i in range(NWS)]
            for l in range(L)]
    vec_d = nc.dram_tensor("vecs", [128, L * NV], F32, kind="ExternalInput").ap()
    cs_d = nc.dram_tensor("cs", [128, T], F32, kind="ExternalInput").ap()
    misc_d = nc.dram_tensor("misc", [128, 512 + TB], F32, kind="ExternalInput").ap()
    out_d = nc.dram_tensor("outT", [NSEQ, D, T], F32, kind="ExternalOutput").ap()
    import os as _os
    if _os.environ.get("KDUMMY"):
        nc.dram_tensor("dummyin", [128, WTOT], F32, kind="ExternalInput")
    wbf_v = [wp_d[l][0].bitcast(BF16) for l in range(L)]

    with ExitStack() as st:
        S = Sched(nc, st)

        def sb(name, shape, dt):
            return st.enter_context(nc.sbuf_tensor(name, shape, dt))

        xT = sb("xT_sb", [128, KC, T], F32)
        XT = [[Buf(f"xT{c}_{b}") for b in range(NBLK)] for c in range(KC)]
        cs = sb("cs_sb", [128, T], F32); CS = Buf("cs")
        KT = sb("KT", [128, 8, T], BF16)
        KTB = [Buf(f"KT{h}") for h in range(8)]
        Vx = sb("Vx", [128, NT, 4, 192], BF16)
        VXB = [Buf(f"Vx{i}") for i in range(NT)]
        VXONES = Buf("vxones")
        skall = sb("skall", [128, NT * 8], F32)
        SK = [Buf(f"sk{b}") for b in range(NBLK)]
        wslot = [sb(f"wslot{i}", [128, SLOT], BF16) for i in range(NSLOT)]
        WS = [Buf(f"wslot{i}") for i in range(NSLOT)]
        vec = sb("vec_sb", [128, L * NV], F32); VEC = Buf("vec")
        cbf = sb("cbf", [128, 128 * 8], BF16); CBF = Buf("cbf")
        cvec = sb("cvec", [128, 8], F32); CVEC = Buf("cvec")
        lbt = sb("lbt", [128, L * 4], F32); LBT = Buf("lbt")
        lbtmp = sb("lbtmp", [128, 40], F32); LBTMP = Buf("lbtmp")
        gqs = sb("gqs", [128, L], F32); GQS = Buf("gqs")
        Sst = sb("Sst", [128, 4, 2, 128], F32)
        SST = [[Buf(f"S{h}_{i}") for i in range(2)] for h in range(4)]
        Sbf = sb("Sbf", [128, 4, 2, 128], BF16)
        SBF = [[Buf(f"Sbf{h}_{i}") for i in range(2)] for h in range(4)]
        hT = sb("hT", [128, KC, TB], BF16)
        HT = [Buf(f"hT{c}") for c in range(KC)]
        NSL = 29
        bfs = sb("bfs", [128, NSL, TB], BF16)
        BFS = [Buf(f"bfs{i}") for i in range(NSL)]
        yT, YT = bfs, BFS
        sqk, SQK = bfs, BFS
        cqn = bfs[:, 8:11, :]; CQN = BFS[8:11]
        ckvn = bfs[:, 11:13, :]; CKVN = BFS[11:13]
        qTh = [bfs[:, 13 + i, :] for i in range(2)]; QTH = BFS[13:15]
        pT = [bfs[:, 15 + i, :] for i in range(3)]; PT = BFS[15:18]
        qp = bfs[:, 18, :]; QP = BFS[18]
        kt_ = bfs[:, 19, :]; KTL = BFS[19]
        kpT = bfs[:, 20, :]; KPT = BFS[20]
        attnT = bfs[:, 21:25, :]; ATT = BFS[21:25]
        ogT = bfs[:, 25:29, :]; OGT = BFS[25:29]
        actT, ACTT = bfs, BFS
        sq = [sb(f"sq{i}", [128, TB], BF16) for i in range(2)]
        SQ = [Buf(f"sq{i}") for i in range(2)]
        rstd = sb("rstd", [128, TB], F32); RSTD = Buf("rstd")
        ft = [sb(f"ft{i}", [128, TB], F32) for i in range(6)]
        FT = [Buf(f"ft{i}") for i in range(6)]
        cqf = [ft[0], ft[1], ft[2]]; CQF = FT[0:3]
        rk, RK = ft[4], FT[4]
        rk2, RK2 = ft[5], FT[5]
        kptok = sb("kptok", [128, 4, 128], BF16); KPTOK = Buf("kptok")
        vtok = sb("vtok", [128, 4, 128], BF16); VTOK = Buf("vtok")
        asb2 = sb("asb2", [128, 4, 64], BF16); ASB2 = [Buf(f"asb2_{i}") for i in range(4)]
        dec = sb("dec", [128, 8], F32); DEC = Buf("dec")
        kst_sb = sb("kst_sb", [128, 32], F32); KSTSB = Buf("kstsb")
        scanm = sb("scanm", [128, TB], F32); SCANM = Buf("scanm")
        trih = sb("trih", [128, 256], BF16); TRIH = Buf("trih")

        NRING = 4
        pring = [st.enter_context(nc.psum_tensor(f"pr{i}", [128, 512], F32)) for i in range(NRING)]
        PR = [Buf(f"pr{i}") for i in range(NRING)]
        pstat = st.enter_context(nc.psum_tensor("pstat", [128, 512], F32)); PSTAT = Buf("pstat")
        pacc = [st.enter_context(nc.psum_tensor(f"pacc{i}", [128, 512], F32)) for i in range(2)]
        PACC = [Buf(f"pacc{i}") for i in range(2)]
        ptr = st.enter_context(nc.psum_tensor("ptr", [128, 1024], BF16)); PTR = Buf("ptr")
        ring_i = [0]

        def ring():
            i = ring_i[0] % NRING
            ring_i[0] += 1
            return pring[i], PR[i]

        def ACT(out, in_, func, reads, writes, scale=None, bias=None):
            kw = {}
            if scale is not None:
                kw["scale"] = scale
            if bias is not None:
                kw["bias"] = bias
            S.op("act", lambda: nc.scalar.activation(out=out, in_=in_, func=func, **kw), reads, writes)

        def TT(eng, out, in0, in1, op, reads, writes):
            e = nc.vector if eng == "dve" else nc.gpsimd
            S.op(eng, lambda: e.tensor_tensor(out=out, in0=in0, in1=in1, op=op), reads, writes)

        def STT(out, in0, scalar, in1, op0, op1, reads, writes):
            S.op("dve", lambda: nc.vector.scalar_tensor_tensor(out=out, in0=in0, scalar=scalar, in1=in1,
                                                               op0=op0, op1=op1), reads, writes)

        def TS(eng, out, in0, s1, s2, op0, op1, reads, writes):
            e = nc.vector if eng == "dve" else nc.gpsimd
            if s2 is None:
                S.op(eng, lambda: e.tensor_scalar(out=out, in0=in0, scalar1=s1, scalar2=None, op0=op0), reads, writes)
            else:
                S.op(eng, lambda: e.tensor_scalar(out=out, in0=in0, scalar1=s1, scalar2=s2, op0=op0, op1=op1),
                     reads, writes)

        def CP(eng, out, in_, reads, writes):
            e = nc.vector if eng == "dve" else nc.gpsimd
            S.op(eng, lambda: e.tensor_copy(out=out, in_=in_), reads, writes)

        def MM(out, lhsT, rhs, start, stop, reads, writes):
            S.op("pe", lambda: nc.tensor.matmul(out, lhsT=lhsT, rhs=rhs, start=start, stop=stop), reads, writes)

        def RECIP(out, in_, reads, writes):
            S.op("dve", lambda: nc.vector.reciprocal(out=out, in_=in_), reads, writes)

        def SCAN(out, d0, d1, reads, writes):
            S.op("dve", lambda: nc.vector.tensor_tensor_scan(out=out, data0=d0, data1=d1, initial=0.0,
                                                             op0=ALU.mult, op1=ALU.add), reads, writes)

        def TRANS(out, in_, ident, reads, writes):
            S.op("pe", lambda: nc.tensor.transpose(out, in_, ident), reads, writes)

        def MSET(ap, val, writes):
            S.op("pool", lambda: nc.gpsimd.memset(ap, val), (), writes)

        S.dma("sp", lambda: nc.sync.dma_start(out=vec[:], in_=vec_d), writes=[VEC], key="vec")
        S.dma("sp", lambda: nc.sync.dma_start(out=cs[:], in_=cs_d), writes=[CS], key="cs")
        S.dma("sp", lambda: nc.sync.dma_start(out=ft[0][:], in_=misc_d[:, 0:512]), writes=[FT[0]], key="misc")
        S.dma("sp", lambda: nc.sync.dma_start(out=scanm[:], in_=misc_d[:, 512:512 + TB]), writes=[SCANM], key="scanm")
        ONES_D, ONES_CQ, ONES_CKV, ONES_Q, ONES_O, ONES_K, TRI, IDN = range(8)

        def cm(i):
            return cbf[:, i * 128:(i + 1) * 128]
        MSET(cm(ONES_D), 1.0 / 1024, [CBF])
        MSET(cm(ONES_CQ), 1.0 / 384, [CBF])
        MSET(cm(ONES_CKV), 1.0 / 256, [CBF])
        MSET(cm(ONES_Q), 1.0 / 96, [CBF])
        MSET(cbf[96:128, ONES_Q * 128:(ONES_Q + 1) * 128], 0.0, [CBF])
        MSET(cm(ONES_O), 1.0 / 128, [CBF])
        MSET(cm(ONES_K), 1.0 / 96, [CBF])
        CP("pool", cm(TRI), ft[0][:, 0:128], [FT[0]], [CBF])
        CP("pool", cm(IDN), ft[0][:, 128:256], [FT[0]], [CBF])
        CP("pool", trih[:], ft[0][:, 256:512], [FT[0]], [TRIH])
        MSET(cvec[:, 0:1], EPS, [CVEC])
        MSET(cvec[:, 1:2], 1.0, [CVEC])
        MSET(cvec[:, 2:3], 0.0, [CVEC])
        C_EPS, C_ONE, C_ZERO = cvec[:, 0:1], cvec[:, 1:2], cvec[:, 2:3]
        MSET(Vx[:, :, :, 64:128], 1.0, [VXONES])
        ll = [vec[:, l * NV + 25: l * NV + 25 + 4 * L] for l in range(1)][0]
        mx, e_all, ssum, rs, cum = lbtmp[:, 0:4], lbtmp[:, 4:4 + 4 * L], lbtmp[:, 16:20], lbtmp[:, 20:24], lbtmp[:, 24:28]
        CP("dve", mx, ll[:, 0:4], [VEC], [LBTMP])
        for l in range(1, L):
            TT("dve", mx, mx, ll[:, 4 * l:4 * l + 4], ALU.max, [VEC, LBTMP], [LBTMP])
        for l in range(L):
            TT("dve", e_all[:, 4 * l:4 * l + 4], ll[:, 4 * l:4 * l + 4], mx, ALU.subtract, [VEC, LBTMP], [LBTMP])
        ACT(e_all, e_all, AF.Exp, [LBTMP], [LBTMP])
        CP("dve", ssum, e_all[:, 0:4], [LBTMP], [LBTMP])
        for l in range(1, L):
            TT("dve", ssum, ssum, e_all[:, 4 * l:4 * l + 4], ALU.add, [LBTMP], [LBTMP])
        RECIP(rs, ssum, [LBTMP], [LBTMP])
        for l in range(L):
            TT("dve", e_all[:, 4 * l:4 * l + 4], e_all[:, 4 * l:4 * l + 4], rs, ALU.mult, [LBTMP], [LBTMP])
        CP("dve", cum, e_all[:, 0:4], [LBTMP], [LBTMP])
        for l in range(L):
            if l > 0:
                TT("dve", cum, cum, e_all[:, 4 * l:4 * l + 4], ALU.add, [LBTMP], [LBTMP])
            TT("dve", lbt[:, 4 * l:4 * l + 4], cum, e_all[:, 0:4], ALU.subtract, [LBTMP], [LBT])
            TS("dve", lbt[:, 4 * l:4 * l + 4], lbt[:, 4 * l:4 * l + 4], 0.0, None, ALU.max, ALU.bypass, [LBT], [LBT])
        for l in range(L):
            TS("dve", gqs[:, l:l + 1], vec[:, l * NV + 21:l * NV + 22], float(96.0 ** -0.5), None, ALU.mult, ALU.bypass,
               [VEC], [GQS])

        RPC = 4
        PCS = RPC * T
        while WTOT % PCS:
            RPC //= 2
            PCS = RPC * T
        NCHK = WTOT // PCS
        WCH = []
        jj = 0
        for l in range(L):
            row = []
            DCH = [Buf(f"dch{l}_{k}") for k in range(NCHK)]
            for k in range(NCHK):
                c0, c1 = k * PCS, (k + 1) * PCS
                hb = jj % (8 // RPC)
                r0 = hb * RPC
                jj += 1
                bq = Buf(f"wch{l}_{c0}")
                xbufs = [XT[c][bb_] for c in range(r0, r0 + RPC) for bb_ in range(NBLK)]
                kbufs = KTB[r0:r0 + RPC]
                xs_ = xT[:, r0:r0 + RPC, :]
                ks_ = KT[:, r0:r0 + RPC, :]

                def ld(l=l, c0=c0, c1=c1, xs_=xs_):
                    return nc.sync.dma_start(out=xs_, in_=wp_d[l][0][:, c0:c1].rearrange("p (r t) -> p r t", t=T))
                S.dma("sp", ld, reads=[DCH[k]], writes=xbufs, key=f"wld{hb}")
                for r in range(RPC):
                    eng = ("act", "dve", "pool")[(jj + r) % 3]
                    c = r0 + r
                    if eng == "act":
                        ACT(KT[:, c, :], xT[:, c, :], AF.Copy, XT[c], [KTB[c]])
                    else:
                        CP(eng, KT[:, c, :], xT[:, c, :], XT[c], [KTB[c]])

                def stf(l=l, c0=c0, c1=c1, ks_=ks_):
                    return nc.sync.dma_start(out=wbf_v[l][:, c0:c1].rearrange("p (r t) -> p r t", t=T), in_=ks_)
                S.dma("sp", stf, reads=kbufs, writes=[bq, DCH[k // 2]], key=f"wst{hb}")
                row.append((c0, c1, bq))
            WCH.append(row)
        MSET(KT[96:128, :, :], 0.0, KTB)

        class WStream:
            def __init__(self):
                self.seq = []
                self.pos = 0
                self.cur = -1
                self.curkey = None

            def plan(self, order):
                self.seq = order

            def _load(self, i):
                l, g = self.seq[i]
                off, size = groups[g]
                slot = i % NSLOT
                rd = [b for (c0, c1, b) in WCH[l] if c0 < off + size and c1 > off]

                def fn(l=l, off=off, size=size, slot=slot):
                    return nc.gpsimd.dma_start(out=wslot[slot][:, 0:size], in_=wbf_v[l][:, off:off + size])
                S.dma("pool", fn, reads=rd, writes=[WS[slot]], key=f"ws{slot}")

            def start(self):
                for i in range(min(NSLOT, len(self.seq))):
                    self._load(i)
                self.pos = min(NSLOT, len(self.seq))
                self.cur = 0

            def get(self, l, name):
                g, off, kc, m = windex[name]
                while self.seq[self.cur] != (l, g):
                    self.cur += 1
                    if self.pos < len(self.seq) and self.pos - self.cur < NSLOT - 1 + 1:
                        pass
                    while self.pos < len(self.seq) and self.pos < self.cur + NSLOT:
                        self._load(self.pos)
                        self.pos += 1
                slot = self.cur % NSLOT
                ap = wslot[slot][:, off:off + kc * m].rearrange("p (k m) -> p k m", m=m)
                return ap, WS[slot]

        WST = WStream()
        order = []
        for s in range(NSEQ):
            for l in range(L):
                for b in range(NBLK):
                    for g in range(NG):
                        order.append((l, g))
        WST.plan(order)
        WST.start()

        def rmsstat_to_rstd(reads_extra=()):
            ACT(rstd[:], pstat[:], AF.Ln, [PSTAT, CVEC], [RSTD], bias=C_EPS)
            ACT(rstd[:], rstd[:], AF.Exp, [RSTD], [RSTD], scale=-0.5)

        def norm_x(l, b, gcol):
            cols = slice(b * TB, (b + 1) * TB)
            for c in range(KC):
                i = c % 2
                ACT(sq[i][:], xT[:, c, cols], AF.Square, [XT[c][b]], [SQ[i]])
                MM(pstat[:], cm(ONES_D), sq[i][:], c == 0, c == KC - 1, [CBF, SQ[i]], [PSTAT])
            rmsstat_to_rstd()
            for c in range(KC):
                STT(hT[:, c, :], xT[:, c, cols], vec[:, l * NV + gcol + c:l * NV + gcol + c + 1], rstd[:],
                    ALU.mult, ALU.mult, [XT[c][b], VEC, RSTD], [HT[c]])

        def proj_fm(l, name, rhs_ap, rhs_bufs, nk, M=128, out_rows=None):
            w, wb = WST.get(l, name)
            ps, psb = ring()
            o = ps[:] if out_rows is None else ps[out_rows[0]:out_rows[1], :]
            for k in range(nk):
                MM(o, w[:, k, :], rhs_ap(k), k == 0, k == nk - 1, [wb] + list(rhs_bufs), [psb])
            return ps, psb

        def mixer_block(l, b):
            cols = slice(b * TB, (b + 1) * TB)
            V0 = l * NV
            norm_x(l, b, 0)
            hrhs = lambda k: hT[:, k, :]
            for j in range(3):
                ps, psb = proj_fm(l, f"cq{j}", hrhs, HT, KC)
                i = j % 2
                ACT(sq[i][:], ps[:], AF.Square, [psb], [SQ[i]])
                ACT(cqf[j][:], ps[:], AF.Copy, [psb], [CQF[j]])
                MM(pstat[:], cm(ONES_CQ), sq[i][:], j == 0, j == 2, [CBF, SQ[i]], [PSTAT])
            rmsstat_to_rstd()
            for j in range(3):
                STT(cqn[:, j, :], cqf[j][:], vec[:, V0 + 16 + j:V0 + 17 + j], rstd[:], ALU.mult, ALU.mult,
                    [CQF[j], VEC, RSTD], [CQN[j]])
            for j in range(2):
                ps, psb = proj_fm(l, f"ckv{j}", hrhs, HT, KC)
                i = j % 2
                ACT(sq[i][:], ps[:], AF.Square, [psb], [SQ[i]])
                ACT(cqf[j][:], ps[:], AF.Copy, [psb], [CQF[j]])
                MM(pstat[:], cm(ONES_CKV), sq[i][:], j == 0, j == 1, [CBF, SQ[i]], [PSTAT])
            rmsstat_to_rstd()
            for j in range(2):
                STT(ckvn[:, j, :], cqf[j][:], vec[:, V0 + 19 + j:V0 + 20 + j], rstd[:], ALU.mult, ALU.mult,
                    [CQF[j], VEC, RSTD], [CKVN[j]])
            ps, psb = proj_fm(l, "kpe", hrhs, HT, KC, M=64, out_rows=(64, 128))
            ACT(sqk[64:128, 4, :], ps[64:128, :], AF.Square, [psb], [SQK[4]])
            STT(rk[64:128, :], ps[64:128, :], vec[64:128, V0 + 23:V0 + 24], cs[64:128, cols], ALU.mult, ALU.mult,
                [psb, VEC, CS], [RK])
            ACT(rk2[64:96, :], rk[96:128, :], AF.Copy, [RK], [RK2])
            TT("pool", rk[64:96, :], rk[64:96, :], rk2[64:96, :], ALU.add, [RK, RK2], [RK])
            for h in range(8):
                CP("pool", KT[64:96, h, cols], rk[64:96, :], [RK], [KTB[h]])
            ckrhs = lambda k: ckvn[:, k, :]
            for j in range(4):
                ps, psb = proj_fm(l, f"uk{j}", ckrhs, CKVN, 2)
                ACT(sqk[:, j, :], ps[:], AF.Square, [psb], [SQK[j]])
                ACT(KT[0:64, 2 * j, cols], ps[0:64, :], AF.Copy, [psb, VEC], [KTB[2 * j]], scale=vec[0:64, V0 + 22:V0 + 23])
                ACT(KT[0:64, 2 * j + 1, cols], ps[64:128, :], AF.Copy, [psb, VEC], [KTB[2 * j + 1]],
                    scale=vec[64:128, V0 + 22:V0 + 23])
            for i in range(4):
                tsl = slice(i * 128, (i + 1) * 128)
                MM(pstat[:, i * 8:(i + 1) * 8], sqk[64:96, 4, tsl], cbf[64:96, ONES_K * 128:ONES_K * 128 + 8], True, False,
                   [SQK[4], CBF], [PSTAT])
                for h in range(8):
                    r0 = (h % 2) * 64
                    MM(pstat[:, i * 8 + h:i * 8 + h + 1], sqk[r0:r0 + 64, h // 2, tsl],
                       cbf[r0:r0 + 64, ONES_K * 128:ONES_K * 128 + 1], False, h == 7, [SQK[h // 2], CBF], [PSTAT])
            ACT(kst_sb[:, :], pstat[:, 0:32], AF.Ln, [PSTAT, CVEC], [KSTSB], bias=C_EPS)
            ACT(skall[:, b * 32:(b + 1) * 32], kst_sb[:, :], AF.Exp, [KSTSB], [SK[b]], scale=-0.5)
            wv, wvb = WST.get(l, "uv")
            for i in range(4):
                tsl = slice(i * 128, (i + 1) * 128)
                ps, psb = ring()
                for k in range(2):
                    MM(ps[:], ckvn[:, k, tsl], wv[:, k, :], k == 0, k == 1, CKVN + [wvb], [psb])
                pv = ps[:].rearrange("p (j e d) -> p j e d", e=2, d=64)
                ti = b * 4 + i
                CP("dve", Vx[:, ti, :, 0:64], pv[:, :, 0, :], [psb], [VXB[ti]])
                ACT(Vx[:, ti, :, 128:192], pv[:, :, 1, :], AF.Copy, [psb], [VXB[ti]])
            cqrhs = lambda k: cqn[:, k, :]
            for h in range(8):
                ps, psb = proj_fm(l, f"uq{h}", cqrhs, CQN, 3)
                i = h % 2
                ACT(sq[i][:], ps[:], AF.Square, [psb], [SQ[i]])
                MM(pstat[:], cm(ONES_Q), sq[i][:], True, True, [CBF, SQ[i]], [PSTAT])
                rmsstat_to_rstd()
                STT(ft[0][:], ps[:], gqs[:, l:l + 1], rstd[:], ALU.mult, ALU.mult, [psb, GQS, RSTD], [FT[0]])
                TT("pool", ft[1][:], ft[0][:], cs[:, cols], ALU.mult, [FT[0], CS], [FT[1]])
                qt, QB = qTh[h % 2], QTH[h % 2]
                CP("pool", qt[0:64, :], ft[1][0:64, :], [FT[1]], [QB])
                ACT(ft[2][64:96, :], ft[1][96:128, :], AF.Copy, [FT[1]], [FT[2]])
                TT("pool", qt[64:96, :], ft[1][64:96, :], ft[2][64:96, :], ALU.add, [FT[1], FT[2]], [QB])
                CP("pool", qt[96:128, :], ft[1][96:128, :], [FT[1]], [QB])
                j, par = h // 2, h % 2
                po, POB = pacc[h % 2], PACC[h % 2]
                nkt = 4 * b + 4
                for kt in range(nkt):
                    r = kt - 4 * b
                    q0 = max(0, r) * 128
                    nq = TB - q0
                    ps2, ps2b = ring()
                    MM(ps2[:, 0:nq], KT[:, h, kt * 128:(kt + 1) * 128], qt[:, q0:TB], True, True,
                       [KTB[h], QB], [ps2b])
                    pt, PTB = pT[kt % 3], PT[kt % 3]
                    ACT(pt[:, 0:nq], ps2[:, 0:nq], AF.Exp, [ps2b, SK[kt // 4]], [PTB],
                        scale=skall[:, kt * 8 + h:kt * 8 + h + 1])
                    if r >= 0:
                        TT("pool", pt[:, 0:128], pt[:, 0:128], cm(TRI), ALU.mult, [PTB, CBF], [PTB])
                    MM(po[:, q0:TB], Vx[:, kt, j, par * 64:par * 64 + 128], pt[:, 0:nq], kt == 0, kt == nkt - 1,
                       [VXB[kt], VXONES, PTB], [POB])
                if par == 0:
                    RECIP(ft[3][64:128, :], po[64:128, :], [POB], [FT[3]])
                    TT("dve", attnT[0:64, j, :], po[0:64, :], ft[3][64:128, :], ALU.mult, [POB, FT[3]], [ATT[j]])
                else:
                    RECIP(ft[3][0:64, :], po[0:64, :], [POB], [FT[3]])
                    TT("dve", attnT[64:128, j, :], po[64:128, :], ft[3][0:64, :], ALU.mult, [POB, FT[3]], [ATT[j]])
            for h in range(4):
                lbc = lbt[:, 4 * l + h:4 * l + h + 1]
                ps, psb = proj_fm(l, f"hf{h}", hrhs, HT, KC)
                e, l1, l2, bb, kk, tmp = ft[0], ft[1], ft[2], ft[3], ft[4], ft[5]
                E, L1, L2, BB, KK, TMP = FT[0], FT[1], FT[2], FT[3], FT[4], FT[5]
                ACT(e[:], ps[:], AF.Exp, [psb], [E], scale=-1.0)
                ACT(l1[:], e[:], AF.Ln, [E, LBT, CVEC], [L1], scale=lbc, bias=C_ONE)
                ACT(l2[:], e[:], AF.Ln, [E, CVEC], [L2], bias=C_ONE)
                TT("pool", l1[:], l1[:], l2[:], ALU.subtract, [L1, L2], [L1])
                SCAN(bb[:], scanm[:], l1[:], [SCANM, L1], [BB])
                ACT(kk[:], l1[:], AF.Exp, [L1], [KK])
                TS("pool", kk[:], kk[:], -1.0, 1.0, ALU.mult, ALU.add, [KK], [KK])
                b3 = bb[:].rearrange("p (c j) -> p c j", j=64)
                ACT(dec[:, :], b3[:, :, 63], AF.Exp, [BB], [DEC])
                ACT(e[:], bb[:], AF.Exp, [BB], [E])
                ACT(l2[:], bb[:], AF.Exp, [BB], [L2], scale=-1.0)
                TT("pool", tmp[:].rearrange("p (c j) -> p c j", j=64), b3, b3[:, :, 63:64].to_broadcast([128, 8, 64]),
                   ALU.subtract, [BB], [TMP])
                ACT(tmp[:], tmp[:], AF.Exp, [TMP], [TMP], scale=-1.0)
                TT("pool", kt_[:], kk[:], l2[:], ALU.mult, [KK, L2], [KTL])
                TT("pool", kpT[:], kk[:], tmp[:], ALU.mult, [KK, TMP], [KPT])
                ps, psb = proj_fm(l, f"hq{h}", hrhs, HT, KC)
                TT("dve", qp[:], ps[:], e[:], ALU.mult, [psb, E], [QP])
                wv, wvb = WST.get(l, f"hi{h}")
                ps, psb = ring()
                for i in range(4):
                    for k in range(KC):
                        MM(ps[:, i * 128:(i + 1) * 128], hT[:, k, i * 128:(i + 1) * 128], wv[:, k, :], k == 0, k == KC - 1,
                           HT + [wvb], [psb])
                CP("dve", vtok[:].rearrange("p i d -> p (i d)"), ps[:], [psb], [VTOK])
                for i in range(4):
                    TRANS(ptr[:, i * 128:(i + 1) * 128], kpT[:, i * 128:(i + 1) * 128], cm(IDN), [KPT, CBF], [PTR])
                CP("dve", kptok[:].rearrange("p i d -> p (i d)"), ptr[:, 0:512], [PTR], [KPTOK])
                for i in range(4):
                    ps, psb = ring()
                    for par in range(2):
                        c = 2 * i + par
                        csl = slice(c * 64, (c + 1) * 64)
                        MM(ps[par * 64:(par + 1) * 64, 0:64], kt_[:, csl], qp[:, csl], True, True, [KTL, QP], [psb])
                    TT("dve", asb2[:, i, :], ps[:, 0:64], trih[:, 0:64], ALU.mult, [psb, TRIH], [ASB2[i]])
                po, POB = pacc[h % 2], PACC[h % 2]
                for c in range(8):
                    gi = b * 8 + c
                    cur, nxt = gi % 2, (gi + 1) % 2
                    i, par = c // 2, c % 2
                    rows = slice(par * 64, (par + 1) * 64)
                    csl = slice(c * 64, (c + 1) * 64)
                    MM(po[:, csl], Sbf[:, h, cur, :], qp[:, csl], True, False, [SBF[h][cur], QP], [POB])
                    MM(po[:, csl], vtok[rows, i, :], asb2[rows, i, :], False, True, [VTOK, ASB2[i]], [POB])
                    psu, psub = ring()
                    MM(psu[:, 0:128], kptok[rows, i, :], vtok[rows, i, :], True, True, [KPTOK, VTOK], [psub])
                    STT(Sst[:, h, nxt, :], Sst[:, h, cur, :], dec[:, c:c + 1], psu[:, 0:128], ALU.mult, ALU.add,
                        [SST[h][cur], DEC, psub], [SST[h][nxt]])
                    ACT(Sbf[:, h, nxt, :], Sst[:, h, nxt, :], AF.Copy, [SST[h][nxt]], [SBF[h][nxt]])
                ACT(sq[0][:], po[:], AF.Square, [POB], [SQ[0]])
                MM(pstat[:], cm(ONES_O), sq[0][:], True, True, [CBF, SQ[0]], [PSTAT])
                rmsstat_to_rstd()
                STT(ft[0][:], po[:], vec[:, V0 + 24:V0 + 25], rstd[:], ALU.mult, ALU.mult, [POB, VEC, RSTD], [FT[0]])
                ps, psb = proj_fm(l, f"hg{h}", hrhs, HT, KC)
                ACT(ft[1][:], ps[:], AF.Silu, [psb], [FT[1]])
                TT("pool", ogT[:, h, :], ft[0][:], ft[1][:], ALU.mult, [FT[0], FT[1]], [OGT[h]])
            for o in range(8):
                psa, psab = proj_fm(l, f"pa{o}", lambda k: attnT[:, k, :], ATT, 4)
                psb_, psbb = proj_fm(l, f"pb{o}", lambda k: ogT[:, k, :], OGT, 4)
                pga, pgab = proj_fm(l, f"ga{o}", hrhs, HT, KC)
                pgb, pgbb = proj_fm(l, f"gb{o}", hrhs, HT, KC)
                ACT(ft[0][:], pga[:], AF.Sigmoid, [pgab], [FT[0]])
                TT("dve", ft[1][:], psa[:], ft[0][:], ALU.mult, [psab, FT[0]], [FT[1]])
                ACT(ft[2][:], pgb[:], AF.Sigmoid, [pgbb], [FT[2]])
                TT("dve", ft[3][:], psb_[:], ft[2][:], ALU.mult, [psbb, FT[2]], [FT[3]])
                TT("pool", yT[:, o, :], ft[1][:], ft[3][:], ALU.add, [FT[1], FT[3]], [YT[o]])
            for o in range(8):
                ps, psb = proj_fm(l, f"wo{o}", lambda k: yT[:, k, :], YT, KC)
                TT("dve", xT[:, o, cols], ps[:], xT[:, o, cols], ALU.add, [psb, XT[o][b]], [XT[o][b]])

        def ffn_block(l, b):
            cols = slice(b * TB, (b + 1) * TB)
            norm_x(l, b, 8)
            hrhs = lambda k: hT[:, k, :]
            for f in range(NF):
                pg, pgb = proj_fm(l, f"gate{f}", hrhs, HT, KC)
                pu, pub = proj_fm(l, f"up{f}", hrhs, HT, KC)
                i = f % 2
                ACT(ft[i][:], pg[:], AF.Silu, [pgb], [FT[i]])
                TT("dve", actT[:, f, :], pu[:], ft[i][:], ALU.mult, [pub, FT[i]], [ACTT[f]])
            for o in range(8):
                wa, wab = WST.get(l, f"dna{o}")
                ps, psb = ring()
                for k in range(11):
                    MM(ps[:], wa[:, k, :], actT[:, k, :], k == 0, False, [wab, ACTT[k]], [psb])
                wb_, wbb = WST.get(l, f"dnb{o}")
                for k in range(11):
                    MM(ps[:], wb_[:, k, :], actT[:, 11 + k, :], False, k == 10, [wbb, ACTT[11 + k]], [psb])
                TT("dve", xT[:, o, cols], ps[:], xT[:, o, cols], ALU.add, [psb, XT[o][b]], [XT[o][b]])

        OUTB = Buf("out")
        for s in range(NSEQ):
            for b in range(NBLK):
                cols = slice(b * TB, (b + 1) * TB)
                def fn(s=s, cols=cols):
                    return nc.sync.dma_start(out=xT[:, :, cols], in_=xT_d[s, :, cols].rearrange("(c p) t -> p c t", p=128))
                S.dma("sp", fn, writes=[XT[c][b] for c in range(KC)], key=f"x_{b}")
            for l in range(L):
                for h in range(4):
                    MSET(Sst[:, h, 0, :], 0.0, [SST[h][0]])
                    MSET(Sbf[:, h, 0, :], 0.0, [SBF[h][0]])
                for b in range(NBLK):
                    mixer_block(l, b)
                    ffn_block(l, b)
                    if l == L - 1:
                        cols = slice(b * TB, (b + 1) * TB)
                        def fn(s=s, cols=cols):
                            return nc.sync.dma_start(out=out_d[s, :, cols].rearrange("(c p) t -> p c t", p=128), in_=xT[:, :, cols])
                        S.dma("sp", fn, reads=[XT[c][b] for c in range(KC)], writes=[OUTB], key=f"x_{b}")
        S.final_wait("sp", [OUTB] + [XT[c][b] for c in range(KC) for b in range(NBLK)])
        S.emit()
    return nc


def _prep_shared(inputs, L, T):
    wpack = np.stack([pack_layer_weights(inputs, l) for l in range(L)], axis=0)
    vecs = np.concatenate([pack_vecs(inputs, l) for l in range(L)], axis=1)
    cs, misc = const_tables(T)
    return wpack, np.ascontiguousarray(vecs), cs, misc


def make_wmap(wpack):
    L, _, WTOT = wpack.shape
    NWS = 1
    WPC = WTOT // NWS
    return {f"wpack{l}_{i}": np.ascontiguousarray(wpack[l, :, i * WPC:(i + 1) * WPC]) for l in range(L) for i in range(NWS)}


LAUNCH_CORES = 2


def kernel(**inputs):
    inputs = {k: np.asarray(v) for k, v in inputs.items()}
    x = inputs["x"]
    B, T, _ = x.shape
    L = inputs["w_in"].shape[0]
    nseq = B // NCORES
    wpack, vecs, cs, misc = _prep_shared(inputs, L, T)
    wmap = make_wmap(wpack)
    nc = build_program(nseq, T, L)
    outs = []
    for c0 in range(0, NCORES, LAUNCH_CORES):
        in_maps = []
        for c in range(c0, c0 + LAUNCH_CORES):
            xs = np.ascontiguousarray(x[c * nseq:(c + 1) * nseq].transpose(0, 2, 1))
            in_maps.append(dict(wmap, xT=xs, vecs=vecs, cs=cs, misc=misc))
        res = run_bass_kernel_spmd(nc, in_maps, core_ids=list(range(LAUNCH_CORES)))
        outs += [np.asarray(r["outT"]).transpose(0, 2, 1) for r in res.results]
    return np.ascontiguousarray(np.concatenate(outs, axis=0)).astype(np.float32)
```

```python
import math
from contextlib import ExitStack

import numpy as np
import concourse.bass as bass
import concourse.mybir as mybir
from concourse.bass_utils import run_bass_kernel_spmd

F32 = mybir.dt.float32
BF16 = mybir.dt.bfloat16
ALU = mybir.AluOpType
AF = mybir.ActivationFunctionType

D = 1024
KC = 8
DFF = 2816
NF = 22
TB = 512
EPS = 1e-6
NCORES = 8
SLOT = 2048
NSLOT = 2
ENGS = ("pe", "act", "dve", "pool", "sp")


class Buf:
    __slots__ = ("name", "w", "r")

    def __init__(self, name):
        self.name = name
        self.w = None
        self.r = {}


class Sched:
    def __init__(self, nc, stack):
        self.nc = nc
        self.stack = stack
        self.ops = {e: [] for e in ENGS}
        self.n = {e: 0 for e in ENGS}
        self.seen = {e: {} for e in ENGS}
        self.prog = {e: stack.enter_context(nc.semaphore("prog_" + e)) for e in ENGS}
        self.dma_sems = {}
        self.dma_cnt = {}

    def dma_sem(self, key):
        if key not in self.dma_sems:
            self.dma_sems[key] = self.stack.enter_context(self.nc.semaphore("d_" + key))
            self.dma_cnt[key] = 0
        return self.dma_sems[key]

    def _need(self, E, dep, waits):
        if dep[0] == "eng":
            _, E2, idx = dep
            if E2 == E and E == "pe":
                return
            key = ("eng", E2)
            if self.seen[E].get(key, 0) >= idx:
                return
            self.seen[E][key] = idx
            waits.append((self.prog[E2], idx))
        else:
            _, skey, cnt = dep
            key = ("dma", skey)
            if self.seen[E].get(key, 0) >= cnt:
                return
            self.seen[E][key] = cnt
            waits.append((self.dma_sems[skey], cnt))

    def _deps(self, E, reads, writes):
        waits = []
        for b in reads:
            if b.w is not None:
                self._need(E, b.w, waits)
        for b in writes:
            if b.w is not None:
                self._need(E, b.w, waits)
            for k, v in b.r.items():
                self._need(E, (k[0], k[1], v), waits)
        return waits

    def op(self, E, fn, reads=(), writes=()):
        waits = self._deps(E, reads, writes)
        self.n[E] += 1
        idx = self.n[E]
        self.ops[E].append((waits, fn, (self.prog[E], 1)))
        me = ("eng", E, idx)
        for b in reads:
            b.r[("eng", E)] = idx
        for b in writes:
            b.w = me
            b.r = {}
        return idx

    def dma(self, Q, fn, reads=(), writes=(), key=None):
        waits = self._deps(Q, reads, writes)
        sem = self.dma_sem(key)
        self.dma_cnt[key] += 16
        cnt = self.dma_cnt[key]
        self.ops[Q].append((waits, fn, (sem, 16)))
        me = ("dma", key, cnt)
        for b in reads:
            b.r[("dma", key)] = cnt
        for b in writes:
            b.w = me
            b.r = {}
        return cnt

    def final_wait(self, E, bufs):
        waits = []
        for b in bufs:
            if b.w is not None:
                self._need(E, b.w, waits)
            for k, v in b.r.items():
                self._need(E, (k[0], k[1], v), waits)
        self.ops[E].append((waits, None, None))

    def emit(self):
        nc = self.nc
        with nc.Block() as block:
            def run(E, eng):
                for waits, fn, inc in self.ops[E]:
                    for sem, val in waits:
                        eng.wait_ge(sem, val)
                    if fn is not None:
                        fn().then_inc(inc[0], inc[1])

            @block.tensor
            def _(e):
                run("pe", e)

            @block.scalar
            def _(e):
                run("act", e)

            @block.vector
            def _(e):
                run("dve", e)

            @block.gpsimd
            def _(e):
                run("pool", e)

            @block.sync
            def _(e):
                run("sp", e)


def weight_items():
    it = []
    for j in range(3):
        it.append((f"cq{j}", 8, 128))
    for j in range(2):
        it.append((f"ckv{j}", 8, 128))
    it.append(("kpe", 8, 64))
    for j in range(4):
        it.append((f"uk{j}", 2, 128))
    it.append(("uv", 2, 512))
    for h in range(8):
        it.append((f"uq{h}", 3, 128))
    for h in range(4):
        it.append((f"hf{h}", 8, 128))
        it.append((f"hq{h}", 8, 128))
        it.append((f"hi{h}", 8, 128))
        it.append((f"hg{h}", 8, 128))
    for o in range(8):
        it.append((f"pa{o}", 4, 128))
        it.append((f"pb{o}", 4, 128))
        it.append((f"ga{o}", 8, 128))
        it.append((f"gb{o}", 8, 128))
    for o in range(8):
        it.append((f"wo{o}", 8, 128))
    for f in range(NF):
        it.append((f"gate{f}", 8, 128))
        it.append((f"up{f}", 8, 128))
    for o in range(8):
        it.append((f"dna{o}", 11, 128))
        it.append((f"dnb{o}", 11, 128))
    return it


def weight_groups():
    groups = []
    index = {}
    cur_off = 0
    cur_size = 0
    start = 0
    for name, kc, m in weight_items():
        n = kc * m
        if cur_size + n > SLOT:
            groups.append((start, cur_size))
            start += cur_size
            cur_size = 0
        index[name] = (len(groups), cur_size, kc, m)
        cur_size += n
    groups.append((start, cur_size))
    total = start + cur_size
    total = ((total + 8191) // 8192) * 8192
    return groups, index, total


IN_OFF = {}
_o = 0
for _n, _s in (("cq", 384), ("ckv", 256), ("kpe", 32), ("hq", 512), ("hf", 512), ("hi", 512), ("hg", 512),
               ("ga", 1024), ("gb", 1024)):
    IN_OFF[_n] = _o
    _o += _s

NV = 40


def pack_layer_weights(inp, l):
    w_in = inp["w_in"][l]
    cols = {}

    def rng(a, n):
        return list(range(a, a + n))
    for j in range(3):
        cols[f"cq{j}"] = (w_in, rng(IN_OFF["cq"] + j * 128, 128))
    for j in range(2):
        cols[f"ckv{j}"] = (w_in, rng(IN_OFF["ckv"] + j * 128, 128))
    kp = IN_OFF["kpe"]
    cols["kpe"] = (w_in, rng(kp, 32) + rng(kp + 16, 16) + rng(kp, 16))
    ukv = inp["mla_w_ukv"][l]
    for j in range(4):
        cols[f"uk{j}"] = (ukv, rng((2 * j) * 128, 64) + rng((2 * j + 1) * 128, 64))
    vc = []
    for h in range(8):
        vc += rng(h * 128 + 64, 64)
    cols["uv"] = (ukv, vc)
    uq = inp["mla_w_uq"][l]
    for h in range(8):
        cols[f"uq{h}"] = (uq, rng(h * 96, 96) + rng(h * 96 + 80, 16) + rng(h * 96 + 64, 16))
    for h in range(4):
        for nm in ("hf", "hq", "hi", "hg"):
            cols[f"{nm}{h}"] = (w_in, rng(IN_OFF[nm] + h * 128, 128))
    for o in range(8):
        cols[f"pa{o}"] = (inp["w_proj_a"][l], rng(o * 128, 128))
        cols[f"pb{o}"] = (inp["w_proj_b"][l], rng(o * 128, 128))
        cols[f"ga{o}"] = (w_in, rng(IN_OFF["ga"] + o * 128, 128))
        cols[f"gb{o}"] = (w_in, rng(IN_OFF["gb"] + o * 128, 128))
        cols[f"wo{o}"] = (inp["w_out"][l], rng(o * 128, 128))
    for f in range(NF):
        cols[f"gate{f}"] = (inp["w_gate"][l], rng(f * 128, 128))
        cols[f"up{f}"] = (inp["w_up"][l], rng(f * 128, 128))
    wd = inp["w_down"][l]
    for o in range(8):
        cols[f"dna{o}"] = (wd[0:11 * 128], rng(o * 128, 128))
        cols[f"dnb{o}"] = (wd[11 * 128:22 * 128], rng(o * 128, 128))
    groups, index, total = weight_groups()
    out = np.zeros((128, total), np.float32)
    for name, kc, m in weight_items():
        W, cl = cols[name]
        g, off, _, _ = index[name]
        base = groups[g][0] + off
        blk = W[:kc * 128][:, cl].reshape(kc, 128, m).transpose(1, 0, 2).reshape(128, kc * m)
        out[:, base:base + kc * m] = blk
    return out


def pack_vecs(inp, l):
    v = np.zeros((128, NV), np.float32)
    v[:, 0:8] = inp["norm_mix"][l].reshape(8, 128).T
    v[:, 8:16] = inp["norm_ffn"][l].reshape(8, 128).T
    v[:, 16:19] = inp["mla_norm_cq"][l].reshape(3, 128).T
    v[:, 19:21] = inp["mla_norm_ckv"][l].reshape(2, 128).T
    qn = inp["mla_q_norm"][l]
    kn = inp["mla_k_norm"][l]
    v[0:96, 21] = qn
    v[96:112, 21] = qn[80:96]
    v[112:128, 21] = qn[64:80]
    v[0:64, 22] = kn[0:64]
    v[64:128, 22] = kn[0:64]
    v[64:96, 23] = kn[64:96]
    v[96:112, 23] = kn[80:96]
    v[112:128, 23] = kn[64:80]
    v[:, 24] = inp["hg_out_norm"][l]
    for ll in range(inp["hg_lb_logits"].shape[0]):
        v[:, 25 + 4 * ll:29 + 4 * ll] = inp["hg_lb_logits"][ll].reshape(4, 128).T
    return v


def const_tables(T):
    pos = np.arange(T, dtype=np.float32)
    inv_freq = (1.0 / (np.float32(10000.0) ** (np.arange(0, 32, 2, dtype=np.float32) / np.float32(32)))).astype(np.float32)
    ang = (pos[:, None] * inv_freq[None, :]).astype(np.float32)
    cos = np.cos(ang).astype(np.float32).T
    sin = np.sin(ang).astype(np.float32).T
    cs = np.ones((128, T), np.float32)
    cs[64:80] = cos
    cs[80:96] = cos
    cs[96:112] = -sin
    cs[112:128] = sin
    k = np.arange(128)
    tri_att = (k[None, :] >= k[:, None]).astype(np.float32)
    ident = np.eye(128, dtype=np.float32)
    s64 = k % 64
    t64 = np.arange(64)
    tri_h = (t64[None, :] >= s64[:, None]).astype(np.float32)
    tri_h4 = np.tile(tri_h, (1, 4))
    scanmask = np.ones((128, TB), np.float32)
    scanmask[:, ::64] = 0.0
    misc = np.concatenate([tri_att, ident, tri_h4, scanmask], axis=1).astype(np.float32)
    return cs, misc


def build_program(NSEQ, T, L, dbg=False):
    NBLK = T // TB
    NT = T // 128
    groups, windex, WTOT = weight_groups()
    NG = len(groups)
    nc = bass.Bass("TRN2", target_bir_lowering=False)
    xT_d = nc.dram_tensor("xT", [NSEQ, D, T], F32, kind="ExternalInput").ap()
    NWS = 1
    WPC = WTOT // NWS
    wp_d = [[nc.dram_tensor(f"wpack{l}_{i}", [128, WPC], F32, kind="ExternalInput").ap() for i in range(NWS)]
            for l in range(L)]
    vec_d = nc.dram_tensor("vecs", [128, L * NV], F32, kind="ExternalInput").ap()
    cs_d = nc.dram_tensor("cs", [128, T], F32, kind="ExternalInput").ap()
    misc_d = nc.dram_tensor("misc", [128, 512 + TB], F32, kind="ExternalInput").ap()
    out_d = nc.dram_tensor("outT", [NSEQ, D, T], F32, kind="ExternalOutput").ap()
    import os as _os
    if _os.environ.get("KDUMMY"):
        nc.dram_tensor("dummyin", [128, WTOT], F32, kind="ExternalInput")
    wbf_v = [wp_d[l][0].bitcast(BF16) for l in range(L)]

    with ExitStack() as st:
        S = Sched(nc, st)

        def sb(name, shape, dt):
            return st.enter_context(nc.sbuf_tensor(name, shape, dt))

        xT = sb("xT_sb", [128, KC, T], F32)
        XT = [[Buf(f"xT{c}_{b}") for b in range(NBLK)] for c in range(KC)]
        cs = sb("cs_sb", [128, T], F32); CS = Buf("cs")
        KT = sb("KT", [128, 8, T], BF16)
        KTB = [Buf(f"KT{h}") for h in range(8)]
        Vx = sb("Vx", [128, NT, 4, 192], BF16)
        VXB = [Buf(f"Vx{i}") for i in range(NT)]
        VXONES = Buf("vxones")
        skall = sb("skall", [128, NT * 8], F32)
        SK = [Buf(f"sk{b}") for b in range(NBLK)]
        wslot = [sb(f"wslot{i}", [128, SLOT], BF16) for i in range(NSLOT)]
        WS = [Buf(f"wslot{i}") for i in range(NSLOT)]
        vec = sb("vec_sb", [128, L * NV], F32); VEC = Buf("vec")
        cbf = sb("cbf", [128, 128 * 8], BF16); CBF = Buf("cbf")
        cvec = sb("cvec", [128, 8], F32); CVEC = Buf("cvec")
        lbt = sb("lbt", [128, L * 4], F32); LBT = Buf("lbt")
        lbtmp = sb("lbtmp", [128, 40], F32); LBTMP = Buf("lbtmp")
        gqs = sb("gqs", [128, L], F32); GQS = Buf("gqs")
        Sst = sb("Sst", [128, 4, 2, 128], F32)
        SST = [[Buf(f"S{h}_{i}") for i in range(2)] for h in range(4)]
        Sbf = sb("Sbf", [128, 8, 128], BF16)
        SBF = [Buf(f"Sbf{i}") for i in range(8)]
        hT = sb("hT", [128, KC, TB], BF16)
        HT = [Buf(f"hT{c}") for c in range(KC)]
        NSL = 29
        bfs = sb("bfs", [128, NSL, TB], BF16)
        BFS = [Buf(f"bfs{i}") for i in range(NSL)]
        yT, YT = bfs, BFS
        sqk, SQK = bfs, BFS
        cqn = bfs[:, 8:11, :]; CQN = BFS[8:11]
        ckvn = bfs[:, 11:13, :]; CKVN = BFS[11:13]
        qTh = [bfs[:, 13 + i, :] for i in range(2)]; QTH = BFS[13:15]
        pT = [bfs[:, 15 + i, :] for i in range(3)]; PT = BFS[15:18]
        qp = bfs[:, 18, :]; QP = BFS[18]
        kt_ = bfs[:, 19, :]; KTL = BFS[19]
        kpT = bfs[:, 20, :]; KPT = BFS[20]
        attnT = bfs[:, 21:25, :]; ATT = BFS[21:25]
        ogT = bfs[:, 25:29, :]; OGT = BFS[25:29]
        actT, ACTT = bfs, BFS
        sq = [sb(f"sq{i}", [128, TB], BF16) for i in range(2)]
        SQ = [Buf(f"sq{i}") for i in range(2)]
        rstd = sb("rstd", [128, TB], F32); RSTD = Buf("rstd")
        ft = [sb(f"ft{i}", [128, TB], F32) for i in range(6)]
        FT = [Buf(f"ft{i}") for i in range(6)]
        cqf = [ft[0], ft[1], ft[2]]; CQF = FT[0:3]
        rk, RK = ft[4], FT[4]
        rk2, RK2 = ft[5], FT[5]
        kptok = sb("kptok", [128, 4, 128], BF16); KPTOK = Buf("kptok")
        vtok = sb("vtok", [128, 4, 128], BF16); VTOK = Buf("vtok")
        asb2 = sb("asb2", [128, 4, 64], BF16); ASB2 = [Buf(f"asb2_{i}") for i in range(4)]
        dec = sb("dec", [128, 8], F32); DEC = Buf("dec")
        kst_sb = sb("kst_sb", [128, 32], F32); KSTSB = Buf("kstsb")
        scanm = sb("scanm", [128, TB], F32); SCANM = Buf("scanm")
        trih = sb("trih", [128, 256], BF16); TRIH = Buf("trih")

        NRING = 4
        pring = [st.enter_context(nc.psum_tensor(f"pr{i}", [128, 512], F32)) for i in range(NRING)]
        PR = [Buf(f"pr{i}") for i in range(NRING)]
        pstat = st.enter_context(nc.psum_tensor("pstat", [128, 512], F32)); PSTAT = Buf("pstat")
        pacc = [st.enter_context(nc.psum_tensor(f"pacc{i}", [128, 512], F32)) for i in range(2)]
        PACC = [Buf(f"pacc{i}") for i in range(2)]
        ptr = st.enter_context(nc.psum_tensor("ptr", [128, 1024], BF16)); PTR = Buf("ptr")
        ring_i = [0]

        def ring():
            i = ring_i[0] % NRING
            ring_i[0] += 1
            return pring[i], PR[i]

        def ACT(out, in_, func, reads, writes, scale=None, bias=None):
            kw = {}
            if scale is not None:
                kw["scale"] = scale
            if bias is not None:
                kw["bias"] = bias
            S.op("act", lambda: nc.scalar.activation(out=out, in_=in_, func=func, **kw), reads, writes)

        def TT(eng, out, in0, in1, op, reads, writes):
            e = nc.vector if eng == "dve" else nc.gpsimd
            S.op(eng, lambda: e.tensor_tensor(out=out, in0=in0, in1=in1, op=op), reads, writes)

        def STT(out, in0, scalar, in1, op0, op1, reads, writes):
            S.op("dve", lambda: nc.vector.scalar_tensor_tensor(out=out, in0=in0, scalar=scalar, in1=in1,
                                                               op0=op0, op1=op1), reads, writes)

        def TS(eng, out, in0, s1, s2, op0, op1, reads, writes):
            e = nc.vector if eng == "dve" else nc.gpsimd
            if s2 is None:
                S.op(eng, lambda: e.tensor_scalar(out=out, in0=in0, scalar1=s1, scalar2=None, op0=op0), reads, writes)
            else:
                S.op(eng, lambda: e.tensor_scalar(out=out, in0=in0, scalar1=s1, scalar2=s2, op0=op0, op1=op1),
                     reads, writes)

        def CP(eng, out, in_, reads, writes):
            e = nc.vector if eng == "dve" else nc.gpsimd
            S.op(eng, lambda: e.tensor_copy(out=out, in_=in_), reads, writes)

        def MM(out, lhsT, rhs, start, stop, reads, writes):
            S.op("pe", lambda: nc.tensor.matmul(out, lhsT=lhsT, rhs=rhs, start=start, stop=stop), reads, writes)

        def RECIP(out, in_, reads, writes):
            S.op("dve", lambda: nc.vector.reciprocal(out=out, in_=in_), reads, writes)

        def SCAN(out, d0, d1, reads, writes):
            S.op("dve", lambda: nc.vector.tensor_tensor_scan(out=out, data0=d0, data1=d1, initial=0.0,
                                                             op0=ALU.mult, op1=ALU.add), reads, writes)

        def TRANS(out, in_, ident, reads, writes):
            S.op("pe", lambda: nc.tensor.transpose(out, in_, ident), reads, writes)

        def MSET(ap, val, writes):
            S.op("pool", lambda: nc.gpsimd.memset(ap, val), (), writes)

        S.dma("sp", lambda: nc.sync.dma_start(out=vec[:], in_=vec_d), writes=[VEC], key="vec")
        S.dma("sp", lambda: nc.sync.dma_start(out=cs[:], in_=cs_d), writes=[CS], key="cs")
        S.dma("sp", lambda: nc.sync.dma_start(out=ft[0][:], in_=misc_d[:, 0:512]), writes=[FT[0]], key="misc")
        S.dma("sp", lambda: nc.sync.dma_start(out=scanm[:], in_=misc_d[:, 512:512 + TB]), writes=[SCANM], key="scanm")
        ONES_D, ONES_CQ, ONES_CKV, ONES_Q, ONES_O, ONES_K, TRI, IDN = range(8)

        def cm(i):
            return cbf[:, i * 128:(i + 1) * 128]
        MSET(cm(ONES_D), 1.0 / 1024, [CBF])
        MSET(cm(ONES_CQ), 1.0 / 384, [CBF])
        MSET(cm(ONES_CKV), 1.0 / 256, [CBF])
        MSET(cm(ONES_Q), 1.0 / 96, [CBF])
        MSET(cbf[96:128, ONES_Q * 128:(ONES_Q + 1) * 128], 0.0, [CBF])
        MSET(cm(ONES_O), 1.0 / 128, [CBF])
        MSET(cm(ONES_K), 0.0, [CBF])
        MSET(cbf[64:96, ONES_K * 128:ONES_K * 128 + 8], 1.0 / 96, [CBF])
        MSET(cbf[0:64, ONES_K * 128 + 8:ONES_K * 128 + 9], 1.0 / 96, [CBF])
        MSET(cbf[64:128, ONES_K * 128 + 9:ONES_K * 128 + 10], 1.0 / 96, [CBF])
        for i_ in range(NSL):
            MSET(bfs[:, i_, :], 0.0, [BFS[i_]])
        CP("pool", cm(TRI), ft[0][:, 0:128], [FT[0]], [CBF])
        CP("pool", cm(IDN), ft[0][:, 128:256], [FT[0]], [CBF])
        CP("pool", trih[:], ft[0][:, 256:512], [FT[0]], [TRIH])
        MSET(cvec[:, 0:1], EPS, [CVEC])
        MSET(cvec[:, 1:2], 1.0, [CVEC])
        MSET(cvec[:, 2:3], 0.0, [CVEC])
        C_EPS, C_ONE, C_ZERO = cvec[:, 0:1], cvec[:, 1:2], cvec[:, 2:3]
        MSET(Vx[:, :, :, 64:128], 1.0, [VXONES])
        ll = [vec[:, l * NV + 25: l * NV + 25 + 4 * L] for l in range(1)][0]
        mx, e_all, ssum, rs, cum = lbtmp[:, 0:4], lbtmp[:, 4:4 + 4 * L], lbtmp[:, 16:20], lbtmp[:, 20:24], lbtmp[:, 24:28]
        CP("dve", mx, ll[:, 0:4], [VEC], [LBTMP])
        for l in range(1, L):
            TT("dve", mx, mx, ll[:, 4 * l:4 * l + 4], ALU.max, [VEC, LBTMP], [LBTMP])
        for l in range(L):
            TT("dve", e_all[:, 4 * l:4 * l + 4], ll[:, 4 * l:4 * l + 4], mx, ALU.subtract, [VEC, LBTMP], [LBTMP])
        ACT(e_all, e_all, AF.Exp, [LBTMP], [LBTMP])
        CP("dve", ssum, e_all[:, 0:4], [LBTMP], [LBTMP])
        for l in range(1, L):
            TT("dve", ssum, ssum, e_all[:, 4 * l:4 * l + 4], ALU.add, [LBTMP], [LBTMP])
        RECIP(rs, ssum, [LBTMP], [LBTMP])
        for l in range(L):
            TT("dve", e_all[:, 4 * l:4 * l + 4], e_all[:, 4 * l:4 * l + 4], rs, ALU.mult, [LBTMP], [LBTMP])
        CP("dve", cum, e_all[:, 0:4], [LBTMP], [LBTMP])
        for l in range(L):
            if l > 0:
                TT("dve", cum, cum, e_all[:, 4 * l:4 * l + 4], ALU.add, [LBTMP], [LBTMP])
            TT("dve", lbt[:, 4 * l:4 * l + 4], cum, e_all[:, 0:4], ALU.subtract, [LBTMP], [LBT])
            TS("dve", lbt[:, 4 * l:4 * l + 4], lbt[:, 4 * l:4 * l + 4], 0.0, None, ALU.max, ALU.bypass, [LBT], [LBT])
        for l in range(L):
            TS("dve", gqs[:, l:l + 1], vec[:, l * NV + 21:l * NV + 22], float(96.0 ** -0.5), None, ALU.mult, ALU.bypass,
               [VEC], [GQS])

        RPC = 4
        PCS = RPC * T
        while WTOT % PCS:
            RPC //= 2
            PCS = RPC * T
        NCHK = WTOT // PCS
        WCH = []
        jj = 0
        for l in range(L):
            row = []
            DCH = [Buf(f"dch{l}_{k}") for k in range(NCHK)]
            for k in range(NCHK):
                c0, c1 = k * PCS, (k + 1) * PCS
                hb = jj % (8 // RPC)
                r0 = hb * RPC
                jj += 1
                bq = Buf(f"wch{l}_{c0}")
                xbufs = [XT[c][bb_] for c in range(r0, r0 + RPC) for bb_ in range(NBLK)]
                kbufs = KTB[r0:r0 + RPC]
                xs_ = xT[:, r0:r0 + RPC, :]
                ks_ = KT[:, r0:r0 + RPC, :]

                def ld(l=l, c0=c0, c1=c1, xs_=xs_):
                    return nc.sync.dma_start(out=xs_, in_=wp_d[l][0][:, c0:c1].rearrange("p (r t) -> p r t", t=T))
                S.dma("sp", ld, reads=[DCH[k]], writes=xbufs, key=f"wld{hb}")
                for r in range(RPC):
                    eng = ("act", "dve", "pool")[(jj + r) % 3]
                    c = r0 + r
                    if eng == "act":
                        ACT(KT[:, c, :], xT[:, c, :], AF.Copy, XT[c], [KTB[c]])
                    else:
                        CP(eng, KT[:, c, :], xT[:, c, :], XT[c], [KTB[c]])

                def stf(l=l, c0=c0, c1=c1, ks_=ks_):
                    return nc.sync.dma_start(out=wbf_v[l][:, c0:c1].rearrange("p (r t) -> p r t", t=T), in_=ks_)
                S.dma("sp", stf, reads=kbufs, writes=[bq, DCH[k // 2]], key=f"wst{hb}")
                row.append((c0, c1, bq))
            WCH.append(row)
        MSET(KT[96:128, :, :], 0.0, KTB)

        class WStream:
            def __init__(self):
                self.seq = []
                self.pos = 0
                self.cur = -1
                self.curkey = None

            def plan(self, order):
                self.seq = order

            def _load(self, i):
                l, g = self.seq[i]
                off, size = groups[g]
                slot = i % NSLOT
                rd = [b for (c0, c1, b) in WCH[l] if c0 < off + size and c1 > off]

                def fn(l=l, off=off, size=size, slot=slot):
                    return nc.gpsimd.dma_start(out=wslot[slot][:, 0:size], in_=wbf_v[l][:, off:off + size])
                S.dma("pool", fn, reads=rd, writes=[WS[slot]], key=f"ws{slot}")

            def start(self):
                for i in range(min(NSLOT, len(self.seq))):
                    self._load(i)
                self.pos = min(NSLOT, len(self.seq))
                self.cur = 0

            def get(self, l, name):
                g, off, kc, m = windex[name]
                while self.seq[self.cur] != (l, g):
                    self.cur += 1
                    if self.pos < len(self.seq) and self.pos - self.cur < NSLOT - 1 + 1:
                        pass
                    while self.pos < len(self.seq) and self.pos < self.cur + NSLOT:
                        self._load(self.pos)
                        self.pos += 1
                slot = self.cur % NSLOT
                ap = wslot[slot][:, off:off + kc * m].rearrange("p (k m) -> p k m", m=m)
                return ap, WS[slot]

        WST = WStream()
        order = []
        for s in range(NSEQ):
            for l in range(L):
                for b in range(NBLK):
                    for g in range(NG):
                        order.append((l, g))
        WST.plan(order)
        WST.start()

        def rmsstat_to_rstd(reads_extra=()):
            ACT(rstd[:], pstat[:], AF.Ln, [PSTAT, CVEC], [RSTD], bias=C_EPS)
            ACT(rstd[:], rstd[:], AF.Exp, [RSTD], [RSTD], scale=-0.5)

        def norm_x(l, b, gcol):
            cols = slice(b * TB, (b + 1) * TB)
            for c in range(KC):
                i = c % 2
                ACT(sq[i][:], xT[:, c, cols], AF.Square, [XT[c][b]], [SQ[i]])
                MM(pstat[:], cm(ONES_D), sq[i][:], c == 0, c == KC - 1, [CBF, SQ[i]], [PSTAT])
            rmsstat_to_rstd()
            for c in range(KC):
                STT(hT[:, c, :], xT[:, c, cols], vec[:, l * NV + gcol + c:l * NV + gcol + c + 1], rstd[:],
                    ALU.mult, ALU.mult, [XT[c][b], VEC, RSTD], [HT[c]])

        def proj_fm(l, name, rhs_ap, rhs_bufs, nk, M=128, out_rows=None):
            w, wb = WST.get(l, name)
            ps, psb = ring()
            o = ps[:] if out_rows is None else ps[out_rows[0]:out_rows[1], :]
            for k in range(nk):
                MM(o, w[:, k, :], rhs_ap(k), k == 0, k == nk - 1, [wb] + list(rhs_bufs), [psb])
            return ps, psb

        def mixer_block(l, b):
            cols = slice(b * TB, (b + 1) * TB)
            V0 = l * NV
            norm_x(l, b, 0)
            hrhs = lambda k: hT[:, k, :]
            for j in range(3):
                ps, psb = proj_fm(l, f"cq{j}", hrhs, HT, KC)
                i = j % 2
                ACT(sq[i][:], ps[:], AF.Square, [psb], [SQ[i]])
                ACT(cqf[j][:], ps[:], AF.Copy, [psb], [CQF[j]])
                MM(pstat[:], cm(ONES_CQ), sq[i][:], j == 0, j == 2, [CBF, SQ[i]], [PSTAT])
            rmsstat_to_rstd()
            for j in range(3):
                STT(cqn[:, j, :], cqf[j][:], vec[:, V0 + 16 + j:V0 + 17 + j], rstd[:], ALU.mult, ALU.mult,
                    [CQF[j], VEC, RSTD], [CQN[j]])
            for j in range(2):
                ps, psb = proj_fm(l, f"ckv{j}", hrhs, HT, KC)
                i = j % 2
                ACT(sq[i][:], ps[:], AF.Square, [psb], [SQ[i]])
                ACT(cqf[j][:], ps[:], AF.Copy, [psb], [CQF[j]])
                MM(pstat[:], cm(ONES_CKV), sq[i][:], j == 0, j == 1, [CBF, SQ[i]], [PSTAT])
            rmsstat_to_rstd()
            for j in range(2):
                STT(ckvn[:, j, :], cqf[j][:], vec[:, V0 + 19 + j:V0 + 20 + j], rstd[:], ALU.mult, ALU.mult,
                    [CQF[j], VEC, RSTD], [CKVN[j]])
            ps, psb = proj_fm(l, "kpe", hrhs, HT, KC, M=64, out_rows=(64, 128))
            ACT(sqk[64:128, 4, :], ps[64:128, :], AF.Square, [psb], [SQK[4]])
            STT(rk[64:128, :], ps[64:128, :], vec[64:128, V0 + 23:V0 + 24], cs[64:128, cols], ALU.mult, ALU.mult,
                [psb, VEC, CS], [RK])
            ACT(rk2[64:96, :], rk[96:128, :], AF.Copy, [RK], [RK2])
            TT("pool", rk[64:96, :], rk[64:96, :], rk2[64:96, :], ALU.add, [RK, RK2], [RK])
            for h in range(8):
                CP("pool", KT[64:96, h, cols], rk[64:96, :], [RK], [KTB[h]])
            ckrhs = lambda k: ckvn[:, k, :]
            for j in range(4):
                ps, psb = proj_fm(l, f"uk{j}", ckrhs, CKVN, 2)
                ACT(sqk[:, j, :], ps[:], AF.Square, [psb], [SQK[j]])
                ACT(KT[0:64, 2 * j, cols], ps[0:64, :], AF.Copy, [psb, VEC], [KTB[2 * j]], scale=vec[0:64, V0 + 22:V0 + 23])
                ACT(KT[0:64, 2 * j + 1, cols], ps[64:128, :], AF.Copy, [psb, VEC], [KTB[2 * j + 1]],
                    scale=vec[64:128, V0 + 22:V0 + 23])
            for i in range(4):
                tsl = slice(i * 128, (i + 1) * 128)
                MM(pstat[:, i * 8:(i + 1) * 8], sqk[:, 4, tsl], cbf[:, ONES_K * 128:ONES_K * 128 + 8], True, False,
                   [SQK[4], CBF], [PSTAT])
                for h in range(8):
                    sel = ONES_K * 128 + 8 + (h % 2)
                    MM(pstat[:, i * 8 + h:i * 8 + h + 1], sqk[:, h // 2, tsl],
                       cbf[:, sel:sel + 1], False, h == 7, [SQK[h // 2], CBF], [PSTAT])
            ACT(kst_sb[:, :], pstat[:, 0:32], AF.Ln, [PSTAT, CVEC], [KSTSB], bias=C_EPS)
            ACT(skall[:, b * 32:(b + 1) * 32], kst_sb[:, :], AF.Exp, [KSTSB], [SK[b]], scale=-0.5)
            wv, wvb = WST.get(l, "uv")
            for i in range(4):
                tsl = slice(i * 128, (i + 1) * 128)
                ps, psb = ring()
                for k in range(2):
                    MM(ps[:], ckvn[:, k, tsl], wv[:, k, :], k == 0, k == 1, CKVN + [wvb], [psb])
                pv = ps[:].rearrange("p (j e d) -> p j e d", e=2, d=64)
                ti = b * 4 + i
                CP("dve", Vx[:, ti, :, 0:64], pv[:, :, 0, :], [psb], [VXB[ti]])
                ACT(Vx[:, ti, :, 128:192], pv[:, :, 1, :], AF.Copy, [psb], [VXB[ti]])
            cqrhs = lambda k: cqn[:, k, :]
            def q_prep(h):
                ps, psb = proj_fm(l, f"uq{h}", cqrhs, CQN, 3)
                i = h % 2
                ACT(sq[i][:], ps[:], AF.Square, [psb], [SQ[i]])
                MM(pstat[:], cm(ONES_Q), sq[i][:], True, True, [CBF, SQ[i]], [PSTAT])
                rmsstat_to_rstd()
                STT(ft[0][:], ps[:], gqs[:, l:l + 1], rstd[:], ALU.mult, ALU.mult, [psb, GQS, RSTD], [FT[0]])
                TT("pool", ft[1][:], ft[0][:], cs[:, cols], ALU.mult, [FT[0], CS], [FT[1]])
                qt, QB = qTh[h % 2], QTH[h % 2]
                CP("pool", qt[0:64, :], ft[1][0:64, :], [FT[1]], [QB])
                ACT(ft[2][64:96, :], ft[1][96:128, :], AF.Copy, [FT[1]], [FT[2]])
                TT("pool", qt[64:96, :], ft[1][64:96, :], ft[2][64:96, :], ALU.add, [FT[1], FT[2]], [QB])
                CP("pool", qt[96:128, :], ft[1][96:128, :], [FT[1]], [QB])

            def attn(h):
                qt, QB = qTh[h % 2], QTH[h % 2]
                j, par = h // 2, h % 2
                po, POB = pacc[h % 2], PACC[h % 2]
                nkt = 4 * b + 4
                for kt in range(nkt):
                    r = kt - 4 * b
                    q0 = max(0, r) * 128
                    nq = TB - q0
                    ps2, ps2b = ring()
                    MM(ps2[:, 0:nq], KT[:, h, kt * 128:(kt + 1) * 128], qt[:, q0:TB], True, True,
                       [KTB[h], QB], [ps2b])
                    pt, PTB = pT[kt % 3], PT[kt % 3]
                    ACT(pt[:, 0:nq], ps2[:, 0:nq], AF.Exp, [ps2b, SK[kt // 4]], [PTB],
                        scale=skall[:, kt * 8 + h:kt * 8 + h + 1])
                    if r >= 0:
                        TT("pool", pt[:, 0:128], pt[:, 0:128], cm(TRI), ALU.mult, [PTB, CBF], [PTB])
                    MM(po[:, q0:TB], Vx[:, kt, j, par * 64:par * 64 + 128], pt[:, 0:nq], kt == 0, kt == nkt - 1,
                       [VXB[kt], VXONES, PTB], [POB])
                if par == 0:
                    RECIP(ft[3][64:128, :], po[64:128, :], [POB], [FT[3]])
                    TT("dve", attnT[0:64, j, :], po[0:64, :], ft[3][64:128, :], ALU.mult, [POB, FT[3]], [ATT[j]])
                else:
                    RECIP(ft[3][0:64, :], po[0:64, :], [POB], [FT[3]])
                    TT("dve", attnT[64:128, j, :], po[64:128, :], ft[3][0:64, :], ALU.mult, [POB, FT[3]], [ATT[j]])

            q_prep(0)
            for h in range(8):
                if h < 7:
                    q_prep(h + 1)
                attn(h)
            for h in range(4):
                lbc = lbt[:, 4 * l + h:4 * l + h + 1]
                ps, psb = proj_fm(l, f"hf{h}", hrhs, HT, KC)
                e, l1, l2, bb, kk, tmp = ft[0], ft[1], ft[2], ft[3], ft[4], ft[5]
                E, L1, L2, BB, KK, TMP = FT[0], FT[1], FT[2], FT[3], FT[4], FT[5]
                ACT(e[:], ps[:], AF.Exp, [psb], [E], scale=-1.0)
                ACT(l1[:], e[:], AF.Ln, [E, LBT, CVEC], [L1], scale=lbc, bias=C_ONE)
                ACT(l2[:], e[:], AF.Ln, [E, CVEC], [L2], bias=C_ONE)
                TT("pool", l1[:], l1[:], l2[:], ALU.subtract, [L1, L2], [L1])
                SCAN(bb[:], scanm[:], l1[:], [SCANM, L1], [BB])
                ACT(kk[:], l1[:], AF.Exp, [L1], [KK])
                TS("pool", kk[:], kk[:], -1.0, 1.0, ALU.mult, ALU.add, [KK], [KK])
                b3 = bb[:].rearrange("p (c j) -> p c j", j=64)
                ACT(dec[:, :], b3[:, :, 63], AF.Exp, [BB], [DEC])
                ACT(e[:], bb[:], AF.Exp, [BB], [E])
                ACT(l2[:], bb[:], AF.Exp, [BB], [L2], scale=-1.0)
                TT("pool", tmp[:].rearrange("p (c j) -> p c j", j=64), b3, b3[:, :, 63:64].to_broadcast([128, 8, 64]),
                   ALU.subtract, [BB], [TMP])
                ACT(tmp[:], tmp[:], AF.Exp, [TMP], [TMP], scale=-1.0)
                TT("pool", kt_[:], kk[:], l2[:], ALU.mult, [KK, L2], [KTL])
                TT("pool", kpT[:], kk[:], tmp[:], ALU.mult, [KK, TMP], [KPT])
                ps, psb = proj_fm(l, f"hq{h}", hrhs, HT, KC)
                TT("dve", qp[:], ps[:], e[:], ALU.mult, [psb, E], [QP])
                wv, wvb = WST.get(l, f"hi{h}")
                ps, psb = ring()
                for i in range(4):
                    for k in range(KC):
                        MM(ps[:, i * 128:(i + 1) * 128], hT[:, k, i * 128:(i + 1) * 128], wv[:, k, :], k == 0, k == KC - 1,
                           HT + [wvb], [psb])
                CP("dve", vtok[:].rearrange("p i d -> p (i d)"), ps[:], [psb], [VTOK])
                for i in range(4):
                    TRANS(ptr[:, i * 128:(i + 1) * 128], kpT[:, i * 128:(i + 1) * 128], cm(IDN), [KPT, CBF], [PTR])
                CP("dve", kptok[:].rearrange("p i d -> p (i d)"), ptr[:, 0:512], [PTR], [KPTOK])
                for i in range(4):
                    ps, psb = ring()
                    for par in range(2):
                        c = 2 * i + par
                        csl = slice(c * 64, (c + 1) * 64)
                        MM(ps[par * 64:(par + 1) * 64, 0:64], kt_[:, csl], qp[:, csl], True, True, [KTL, QP], [psb])
                    TT("dve", asb2[:, i, :], ps[:, 0:64], trih[:, 0:64], ALU.mult, [psb, TRIH], [ASB2[i]])
                ubank = [ring(), ring()]
                for c in range(8):
                    i, par = c // 2, c % 2
                    rows = slice(par * 64, (par + 1) * 64)
                    psu, psub = ubank[c % 2]
                    MM(psu[:, (c // 2) * 128:(c // 2 + 1) * 128], kptok[rows, i, :], vtok[rows, i, :], True, True,
                       [KPTOK, VTOK], [psub])
                for c in range(8):
                    gi = b * 8 + c
                    cur, nxt = gi % 2, (gi + 1) % 2
                    psu, psub = ubank[c % 2]
                    ACT(Sbf[:, c, :], Sst[:, h, cur, :], AF.Copy, [SST[h][cur]], [SBF[c]])
                    STT(Sst[:, h, nxt, :], Sst[:, h, cur, :], dec[:, c:c + 1], psu[:, (c // 2) * 128:(c // 2 + 1) * 128],
                        ALU.mult, ALU.add, [SST[h][cur], DEC, psub], [SST[h][nxt]])
                po, POB = pacc[h % 2], PACC[h % 2]
                for c in range(8):
                    i, par = c // 2, c % 2
                    rows = slice(par * 64, (par + 1) * 64)
                    csl = slice(c * 64, (c + 1) * 64)
                    MM(po[:, csl], Sbf[:, c, :], qp[:, csl], True, False, [SBF[c], QP], [POB])
                    MM(po[:, csl], vtok[rows, i, :], asb2[rows, i, :], False, True, [VTOK, ASB2[i]], [POB])
                ACT(sq[0][:], po[:], AF.Square, [POB], [SQ[0]])
                MM(pstat[:], cm(ONES_O), sq[0][:], True, True, [CBF, SQ[0]], [PSTAT])
                rmsstat_to_rstd()
                STT(ft[0][:], po[:], vec[:, V0 + 24:V0 + 25], rstd[:], ALU.mult, ALU.mult, [POB, VEC, RSTD], [FT[0]])
                ps, psb = proj_fm(l, f"hg{h}", hrhs, HT, KC)
                ACT(ft[1][:], ps[:], AF.Silu, [psb], [FT[1]])
                TT("pool", ogT[:, h, :], ft[0][:], ft[1][:], ALU.mult, [FT[0], FT[1]], [OGT[h]])
            for o in range(8):
                psa, psab = proj_fm(l, f"pa{o}", lambda k: attnT[:, k, :], ATT, 4)
                psb_, psbb = proj_fm(l, f"pb{o}", lambda k: ogT[:, k, :], OGT, 4)
                pga, pgab = proj_fm(l, f"ga{o}", hrhs, HT, KC)
                pgb, pgbb = proj_fm(l, f"gb{o}", hrhs, HT, KC)
                ACT(ft[0][:], pga[:], AF.Sigmoid, [pgab], [FT[0]])
                TT("dve", ft[1][:], psa[:], ft[0][:], ALU.mult, [psab, FT[0]], [FT[1]])
                ACT(ft[2][:], pgb[:], AF.Sigmoid, [pgbb], [FT[2]])
                TT("dve", ft[3][:], psb_[:], ft[2][:], ALU.mult, [psbb, FT[2]], [FT[3]])
                TT("pool", yT[:, o, :], ft[1][:], ft[3][:], ALU.add, [FT[1], FT[3]], [YT[o]])
            for o in range(8):
                ps, psb = proj_fm(l, f"wo{o}", lambda k: yT[:, k, :], YT, KC)
                TT("dve", xT[:, o, cols], ps[:], xT[:, o, cols], ALU.add, [psb, XT[o][b]], [XT[o][b]])

        def ffn_block(l, b):
            cols = slice(b * TB, (b + 1) * TB)
            norm_x(l, b, 8)
            hrhs = lambda k: hT[:, k, :]
            for f in range(NF):
                pg, pgb = proj_fm(l, f"gate{f}", hrhs, HT, KC)
                pu, pub = proj_fm(l, f"up{f}", hrhs, HT, KC)
                i = f % 2
                ACT(ft[i][:], pg[:], AF.Silu, [pgb], [FT[i]])
                TT("dve", actT[:, f, :], pu[:], ft[i][:], ALU.mult, [pub, FT[i]], [ACTT[f]])
            for o in range(8):
                wa, wab = WST.get(l, f"dna{o}")
                ps, psb = ring()
                for k in range(11):
                    MM(ps[:], wa[:, k, :], actT[:, k, :], k == 0, False, [wab, ACTT[k]], [psb])
                wb_, wbb = WST.get(l, f"dnb{o}")
                for k in range(11):
                    MM(ps[:], wb_[:, k, :], actT[:, 11 + k, :], False, k == 10, [wbb, ACTT[11 + k]], [psb])
                TT("dve", xT[:, o, cols], ps[:], xT[:, o, cols], ALU.add, [psb, XT[o][b]], [XT[o][b]])

        OUTB = Buf("out")
        for s in range(NSEQ):
            for b in range(NBLK):
                cols = slice(b * TB, (b + 1) * TB)
                def fn(s=s, cols=cols):
                    return nc.sync.dma_start(out=xT[:, :, cols], in_=xT_d[s, :, cols].rearrange("(c p) t -> p c t", p=128))
                S.dma("sp", fn, writes=[XT[c][b] for c in range(KC)], key=f"x_{b}")
            for l in range(L):
                for h in range(4):
                    MSET(Sst[:, h, 0, :], 0.0, [SST[h][0]])
                for b in range(NBLK):
                    mixer_block(l, b)
                    ffn_block(l, b)
                    if l == L - 1:
                        cols = slice(b * TB, (b + 1) * TB)
                        def fn(s=s, cols=cols):
                            return nc.sync.dma_start(out=out_d[s, :, cols].rearrange("(c p) t -> p c t", p=128), in_=xT[:, :, cols])
                        S.dma("sp", fn, reads=[XT[c][b] for c in range(KC)], writes=[OUTB], key=f"x_{b}")
        S.final_wait("sp", [OUTB] + [XT[c][b] for c in range(KC) for b in range(NBLK)])
        S.emit()
    return nc


def _prep_shared(inputs, L, T):
    wpack = np.stack([pack_layer_weights(inputs, l) for l in range(L)], axis=0)
    vecs = np.concatenate([pack_vecs(inputs, l) for l in range(L)], axis=1)
    cs, misc = const_tables(T)
    return wpack, np.ascontiguousarray(vecs), cs, misc


def make_wmap(wpack):
    L, _, WTOT = wpack.shape
    NWS = 1
    WPC = WTOT // NWS
    return {f"wpack{l}_{i}": np.ascontiguousarray(wpack[l, :, i * WPC:(i + 1) * WPC]) for l in range(L) for i in range(NWS)}


def kernel(**inputs):
    inputs = {k: np.asarray(v) for k, v in inputs.items()}
    x = inputs["x"]
    B, T, _ = x.shape
    L = inputs["w_in"].shape[0]
    nseq = B // NCORES
    wpack, vecs, cs, misc = _prep_shared(inputs, L, T)
    wmap = make_wmap(wpack)
    nc = build_program(nseq, T, L)
    in_maps = []
    for c in range(NCORES):
        xs = np.ascontiguousarray(x[c * nseq:(c + 1) * nseq].transpose(0, 2, 1))
        in_maps.append(dict(wmap, xT=xs, vecs=vecs, cs=cs, misc=misc))
    res = run_bass_kernel_spmd(nc, in_maps, core_ids=list(range(NCORES)))
    outs = [np.asarray(r["outT"]).transpose(0, 2, 1) for r in res.results]
    return np.ascontiguousarray(np.concatenate(outs, axis=0)).astype(np.float32)
```

```python
import math
from contextlib import ExitStack

import numpy as np
import concourse.bass as bass
import concourse.mybir as mybir
from concourse.bass_utils import run_bass_kernel_spmd

F32 = mybir.dt.float32
BF16 = mybir.dt.bfloat16
ALU = mybir.AluOpType
AF = mybir.ActivationFunctionType

D = 1024
KC = 8
DFF = 2816
NF = 22
TB = 512
EPS = 1e-6
NCORES = 8
SLOT = 2048
NSLOT = 3
ENGS = ("pe", "act", "dve", "pool", "sp")


class Buf:
    __slots__ = ("name", "w", "r")

    def __init__(self, name):
        self.name = name
        self.w = None
        self.r = {}


class Sched:
    def __init__(self, nc, stack):
        self.nc = nc
        self.stack = stack
        self.ops = {e: [] for e in ENGS}
        self.n = {e: 0 for e in ENGS}
        self.seen = {e: {} for e in ENGS}
        self.prog = {e: stack.enter_context(nc.semaphore("prog_" + e)) for e in ENGS}
        self.dma_sems = {}
        self.dma_cnt = {}

    def dma_sem(self, key):
        if key not in self.dma_sems:
            self.dma_sems[key] = self.stack.enter_context(self.nc.semaphore("d_" + key))
            self.dma_cnt[key] = 0
        return self.dma_sems[key]

    def _need(self, E, dep, waits):
        if dep[0] == "eng":
            _, E2, idx = dep
            if E2 == E and E == "pe":
                return
            key = ("eng", E2)
            if self.seen[E].get(key, 0) >= idx:
                return
            self.seen[E][key] = idx
            waits.append((self.prog[E2], idx))
        else:
            _, skey, cnt = dep
            key = ("dma", skey)
            if self.seen[E].get(key, 0) >= cnt:
                return
            self.seen[E][key] = cnt
            waits.append((self.dma_sems[skey], cnt))

    def _deps(self, E, reads, writes):
        waits = []
        for b in reads:
            if b.w is not None:
                self._need(E, b.w, waits)
        for b in writes:
            if b.w is not None:
                self._need(E, b.w, waits)
            for k, v in b.r.items():
                self._need(E, (k[0], k[1], v), waits)
        return waits

    def op(self, E, fn, reads=(), writes=()):
        waits = self._deps(E, reads, writes)
        self.n[E] += 1
        idx = self.n[E]
        self.ops[E].append((waits, fn, (self.prog[E], 1)))
        me = ("eng", E, idx)
        for b in reads:
            b.r[("eng", E)] = idx
        for b in writes:
            b.w = me
            b.r = {}
        return idx

    def dma(self, Q, fn, reads=(), writes=(), key=None):
        waits = self._deps(Q, reads, writes)
        sem = self.dma_sem(key)
        self.dma_cnt[key] += 16
        cnt = self.dma_cnt[key]
        self.ops[Q].append((waits, fn, (sem, 16)))
        me = ("dma", key, cnt)
        for b in reads:
            b.r[("dma", key)] = cnt
        for b in writes:
            b.w = me
            b.r = {}
        return cnt

    def final_wait(self, E, bufs):
        waits = []
        for b in bufs:
            if b.w is not None:
                self._need(E, b.w, waits)
            for k, v in b.r.items():
                self._need(E, (k[0], k[1], v), waits)
        self.ops[E].append((waits, None, None))

    def emit(self):
        nc = self.nc
        with nc.Block() as block:
            def run(E, eng):
                for waits, fn, inc in self.ops[E]:
                    for sem, val in waits:
                        eng.wait_ge(sem, val)
                    if fn is not None:
                        fn().then_inc(inc[0], inc[1])

            @block.tensor
            def _(e):
                run("pe", e)

            @block.scalar
            def _(e):
                run("act", e)

            @block.vector
            def _(e):
                run("dve", e)

            @block.gpsimd
            def _(e):
                run("pool", e)

            @block.sync
            def _(e):
                run("sp", e)


def weight_items():
    it = []
    for j in range(3):
        it.append((f"cq{j}", 8, 128))
    for j in range(2):
        it.append((f"ckv{j}", 8, 128))
    it.append(("kpe", 8, 64))
    for j in range(4):
        it.append((f"uk{j}", 2, 128))
    it.append(("uv", 2, 512))
    for h in range(8):
        it.append((f"uq{h}", 3, 128))
    for h in range(4):
        it.append((f"hf{h}", 8, 128))
        it.append((f"hq{h}", 8, 128))
        it.append((f"hi{h}", 8, 128))
        it.append((f"hg{h}", 8, 128))
    for o in range(8):
        it.append((f"pa{o}", 4, 128))
        it.append((f"pb{o}", 4, 128))
        it.append((f"ga{o}", 8, 128))
        it.append((f"gb{o}", 8, 128))
    for o in range(8):
        it.append((f"wo{o}", 8, 128))
    for f in range(NF):
        it.append((f"gate{f}", 8, 128))
        it.append((f"up{f}", 8, 128))
    for o in range(8):
        it.append((f"dna{o}", 11, 128))
        it.append((f"dnb{o}", 11, 128))
    return it


def weight_groups():
    groups = []
    index = {}
    cur_off = 0
    cur_size = 0
    start = 0
    for name, kc, m in weight_items():
        n = kc * m
        if cur_size + n > SLOT:
            groups.append((start, cur_size))
            start += cur_size
            cur_size = 0
        index[name] = (len(groups), cur_size, kc, m)
        cur_size += n
    groups.append((start, cur_size))
    total = start + cur_size
    total = ((total + 8191) // 8192) * 8192
    return groups, index, total


IN_OFF = {}
_o = 0
for _n, _s in (("cq", 384), ("ckv", 256), ("kpe", 32), ("hq", 512), ("hf", 512), ("hi", 512), ("hg", 512),
               ("ga", 1024), ("gb", 1024)):
    IN_OFF[_n] = _o
    _o += _s

NV = 40


def pack_layer_weights(inp, l):
    w_in = inp["w_in"][l]
    cols = {}

    def rng(a, n):
        return list(range(a, a + n))
    for j in range(3):
        cols[f"cq{j}"] = (w_in, rng(IN_OFF["cq"] + j * 128, 128))
    for j in range(2):
        cols[f"ckv{j}"] = (w_in, rng(IN_OFF["ckv"] + j * 128, 128))
    kp = IN_OFF["kpe"]
    cols["kpe"] = (w_in, rng(kp, 32) + rng(kp + 16, 16) + rng(kp, 16))
    ukv = inp["mla_w_ukv"][l]
    for j in range(4):
        cols[f"uk{j}"] = (ukv, rng((2 * j) * 128, 64) + rng((2 * j + 1) * 128, 64))
    vc = []
    for h in range(8):
        vc += rng(h * 128 + 64, 64)
    cols["uv"] = (ukv, vc)
    uq = inp["mla_w_uq"][l]
    for h in range(8):
        cols[f"uq{h}"] = (uq, rng(h * 96, 96) + rng(h * 96 + 80, 16) + rng(h * 96 + 64, 16))
    for h in range(4):
        for nm in ("hf", "hq", "hi", "hg"):
            cols[f"{nm}{h}"] = (w_in, rng(IN_OFF[nm] + h * 128, 128))
    for o in range(8):
        cols[f"pa{o}"] = (inp["w_proj_a"][l], rng(o * 128, 128))
        cols[f"pb{o}"] = (inp["w_proj_b"][l], rng(o * 128, 128))
        cols[f"ga{o}"] = (w_in, rng(IN_OFF["ga"] + o * 128, 128))
        cols[f"gb{o}"] = (w_in, rng(IN_OFF["gb"] + o * 128, 128))
        cols[f"wo{o}"] = (inp["w_out"][l], rng(o * 128, 128))
    for f in range(NF):
        cols[f"gate{f}"] = (inp["w_gate"][l], rng(f * 128, 128))
        cols[f"up{f}"] = (inp["w_up"][l], rng(f * 128, 128))
    wd = inp["w_down"][l]
    for o in range(8):
        cols[f"dna{o}"] = (wd[0:11 * 128], rng(o * 128, 128))
        cols[f"dnb{o}"] = (wd[11 * 128:22 * 128], rng(o * 128, 128))
    groups, index, total = weight_groups()
    out = np.zeros((128, total), np.float32)
    for name, kc, m in weight_items():
        W, cl = cols[name]
        g, off, _, _ = index[name]
        base = groups[g][0] + off
        blk = W[:kc * 128][:, cl].reshape(kc, 128, m).transpose(1, 0, 2).reshape(128, kc * m)
        out[:, base:base + kc * m] = blk
    return out


def pack_vecs(inp, l):
    v = np.zeros((128, NV), np.float32)
    v[:, 0:8] = inp["norm_mix"][l].reshape(8, 128).T
    v[:, 8:16] = inp["norm_ffn"][l].reshape(8, 128).T
    v[:, 16:19] = inp["mla_norm_cq"][l].reshape(3, 128).T
    v[:, 19:21] = inp["mla_norm_ckv"][l].reshape(2, 128).T
    qn = inp["mla_q_norm"][l]
    kn = inp["mla_k_norm"][l]
    v[0:96, 21] = qn
    v[96:112, 21] = qn[80:96]
    v[112:128, 21] = qn[64:80]
    v[0:64, 22] = kn[0:64]
    v[64:128, 22] = kn[0:64]
    v[64:96, 23] = kn[64:96]
    v[96:112, 23] = kn[80:96]
    v[112:128, 23] = kn[64:80]
    v[:, 24] = inp["hg_out_norm"][l]
    for ll in range(inp["hg_lb_logits"].shape[0]):
        v[:, 25 + 4 * ll:29 + 4 * ll] = inp["hg_lb_logits"][ll].reshape(4, 128).T
    return v


def const_tables(T):
    pos = np.arange(T, dtype=np.float32)
    inv_freq = (1.0 / (np.float32(10000.0) ** (np.arange(0, 32, 2, dtype=np.float32) / np.float32(32)))).astype(np.float32)
    ang = (pos[:, None] * inv_freq[None, :]).astype(np.float32)
    cos = np.cos(ang).astype(np.float32).T
    sin = np.sin(ang).astype(np.float32).T
    cs = np.ones((128, T), np.float32)
    cs[64:80] = cos
    cs[80:96] = cos
    cs[96:112] = -sin
    cs[112:128] = sin
    k = np.arange(128)
    tri_att = (k[None, :] >= k[:, None]).astype(np.float32)
    ident = np.eye(128, dtype=np.float32)
    s64 = k % 64
    t64 = np.arange(64)
    tri_h = (t64[None, :] >= s64[:, None]).astype(np.float32)
    tri_h4 = np.tile(tri_h, (1, 4))
    scanmask = np.ones((128, TB), np.float32)
    scanmask[:, ::64] = 0.0
    misc = np.concatenate([tri_att, ident, tri_h4, scanmask], axis=1).astype(np.float32)
    return cs, misc


def build_program(NSEQ, T, L, dbg=False):
    NBLK = T // TB
    NT = T // 128
    groups, windex, WTOT = weight_groups()
    NG = len(groups)
    nc = bass.Bass("TRN2", target_bir_lowering=False)
    xT_d = nc.dram_tensor("xT", [NSEQ, D, T], F32, kind="ExternalInput").ap()
    NWS = 1
    WPC = WTOT // NWS
    wp_d = [[nc.dram_tensor(f"wpack{l}_{i}", [128, WPC], F32, kind="ExternalInput").ap() for i in range(NWS)]
            for l in range(L)]
    vec_d = nc.dram_tensor("vecs", [128, L * NV], F32, kind="ExternalInput").ap()
    cs_d = nc.dram_tensor("cs", [128, T], F32, kind="ExternalInput").ap()
    misc_d = nc.dram_tensor("misc", [128, 512 + TB], F32, kind="ExternalInput").ap()
    out_d = nc.dram_tensor("outT", [NSEQ, D, T], F32, kind="ExternalOutput").ap()
    import os as _os
    if _os.environ.get("KDUMMY"):
        nc.dram_tensor("dummyin", [128, WTOT], F32, kind="ExternalInput")
    wbf_v = [wp_d[l][0].bitcast(BF16) for l in range(L)]

    with ExitStack() as st:
        S = Sched(nc, st)

        def sb(name, shape, dt):
            return st.enter_context(nc.sbuf_tensor(name, shape, dt))

        xT = sb("xT_sb", [128, KC, T], F32)
        XT = [[Buf(f"xT{c}_{b}") for b in range(NBLK)] for c in range(KC)]
        cs = sb("cs_sb", [128, T], F32); CS = Buf("cs")
        KT = sb("KT", [128, 8, T], BF16)
        KTB = [Buf(f"KT{h}") for h in range(8)]
        Vx = sb("Vx", [128, NT, 4, 192], BF16)
        VXB = [Buf(f"Vx{i}") for i in range(NT)]
        VXONES = Buf("vxones")
        skall = sb("skall", [128, NT * 8], F32)
        SK = [Buf(f"sk{b}") for b in range(NBLK)]
        wslot = [sb(f"wslot{i}", [128, SLOT], BF16) for i in range(NSLOT)]
        WS = [Buf(f"wslot{i}") for i in range(NSLOT)]
        vec = sb("vec_sb", [128, L * NV], F32); VEC = Buf("vec")
        cbf = sb("cbf", [128, 128 * 8], BF16); CBF = Buf("cbf")
        cvec = sb("cvec", [128, 8], F32); CVEC = Buf("cvec")
        lbt = sb("lbt", [128, L * 4], F32); LBT = Buf("lbt")
        lbtmp = sb("lbtmp", [128, 40], F32); LBTMP = Buf("lbtmp")
        gqs = sb("gqs", [128, L], F32); GQS = Buf("gqs")
        Sst = sb("Sst", [128, 4, 2, 128], F32)
        SST = [[Buf(f"S{h}_{i}") for i in range(2)] for h in range(4)]
        Sbf = sb("Sbf", [128, 8, 128], BF16)
        SBF = [Buf(f"Sbf{i}") for i in range(8)]
        hT = sb("hT", [128, KC, TB], BF16)
        HT = [Buf(f"hT{c}") for c in range(KC)]
        NSL = 22
        bfs = sb("bfs", [128, NSL, TB], BF16)
        BFS = [Buf(f"bfs{i}") for i in range(NSL)]
        yT, YT = bfs, BFS
        sqk = bfs[:, 8:13, :]; SQK = BFS[8:13]
        cqn = bfs[:, 0:3, :]; CQN = BFS[0:3]
        ckvn = bfs[:, 3:5, :]; CKVN = BFS[3:5]
        qTh = [bfs[:, 5 + i, :] for i in range(2)]; QTH = BFS[5:7]
        pT = [bfs[:, 7 + i, :] for i in range(3)]; PT = BFS[7:10]
        qp = bfs[:, 10, :]; QP = BFS[10]
        kt_ = bfs[:, 11, :]; KTL = BFS[11]
        kpT = bfs[:, 12, :]; KPT = BFS[12]
        attnT = bfs[:, 13:17, :]; ATT = BFS[13:17]
        ogT = bfs[:, 17:21, :]; OGT = BFS[17:21]
        actT, ACTT = bfs, BFS
        sq = [sb(f"sq{i}", [128, TB], BF16) for i in range(2)]
        SQ = [Buf(f"sq{i}") for i in range(2)]
        rstd = sb("rstd", [128, TB], F32); RSTD = Buf("rstd")
        ft = [sb(f"ft{i}", [128, TB], F32) for i in range(6)]
        FT = [Buf(f"ft{i}") for i in range(6)]
        cqf = [ft[0], ft[1], ft[2]]; CQF = FT[0:3]
        rk, RK = ft[4], FT[4]
        rk2, RK2 = ft[5], FT[5]
        kptok = sb("kptok", [128, 4, 128], BF16); KPTOK = Buf("kptok")
        vtok = sb("vtok", [128, 4, 128], BF16); VTOK = Buf("vtok")
        asb2 = sb("asb2", [128, 4, 64], BF16); ASB2 = [Buf(f"asb2_{i}") for i in range(4)]
        dec = sb("dec", [128, 8], F32); DEC = Buf("dec")
        kst_sb = sb("kst_sb", [128, 32], F32); KSTSB = Buf("kstsb")
        scanm = sb("scanm", [128, TB], F32); SCANM = Buf("scanm")
        trih = sb("trih", [128, 256], BF16); TRIH = Buf("trih")

        NRING = 4
        pring = [st.enter_context(nc.psum_tensor(f"pr{i}", [128, 512], F32)) for i in range(NRING)]
        PR = [Buf(f"pr{i}") for i in range(NRING)]
        pstat = st.enter_context(nc.psum_tensor("pstat", [128, 512], F32)); PSTAT = Buf("pstat")
        pacc = [st.enter_context(nc.psum_tensor(f"pacc{i}", [128, 512], F32)) for i in range(2)]
        PACC = [Buf(f"pacc{i}") for i in range(2)]
        ptr = st.enter_context(nc.psum_tensor("ptr", [128, 1024], BF16)); PTR = Buf("ptr")
        ring_i = [0]

        def ring():
            i = ring_i[0] % NRING
            ring_i[0] += 1
            return pring[i], PR[i]

        def ACT(out, in_, func, reads, writes, scale=None, bias=None):
            kw = {}
            if scale is not None:
                kw["scale"] = scale
            if bias is not None:
                kw["bias"] = bias
            S.op("act", lambda: nc.scalar.activation(out=out, in_=in_, func=func, **kw), reads, writes)

        def TT(eng, out, in0, in1, op, reads, writes):
            e = nc.vector if eng == "dve" else nc.gpsimd
            S.op(eng, lambda: e.tensor_tensor(out=out, in0=in0, in1=in1, op=op), reads, writes)

        def STT(out, in0, scalar, in1, op0, op1, reads, writes):
            S.op("dve", lambda: nc.vector.scalar_tensor_tensor(out=out, in0=in0, scalar=scalar, in1=in1,
                                                               op0=op0, op1=op1), reads, writes)

        def TS(eng, out, in0, s1, s2, op0, op1, reads, writes):
            e = nc.vector if eng == "dve" else nc.gpsimd
            if s2 is None:
                S.op(eng, lambda: e.tensor_scalar(out=out, in0=in0, scalar1=s1, scalar2=None, op0=op0), reads, writes)
            else:
                S.op(eng, lambda: e.tensor_scalar(out=out, in0=in0, scalar1=s1, scalar2=s2, op0=op0, op1=op1),
                     reads, writes)

        def CP(eng, out, in_, reads, writes):
            e = nc.vector if eng == "dve" else nc.gpsimd
            S.op(eng, lambda: e.tensor_copy(out=out, in_=in_), reads, writes)

        def MM(out, lhsT, rhs, start, stop, reads, writes):
            S.op("pe", lambda: nc.tensor.matmul(out, lhsT=lhsT, rhs=rhs, start=start, stop=stop), reads, writes)

        def RECIP(out, in_, reads, writes):
            S.op("dve", lambda: nc.vector.reciprocal(out=out, in_=in_), reads, writes)

        def SCAN(out, d0, d1, reads, writes):
            S.op("dve", lambda: nc.vector.tensor_tensor_scan(out=out, data0=d0, data1=d1, initial=0.0,
                                                             op0=ALU.mult, op1=ALU.add), reads, writes)

        def TRANS(out, in_, ident, reads, writes):
            S.op("pe", lambda: nc.tensor.transpose(out, in_, ident), reads, writes)

        def MSET(ap, val, writes):
            S.op("pool", lambda: nc.gpsimd.memset(ap, val), (), writes)

        S.dma("sp", lambda: nc.sync.dma_start(out=vec[:], in_=vec_d), writes=[VEC], key="vec")
        S.dma("sp", lambda: nc.sync.dma_start(out=cs[:], in_=cs_d), writes=[CS], key="cs")
        S.dma("sp", lambda: nc.sync.dma_start(out=ft[0][:], in_=misc_d[:, 0:512]), writes=[FT[0]], key="misc")
        S.dma("sp", lambda: nc.sync.dma_start(out=scanm[:], in_=misc_d[:, 512:512 + TB]), writes=[SCANM], key="scanm")
        ONES_D, ONES_CQ, ONES_CKV, ONES_Q, ONES_O, ONES_K, TRI, IDN = range(8)

        def cm(i):
            return cbf[:, i * 128:(i + 1) * 128]
        MSET(cm(ONES_D), 1.0 / 1024, [CBF])
        MSET(cm(ONES_CQ), 1.0 / 384, [CBF])
        MSET(cm(ONES_CKV), 1.0 / 256, [CBF])
        MSET(cm(ONES_Q), 1.0 / 96, [CBF])
        MSET(cbf[96:128, ONES_Q * 128:(ONES_Q + 1) * 128], 0.0, [CBF])
        MSET(cm(ONES_O), 1.0 / 128, [CBF])
        MSET(cm(ONES_K), 0.0, [CBF])
        MSET(cbf[64:96, ONES_K * 128:ONES_K * 128 + 8], 1.0 / 96, [CBF])
        MSET(cbf[0:64, ONES_K * 128 + 8:ONES_K * 128 + 9], 1.0 / 96, [CBF])
        MSET(cbf[64:128, ONES_K * 128 + 9:ONES_K * 128 + 10], 1.0 / 96, [CBF])
        for i_ in range(NSL):
            MSET(bfs[:, i_, :], 0.0, [BFS[i_]])
        CP("pool", cm(TRI), ft[0][:, 0:128], [FT[0]], [CBF])
        CP("pool", cm(IDN), ft[0][:, 128:256], [FT[0]], [CBF])
        CP("pool", trih[:], ft[0][:, 256:512], [FT[0]], [TRIH])
        MSET(cvec[:, 0:1], EPS, [CVEC])
        MSET(cvec[:, 1:2], 1.0, [CVEC])
        MSET(cvec[:, 2:3], 0.0, [CVEC])
        C_EPS, C_ONE, C_ZERO = cvec[:, 0:1], cvec[:, 1:2], cvec[:, 2:3]
        MSET(Vx[:, :, :, 64:128], 1.0, [VXONES])
        ll = [vec[:, l * NV + 25: l * NV + 25 + 4 * L] for l in range(1)][0]
        mx, e_all, ssum, rs, cum = lbtmp[:, 0:4], lbtmp[:, 4:4 + 4 * L], lbtmp[:, 16:20], lbtmp[:, 20:24], lbtmp[:, 24:28]
        CP("dve", mx, ll[:, 0:4], [VEC], [LBTMP])
        for l in range(1, L):
            TT("dve", mx, mx, ll[:, 4 * l:4 * l + 4], ALU.max, [VEC, LBTMP], [LBTMP])
        for l in range(L):
            TT("dve", e_all[:, 4 * l:4 * l + 4], ll[:, 4 * l:4 * l + 4], mx, ALU.subtract, [VEC, LBTMP], [LBTMP])
        ACT(e_all, e_all, AF.Exp, [LBTMP], [LBTMP])
        CP("dve", ssum, e_all[:, 0:4], [LBTMP], [LBTMP])
        for l in range(1, L):
            TT("dve", ssum, ssum, e_all[:, 4 * l:4 * l + 4], ALU.add, [LBTMP], [LBTMP])
        RECIP(rs, ssum, [LBTMP], [LBTMP])
        for l in range(L):
            TT("dve", e_all[:, 4 * l:4 * l + 4], e_all[:, 4 * l:4 * l + 4], rs, ALU.mult, [LBTMP], [LBTMP])
        CP("dve", cum, e_all[:, 0:4], [LBTMP], [LBTMP])
        for l in range(L):
            if l > 0:
                TT("dve", cum, cum, e_all[:, 4 * l:4 * l + 4], ALU.add, [LBTMP], [LBTMP])
            TT("dve", lbt[:, 4 * l:4 * l + 4], cum, e_all[:, 0:4], ALU.subtract, [LBTMP], [LBT])
            TS("dve", lbt[:, 4 * l:4 * l + 4], lbt[:, 4 * l:4 * l + 4], 0.0, None, ALU.max, ALU.bypass, [LBT], [LBT])
        for l in range(L):
            TS("dve", gqs[:, l:l + 1], vec[:, l * NV + 21:l * NV + 22], float(96.0 ** -0.5), None, ALU.mult, ALU.bypass,
               [VEC], [GQS])

        RPC = 4
        PCS = RPC * T
        while WTOT % PCS:
            RPC //= 2
            PCS = RPC * T
        NCHK = WTOT // PCS
        WCH = []
        jj = 0
        for l in range(L):
            row = []
            DCH = [Buf(f"dch{l}_{k}") for k in range(NCHK)]
            for k in range(NCHK):
                c0, c1 = k * PCS, (k + 1) * PCS
                hb = jj % (8 // RPC)
                r0 = hb * RPC
                jj += 1
                bq = Buf(f"wch{l}_{c0}")
                xbufs = [XT[c][bb_] for c in range(r0, r0 + RPC) for bb_ in range(NBLK)]
                kbufs = KTB[r0:r0 + RPC]
                xs_ = xT[:, r0:r0 + RPC, :]
                ks_ = KT[:, r0:r0 + RPC, :]

                def ld(l=l, c0=c0, c1=c1, xs_=xs_):
                    return nc.sync.dma_start(out=xs_, in_=wp_d[l][0][:, c0:c1].rearrange("p (r t) -> p r t", t=T))
                S.dma("sp", ld, reads=[DCH[k]], writes=xbufs, key=f"wld{hb}")
                for r in range(RPC):
                    eng = ("act", "dve", "pool")[(jj + r) % 3]
                    c = r0 + r
                    if eng == "act":
                        ACT(KT[:, c, :], xT[:, c, :], AF.Copy, XT[c], [KTB[c]])
                    else:
                        CP(eng, KT[:, c, :], xT[:, c, :], XT[c], [KTB[c]])

                def stf(l=l, c0=c0, c1=c1, ks_=ks_):
                    return nc.sync.dma_start(out=wbf_v[l][:, c0:c1].rearrange("p (r t) -> p r t", t=T), in_=ks_)
                S.dma("sp", stf, reads=kbufs, writes=[bq, DCH[k // 2]], key=f"wst{hb}")
                row.append((c0, c1, bq))
            WCH.append(row)
        MSET(KT[96:128, :, :], 0.0, KTB)

        class WStream:
            def __init__(self):
                self.seq = []
                self.pos = 0
                self.cur = -1
                self.curkey = None

            def plan(self, order):
                self.seq = order

            def _load(self, i):
                l, g = self.seq[i]
                off, size = groups[g]
                slot = i % NSLOT
                rd = [b for (c0, c1, b) in WCH[l] if c0 < off + size and c1 > off]

                def fn(l=l, off=off, size=size, slot=slot):
                    return nc.sync.dma_start(out=wslot[slot][:, 0:size], in_=wbf_v[l][:, off:off + size])
                S.dma("sp", fn, reads=rd, writes=[WS[slot]], key=f"ws{slot}")

            def start(self):
                for i in range(min(NSLOT, len(self.seq))):
                    self._load(i)
                self.pos = min(NSLOT, len(self.seq))
                self.cur = 0

            def get(self, l, name):
                g, off, kc, m = windex[name]
                while self.seq[self.cur] != (l, g):
                    self.cur += 1
                    if self.pos < len(self.seq) and self.pos - self.cur < NSLOT - 1 + 1:
                        pass
                    while self.pos < len(self.seq) and self.pos < self.cur + NSLOT:
                        self._load(self.pos)
                        self.pos += 1
                slot = self.cur % NSLOT
                ap = wslot[slot][:, off:off + kc * m].rearrange("p (k m) -> p k m", m=m)
                return ap, WS[slot]

        WST = WStream()
        order = []
        for s in range(NSEQ):
            for l in range(L):
                for b in range(NBLK):
                    for g in range(NG):
                        order.append((l, g))
        WST.plan(order)
        WST.start()

        def rmsstat_to_rstd(reads_extra=()):
            ACT(rstd[:], pstat[:], AF.Ln, [PSTAT, CVEC], [RSTD], bias=C_EPS)
            ACT(rstd[:], rstd[:], AF.Exp, [RSTD], [RSTD], scale=-0.5)

        def norm_x(l, b, gcol):
            cols = slice(b * TB, (b + 1) * TB)
            for c in range(KC):
                i = c % 2
                ACT(sq[i][:], xT[:, c, cols], AF.Square, [XT[c][b]], [SQ[i]])
                MM(pstat[:], cm(ONES_D), sq[i][:], c == 0, c == KC - 1, [CBF, SQ[i]], [PSTAT])
            rmsstat_to_rstd()
            for c in range(KC):
                STT(hT[:, c, :], xT[:, c, cols], vec[:, l * NV + gcol + c:l * NV + gcol + c + 1], rstd[:],
                    ALU.mult, ALU.mult, [XT[c][b], VEC, RSTD], [HT[c]])

        def proj_fm(l, name, rhs_ap, rhs_bufs, nk, M=128, out_rows=None):
            w, wb = WST.get(l, name)
            ps, psb = ring()
            o = ps[:] if out_rows is None else ps[out_rows[0]:out_rows[1], :]
            for k in range(nk):
                MM(o, w[:, k, :], rhs_ap(k), k == 0, k == nk - 1, [wb] + list(rhs_bufs), [psb])
            return ps, psb

        def mixer_block(l, b):
            cols = slice(b * TB, (b + 1) * TB)
            V0 = l * NV
            norm_x(l, b, 0)
            hrhs = lambda k: hT[:, k, :]
            for j in range(3):
                ps, psb = proj_fm(l, f"cq{j}", hrhs, HT, KC)
                i = j % 2
                ACT(sq[i][:], ps[:], AF.Square, [psb], [SQ[i]])
                ACT(cqf[j][:], ps[:], AF.Copy, [psb], [CQF[j]])
                MM(pstat[:], cm(ONES_CQ), sq[i][:], j == 0, j == 2, [CBF, SQ[i]], [PSTAT])
            rmsstat_to_rstd()
            for j in range(3):
                STT(cqn[:, j, :], cqf[j][:], vec[:, V0 + 16 + j:V0 + 17 + j], rstd[:], ALU.mult, ALU.mult,
                    [CQF[j], VEC, RSTD], [CQN[j]])
            for j in range(2):
                ps, psb = proj_fm(l, f"ckv{j}", hrhs, HT, KC)
                i = j % 2
                ACT(sq[i][:], ps[:], AF.Square, [psb], [SQ[i]])
                ACT(cqf[j][:], ps[:], AF.Copy, [psb], [CQF[j]])
                MM(pstat[:], cm(ONES_CKV), sq[i][:], j == 0, j == 1, [CBF, SQ[i]], [PSTAT])
            rmsstat_to_rstd()
            for j in range(2):
                STT(ckvn[:, j, :], cqf[j][:], vec[:, V0 + 19 + j:V0 + 20 + j], rstd[:], ALU.mult, ALU.mult,
                    [CQF[j], VEC, RSTD], [CKVN[j]])
            ps, psb = proj_fm(l, "kpe", hrhs, HT, KC, M=64, out_rows=(64, 128))
            ACT(sqk[64:128, 4, :], ps[64:128, :], AF.Square, [psb], [SQK[4]])
            STT(rk[64:128, :], ps[64:128, :], vec[64:128, V0 + 23:V0 + 24], cs[64:128, cols], ALU.mult, ALU.mult,
                [psb, VEC, CS], [RK])
            ACT(rk2[64:96, :], rk[96:128, :], AF.Copy, [RK], [RK2])
            TT("pool", rk[64:96, :], rk[64:96, :], rk2[64:96, :], ALU.add, [RK, RK2], [RK])
            for h in range(8):
                CP("pool", KT[64:96, h, cols], rk[64:96, :], [RK], [KTB[h]])
            ckrhs = lambda k: ckvn[:, k, :]
            for j in range(4):
                ps, psb = proj_fm(l, f"uk{j}", ckrhs, CKVN, 2)
                ACT(sqk[:, j, :], ps[:], AF.Square, [psb], [SQK[j]])
                ACT(KT[0:64, 2 * j, cols], ps[0:64, :], AF.Copy, [psb, VEC], [KTB[2 * j]], scale=vec[0:64, V0 + 22:V0 + 23])
                ACT(KT[0:64, 2 * j + 1, cols], ps[64:128, :], AF.Copy, [psb, VEC], [KTB[2 * j + 1]],
                    scale=vec[64:128, V0 + 22:V0 + 23])
            for i in range(4):
                tsl = slice(i * 128, (i + 1) * 128)
                MM(pstat[:, i * 8:(i + 1) * 8], sqk[:, 4, tsl], cbf[:, ONES_K * 128:ONES_K * 128 + 8], True, False,
                   [SQK[4], CBF], [PSTAT])
                for h in range(8):
                    sel = ONES_K * 128 + 8 + (h % 2)
                    MM(pstat[:, i * 8 + h:i * 8 + h + 1], sqk[:, h // 2, tsl],
                       cbf[:, sel:sel + 1], False, h == 7, [SQK[h // 2], CBF], [PSTAT])
            ACT(kst_sb[:, :], pstat[:, 0:32], AF.Ln, [PSTAT, CVEC], [KSTSB], bias=C_EPS)
            ACT(skall[:, b * 32:(b + 1) * 32], kst_sb[:, :], AF.Exp, [KSTSB], [SK[b]], scale=-0.5)
            wv, wvb = WST.get(l, "uv")
            for i in range(4):
                tsl = slice(i * 128, (i + 1) * 128)
                ps, psb = ring()
                for k in range(2):
                    MM(ps[:], ckvn[:, k, tsl], wv[:, k, :], k == 0, k == 1, CKVN + [wvb], [psb])
                pv = ps[:].rearrange("p (j e d) -> p j e d", e=2, d=64)
                ti = b * 4 + i
                CP("dve", Vx[:, ti, :, 0:64], pv[:, :, 0, :], [psb], [VXB[ti]])
                ACT(Vx[:, ti, :, 128:192], pv[:, :, 1, :], AF.Copy, [psb], [VXB[ti]])
            cqrhs = lambda k: cqn[:, k, :]
            def q_prep(h):
                ps, psb = proj_fm(l, f"uq{h}", cqrhs, CQN, 3)
                i = h % 2
                ACT(sq[i][:], ps[:], AF.Square, [psb], [SQ[i]])
                MM(pstat[:], cm(ONES_Q), sq[i][:], True, True, [CBF, SQ[i]], [PSTAT])
                rmsstat_to_rstd()
                STT(ft[0][:], ps[:], gqs[:, l:l + 1], rstd[:], ALU.mult, ALU.mult, [psb, GQS, RSTD], [FT[0]])
                TT("pool", ft[1][:], ft[0][:], cs[:, cols], ALU.mult, [FT[0], CS], [FT[1]])
                qt, QB = qTh[h % 2], QTH[h % 2]
                CP("pool", qt[0:64, :], ft[1][0:64, :], [FT[1]], [QB])
                ACT(ft[2][64:96, :], ft[1][96:128, :], AF.Copy, [FT[1]], [FT[2]])
                TT("pool", qt[64:96, :], ft[1][64:96, :], ft[2][64:96, :], ALU.add, [FT[1], FT[2]], [QB])
                CP("pool", qt[96:128, :], ft[1][96:128, :], [FT[1]], [QB])

            def attn(h):
                qt, QB = qTh[h % 2], QTH[h % 2]
                j, par = h // 2, h % 2
                po, POB = pacc[h % 2], PACC[h % 2]
                nkt = 4 * b + 4
                for kt in range(nkt):
                    r = kt - 4 * b
                    q0 = max(0, r) * 128
                    nq = TB - q0
                    ps2, ps2b = ring()
                    MM(ps2[:, 0:nq], KT[:, h, kt * 128:(kt + 1) * 128], qt[:, q0:TB], True, True,
                       [KTB[h], QB], [ps2b])
                    pt, PTB = pT[kt % 3], PT[kt % 3]
                    ACT(pt[:, 0:nq], ps2[:, 0:nq], AF.Exp, [ps2b, SK[kt // 4]], [PTB],
                        scale=skall[:, kt * 8 + h:kt * 8 + h + 1])
                    if r >= 0:
                        TT("pool", pt[:, 0:128], pt[:, 0:128], cm(TRI), ALU.mult, [PTB, CBF], [PTB])
                    MM(po[:, q0:TB], Vx[:, kt, j, par * 64:par * 64 + 128], pt[:, 0:nq], kt == 0, kt == nkt - 1,
                       [VXB[kt], VXONES, PTB], [POB])
                if par == 0:
                    RECIP(ft[3][64:128, :], po[64:128, :], [POB], [FT[3]])
                    TT("dve", attnT[0:64, j, :], po[0:64, :], ft[3][64:128, :], ALU.mult, [POB, FT[3]], [ATT[j]])
                else:
                    RECIP(ft[3][0:64, :], po[0:64, :], [POB], [FT[3]])
                    TT("dve", attnT[64:128, j, :], po[64:128, :], ft[3][0:64, :], ALU.mult, [POB, FT[3]], [ATT[j]])

            q_prep(0)
            for h in range(8):
                if h < 7:
                    q_prep(h + 1)
                attn(h)
            for h in range(4):
                lbc = lbt[:, 4 * l + h:4 * l + h + 1]
                ps, psb = proj_fm(l, f"hf{h}", hrhs, HT, KC)
                e, l1, l2, bb, kk, tmp = ft[0], ft[1], ft[2], ft[3], ft[4], ft[5]
                E, L1, L2, BB, KK, TMP = FT[0], FT[1], FT[2], FT[3], FT[4], FT[5]
                ACT(e[:], ps[:], AF.Exp, [psb], [E], scale=-1.0)
                ACT(l1[:], e[:], AF.Ln, [E, LBT, CVEC], [L1], scale=lbc, bias=C_ONE)
                ACT(l2[:], e[:], AF.Ln, [E, CVEC], [L2], bias=C_ONE)
                TT("pool", l1[:], l1[:], l2[:], ALU.subtract, [L1, L2], [L1])
                SCAN(bb[:], scanm[:], l1[:], [SCANM, L1], [BB])
                ACT(kk[:], l1[:], AF.Exp, [L1], [KK])
                TS("pool", kk[:], kk[:], -1.0, 1.0, ALU.mult, ALU.add, [KK], [KK])
                b3 = bb[:].rearrange("p (c j) -> p c j", j=64)
                ACT(dec[:, :], b3[:, :, 63], AF.Exp, [BB], [DEC])
                ACT(e[:], bb[:], AF.Exp, [BB], [E])
                ACT(l2[:], bb[:], AF.Exp, [BB], [L2], scale=-1.0)
                TT("pool", tmp[:].rearrange("p (c j) -> p c j", j=64), b3, b3[:, :, 63:64].to_broadcast([128, 8, 64]),
                   ALU.subtract, [BB], [TMP])
                ACT(tmp[:], tmp[:], AF.Exp, [TMP], [TMP], scale=-1.0)
                TT("pool", kt_[:], kk[:], l2[:], ALU.mult, [KK, L2], [KTL])
                TT("pool", kpT[:], kk[:], tmp[:], ALU.mult, [KK, TMP], [KPT])
                ps, psb = proj_fm(l, f"hq{h}", hrhs, HT, KC)
                TT("dve", qp[:], ps[:], e[:], ALU.mult, [psb, E], [QP])
                wv, wvb = WST.get(l, f"hi{h}")
                ps, psb = ring()
                for i in range(4):
                    for k in range(KC):
                        MM(ps[:, i * 128:(i + 1) * 128], hT[:, k, i * 128:(i + 1) * 128], wv[:, k, :], k == 0, k == KC - 1,
                           HT + [wvb], [psb])
                CP("dve", vtok[:].rearrange("p i d -> p (i d)"), ps[:], [psb], [VTOK])
                for i in range(4):
                    TRANS(ptr[:, i * 128:(i + 1) * 128], kpT[:, i * 128:(i + 1) * 128], cm(IDN), [KPT, CBF], [PTR])
                CP("dve", kptok[:].rearrange("p i d -> p (i d)"), ptr[:, 0:512], [PTR], [KPTOK])
                for i in range(4):
                    ps, psb = ring()
                    for par in range(2):
                        c = 2 * i + par
                        csl = slice(c * 64, (c + 1) * 64)
                        MM(ps[par * 64:(par + 1) * 64, 0:64], kt_[:, csl], qp[:, csl], True, True, [KTL, QP], [psb])
                    TT("dve", asb2[:, i, :], ps[:, 0:64], trih[:, 0:64], ALU.mult, [psb, TRIH], [ASB2[i]])
                ubank = [ring(), ring()]
                for c in range(8):
                    i, par = c // 2, c % 2
                    rows = slice(par * 64, (par + 1) * 64)
                    psu, psub = ubank[c % 2]
                    MM(psu[:, (c // 2) * 128:(c // 2 + 1) * 128], kptok[rows, i, :], vtok[rows, i, :], True, True,
                       [KPTOK, VTOK], [psub])
                for c in range(8):
                    gi = b * 8 + c
                    cur, nxt = gi % 2, (gi + 1) % 2
                    psu, psub = ubank[c % 2]
                    ACT(Sbf[:, c, :], Sst[:, h, cur, :], AF.Copy, [SST[h][cur]], [SBF[c]])
                    STT(Sst[:, h, nxt, :], Sst[:, h, cur, :], dec[:, c:c + 1], psu[:, (c // 2) * 128:(c // 2 + 1) * 128],
                        ALU.mult, ALU.add, [SST[h][cur], DEC, psub], [SST[h][nxt]])
                po, POB = pacc[h % 2], PACC[h % 2]
                for c in range(8):
                    i, par = c // 2, c % 2
                    rows = slice(par * 64, (par + 1) * 64)
                    csl = slice(c * 64, (c + 1) * 64)
                    MM(po[:, csl], Sbf[:, c, :], qp[:, csl], True, False, [SBF[c], QP], [POB])
                    MM(po[:, csl], vtok[rows, i, :], asb2[rows, i, :], False, True, [VTOK, ASB2[i]], [POB])
                ACT(sq[0][:], po[:], AF.Square, [POB], [SQ[0]])
                MM(pstat[:], cm(ONES_O), sq[0][:], True, True, [CBF, SQ[0]], [PSTAT])
                rmsstat_to_rstd()
                STT(ft[0][:], po[:], vec[:, V0 + 24:V0 + 25], rstd[:], ALU.mult, ALU.mult, [POB, VEC, RSTD], [FT[0]])
                ps, psb = proj_fm(l, f"hg{h}", hrhs, HT, KC)
                ACT(ft[1][:], ps[:], AF.Silu, [psb], [FT[1]])
                TT("pool", ogT[:, h, :], ft[0][:], ft[1][:], ALU.mult, [FT[0], FT[1]], [OGT[h]])
            for o in range(8):
                psa, psab = proj_fm(l, f"pa{o}", lambda k: attnT[:, k, :], ATT, 4)
                psb_, psbb = proj_fm(l, f"pb{o}", lambda k: ogT[:, k, :], OGT, 4)
                pga, pgab = proj_fm(l, f"ga{o}", hrhs, HT, KC)
                pgb, pgbb = proj_fm(l, f"gb{o}", hrhs, HT, KC)
                ACT(ft[0][:], pga[:], AF.Sigmoid, [pgab], [FT[0]])
                TT("dve", ft[1][:], psa[:], ft[0][:], ALU.mult, [psab, FT[0]], [FT[1]])
                ACT(ft[2][:], pgb[:], AF.Sigmoid, [pgbb], [FT[2]])
                TT("dve", ft[3][:], psb_[:], ft[2][:], ALU.mult, [psbb, FT[2]], [FT[3]])
                TT("pool", yT[:, o, :], ft[1][:], ft[3][:], ALU.add, [FT[1], FT[3]], [YT[o]])
            for o in range(8):
                ps, psb = proj_fm(l, f"wo{o}", lambda k: yT[:, k, :], YT, KC)
                TT("dve", xT[:, o, cols], ps[:], xT[:, o, cols], ALU.add, [psb, XT[o][b]], [XT[o][b]])

        def ffn_block(l, b):
            cols = slice(b * TB, (b + 1) * TB)
            norm_x(l, b, 8)
            hrhs = lambda k: hT[:, k, :]
            for f in range(NF):
                pg, pgb = proj_fm(l, f"gate{f}", hrhs, HT, KC)
                pu, pub = proj_fm(l, f"up{f}", hrhs, HT, KC)
                i = f % 2
                ACT(ft[i][:], pg[:], AF.Silu, [pgb], [FT[i]])
                TT("dve", actT[:, f, :], pu[:], ft[i][:], ALU.mult, [pub, FT[i]], [ACTT[f]])
            for o in range(8):
                wa, wab = WST.get(l, f"dna{o}")
                ps, psb = ring()
                for k in range(11):
                    MM(ps[:], wa[:, k, :], actT[:, k, :], k == 0, False, [wab, ACTT[k]], [psb])
                wb_, wbb = WST.get(l, f"dnb{o}")
                for k in range(11):
                    MM(ps[:], wb_[:, k, :], actT[:, 11 + k, :], False, k == 10, [wbb, ACTT[11 + k]], [psb])
                TT("dve", xT[:, o, cols], ps[:], xT[:, o, cols], ALU.add, [psb, XT[o][b]], [XT[o][b]])

        OUTB = Buf("out")
        for s in range(NSEQ):
            for b in range(NBLK):
                cols = slice(b * TB, (b + 1) * TB)
                def fn(s=s, cols=cols):
                    return nc.sync.dma_start(out=xT[:, :, cols], in_=xT_d[s, :, cols].rearrange("(c p) t -> p c t", p=128))
                S.dma("sp", fn, writes=[XT[c][b] for c in range(KC)], key=f"x_{b}")
            for l in range(L):
                for h in range(4):
                    MSET(Sst[:, h, 0, :], 0.0, [SST[h][0]])
                for b in range(NBLK):
                    mixer_block(l, b)
                    ffn_block(l, b)
                    if l == L - 1:
                        cols = slice(b * TB, (b + 1) * TB)
                        def fn(s=s, cols=cols):
                            return nc.sync.dma_start(out=out_d[s, :, cols].rearrange("(c p) t -> p c t", p=128), in_=xT[:, :, cols])
                        S.dma("sp", fn, reads=[XT[c][b] for c in range(KC)], writes=[OUTB], key=f"x_{b}")
        S.final_wait("sp", [OUTB] + [XT[c][b] for c in range(KC) for b in range(NBLK)])
        S.emit()
    return nc


def _prep_shared(inputs, L, T):
    wpack = np.stack([pack_layer_weights(inputs, l) for l in range(L)], axis=0)
    vecs = np.concatenate([pack_vecs(inputs, l) for l in range(L)], axis=1)
    cs, misc = const_tables(T)
    return wpack, np.ascontiguousarray(vecs), cs, misc


def make_wmap(wpack):
    L, _, WTOT = wpack.shape
    NWS = 1
    WPC = WTOT // NWS
    return {f"wpack{l}_{i}": np.ascontiguousarray(wpack[l, :, i * WPC:(i + 1) * WPC]) for l in range(L) for i in range(NWS)}


def kernel(**inputs):
    inputs = {k: np.asarray(v) for k, v in inputs.items()}
    x = inputs["x"]
    B, T, _ = x.shape
    L = inputs["w_in"].shape[0]
    nseq = B // NCORES
    wpack, vecs, cs, misc = _prep_shared(inputs, L, T)
    wmap = make_wmap(wpack)
    nc = build_program(nseq, T, L)
    in_maps = []
    for c in range(NCORES):
        xs = np.ascontiguousarray(x[c * nseq:(c + 1) * nseq].transpose(0, 2, 1))
        in_maps.append(dict(wmap, xT=xs, vecs=vecs, cs=cs, misc=misc))
    res = run_bass_kernel_spmd(nc, in_maps, core_ids=list(range(NCORES)))
    outs = [np.asarray(r["outT"]).transpose(0, 2, 1) for r in res.results]
    return np.ascontiguousarray(np.concatenate(outs, axis=0)).astype(np.float32)
```

```python
import math
from contextlib import ExitStack

import numpy as np
import concourse.bass as bass
import concourse.mybir as mybir
from concourse.bass_utils import run_bass_kernel_spmd

F32 = mybir.dt.float32
BF16 = mybir.dt.bfloat16
ALU = mybir.AluOpType
AF = mybir.ActivationFunctionType

D = 1024
KC = 8
DFF = 2816
NF = 22
TB = 512
EPS = 1e-6
NCORES = 8
SLOT = 2048
NSLOT = 3
ENGS = ("pe", "act", "dve", "pool", "sp")


class Buf:
    __slots__ = ("name", "w", "r")

    def __init__(self, name):
        self.name = name
        self.w = None
        self.r = {}


class Sched:
    def __init__(self, nc, stack):
        self.nc = nc
        self.stack = stack
        self.ops = {e: [] for e in ENGS}
        self.n = {e: 0 for e in ENGS}
        self.seen = {e: {} for e in ENGS}
        self.prog = {e: stack.enter_context(nc.semaphore("prog_" + e)) for e in ENGS}
        self.dma_sems = {}
        self.dma_cnt = {}

    def dma_sem(self, key):
        if key not in self.dma_sems:
            self.dma_sems[key] = self.stack.enter_context(self.nc.semaphore("d_" + key))
            self.dma_cnt[key] = 0
        return self.dma_sems[key]

    def _need(self, E, dep, waits):
        if dep[0] == "eng":
            _, E2, idx = dep
            if E2 == E and E == "pe":
                return
            key = ("eng", E2)
            if self.seen[E].get(key, 0) >= idx:
                return
            self.seen[E][key] = idx
            waits.append((self.prog[E2], idx))
        else:
            _, skey, cnt = dep
            key = ("dma", skey)
            if self.seen[E].get(key, 0) >= cnt:
                return
            self.seen[E][key] = cnt
            waits.append((self.dma_sems[skey], cnt))

    def _deps(self, E, reads, writes):
        waits = []
        for b in reads:
            if b.w is not None:
                self._need(E, b.w, waits)
        for b in writes:
            if b.w is not None:
                self._need(E, b.w, waits)
            for k, v in b.r.items():
                self._need(E, (k[0], k[1], v), waits)
        return waits

    def op(self, E, fn, reads=(), writes=()):
        waits = self._deps(E, reads, writes)
        self.n[E] += 1
        idx = self.n[E]
        self.ops[E].append((waits, fn, (self.prog[E], 1)))
        me = ("eng", E, idx)
        for b in reads:
            b.r[("eng", E)] = idx
        for b in writes:
            b.w = me
            b.r = {}
        return idx

    def dma(self, Q, fn, reads=(), writes=(), key=None):
        waits = self._deps(Q, reads, writes)
        sem = self.dma_sem(key)
        self.dma_cnt[key] += 16
        cnt = self.dma_cnt[key]
        self.ops[Q].append((waits, fn, (sem, 16)))
        me = ("dma", key, cnt)
        for b in reads:
            b.r[("dma", key)] = cnt
        for b in writes:
            b.w = me
            b.r = {}
        return cnt

    def final_wait(self, E, bufs):
        waits = []
        for b in bufs:
            if b.w is not None:
                self._need(E, b.w, waits)
            for k, v in b.r.items():
                self._need(E, (k[0], k[1], v), waits)
        self.ops[E].append((waits, None, None))

    def emit(self):
        nc = self.nc
        with nc.Block() as block:
            def run(E, eng):
                for waits, fn, inc in self.ops[E]:
                    for sem, val in waits:
                        eng.wait_ge(sem, val)
                    if fn is not None:
                        fn().then_inc(inc[0], inc[1])

            @block.tensor
            def _(e):
                run("pe", e)

            @block.scalar
            def _(e):
                run("act", e)

            @block.vector
            def _(e):
                run("dve", e)

            @block.gpsimd
            def _(e):
                run("pool", e)

            @block.sync
            def _(e):
                run("sp", e)


def weight_items():
    it = []
    for j in range(3):
        it.append((f"cq{j}", 8, 128))
    for j in range(2):
        it.append((f"ckv{j}", 8, 128))
    it.append(("kpe", 8, 64))
    for j in range(4):
        it.append((f"uk{j}", 2, 128))
    it.append(("uv", 2, 512))
    for h in range(8):
        it.append((f"uq{h}", 3, 128))
    for h in range(4):
        it.append((f"hf{h}", 8, 128))
        it.append((f"hq{h}", 8, 128))
        it.append((f"hi{h}", 8, 128))
        it.append((f"hg{h}", 8, 128))
    for o in range(8):
        it.append((f"pa{o}", 4, 128))
        it.append((f"pb{o}", 4, 128))
        it.append((f"ga{o}", 8, 128))
        it.append((f"gb{o}", 8, 128))
    for o in range(8):
        it.append((f"wo{o}", 8, 128))
    for f in range(NF):
        it.append((f"gate{f}", 8, 128))
        it.append((f"up{f}", 8, 128))
    for o in range(8):
        it.append((f"dna{o}", 11, 128))
        it.append((f"dnb{o}", 11, 128))
    return it


def weight_groups():
    groups = []
    index = {}
    cur_off = 0
    cur_size = 0
    start = 0
    for name, kc, m in weight_items():
        n = kc * m
        if cur_size + n > SLOT:
            groups.append((start, cur_size))
            start += cur_size
            cur_size = 0
        index[name] = (len(groups), cur_size, kc, m)
        cur_size += n
    groups.append((start, cur_size))
    total = start + cur_size
    total = ((total + 8191) // 8192) * 8192
    return groups, index, total


IN_OFF = {}
_o = 0
for _n, _s in (("cq", 384), ("ckv", 256), ("kpe", 32), ("hq", 512), ("hf", 512), ("hi", 512), ("hg", 512),
               ("ga", 1024), ("gb", 1024)):
    IN_OFF[_n] = _o
    _o += _s

NV = 40


def pack_layer_weights(inp, l):
    w_in = inp["w_in"][l]
    cols = {}

    def rng(a, n):
        return list(range(a, a + n))
    for j in range(3):
        cols[f"cq{j}"] = (w_in, rng(IN_OFF["cq"] + j * 128, 128))
    for j in range(2):
        cols[f"ckv{j}"] = (w_in, rng(IN_OFF["ckv"] + j * 128, 128))
    kp = IN_OFF["kpe"]
    cols["kpe"] = (w_in, rng(kp, 32) + rng(kp + 16, 16) + rng(kp, 16))
    ukv = inp["mla_w_ukv"][l]
    for j in range(4):
        cols[f"uk{j}"] = (ukv, rng((2 * j) * 128, 64) + rng((2 * j + 1) * 128, 64))
    vc = []
    for h in range(8):
        vc += rng(h * 128 + 64, 64)
    cols["uv"] = (ukv, vc)
    uq = inp["mla_w_uq"][l]
    for h in range(8):
        cols[f"uq{h}"] = (uq, rng(h * 96, 96) + rng(h * 96 + 80, 16) + rng(h * 96 + 64, 16))
    for h in range(4):
        for nm in ("hf", "hq", "hi", "hg"):
            cols[f"{nm}{h}"] = (w_in, rng(IN_OFF[nm] + h * 128, 128))
    for o in range(8):
        cols[f"pa{o}"] = (inp["w_proj_a"][l], rng(o * 128, 128))
        cols[f"pb{o}"] = (inp["w_proj_b"][l], rng(o * 128, 128))
        cols[f"ga{o}"] = (w_in, rng(IN_OFF["ga"] + o * 128, 128))
        cols[f"gb{o}"] = (w_in, rng(IN_OFF["gb"] + o * 128, 128))
        cols[f"wo{o}"] = (inp["w_out"][l], rng(o * 128, 128))
    for f in range(NF):
        cols[f"gate{f}"] = (inp["w_gate"][l], rng(f * 128, 128))
        cols[f"up{f}"] = (inp["w_up"][l], rng(f * 128, 128))
    wd = inp["w_down"][l]
    for o in range(8):
        cols[f"dna{o}"] = (wd[0:11 * 128], rng(o * 128, 128))
        cols[f"dnb{o}"] = (wd[11 * 128:22 * 128], rng(o * 128, 128))
    groups, index, total = weight_groups()
    out = np.zeros((128, total), np.float32)
    for name, kc, m in weight_items():
        W, cl = cols[name]
        g, off, _, _ = index[name]
        base = groups[g][0] + off
        blk = W[:kc * 128][:, cl].reshape(kc, 128, m).transpose(1, 0, 2).reshape(128, kc * m)
        out[:, base:base + kc * m] = blk
    return out


def pack_vecs(inp, l):
    v = np.zeros((128, NV), np.float32)
    v[:, 0:8] = inp["norm_mix"][l].reshape(8, 128).T
    v[:, 8:16] = inp["norm_ffn"][l].reshape(8, 128).T
    v[:, 16:19] = inp["mla_norm_cq"][l].reshape(3, 128).T
    v[:, 19:21] = inp["mla_norm_ckv"][l].reshape(2, 128).T
    qn = inp["mla_q_norm"][l]
    kn = inp["mla_k_norm"][l]
    v[0:96, 21] = qn
    v[96:112, 21] = qn[80:96]
    v[112:128, 21] = qn[64:80]
    v[0:64, 22] = kn[0:64]
    v[64:128, 22] = kn[0:64]
    v[64:96, 23] = kn[64:96]
    v[96:112, 23] = kn[80:96]
    v[112:128, 23] = kn[64:80]
    v[:, 24] = inp["hg_out_norm"][l]
    for ll in range(inp["hg_lb_logits"].shape[0]):
        v[:, 25 + 4 * ll:29 + 4 * ll] = inp["hg_lb_logits"][ll].reshape(4, 128).T
    return v


def const_tables(T):
    pos = np.arange(T, dtype=np.float32)
    inv_freq = (1.0 / (np.float32(10000.0) ** (np.arange(0, 32, 2, dtype=np.float32) / np.float32(32)))).astype(np.float32)
    ang = (pos[:, None] * inv_freq[None, :]).astype(np.float32)
    cos = np.cos(ang).astype(np.float32).T
    sin = np.sin(ang).astype(np.float32).T
    cs = np.ones((128, T), np.float32)
    cs[64:80] = cos
    cs[80:96] = cos
    cs[96:112] = -sin
    cs[112:128] = sin
    k = np.arange(128)
    tri_att = (k[None, :] >= k[:, None]).astype(np.float32)
    ident = np.eye(128, dtype=np.float32)
    s64 = k % 64
    t64 = np.arange(64)
    tri_h = (t64[None, :] >= s64[:, None]).astype(np.float32)
    tri_h4 = np.tile(tri_h, (1, 4))
    scanmask = np.ones((128, TB), np.float32)
    scanmask[:, ::64] = 0.0
    misc = np.concatenate([tri_att, ident, tri_h4, scanmask], axis=1).astype(np.float32)
    return cs, misc


def build_program(NSEQ, T, L, dbg=False):
    NBLK = T // TB
    NT = T // 128
    groups, windex, WTOT = weight_groups()
    NG = len(groups)
    nc = bass.Bass("TRN2", target_bir_lowering=False)
    xT_d = nc.dram_tensor("xT", [NSEQ, D, T], F32, kind="ExternalInput").ap()
    NWS = 1
    WPC = WTOT // NWS
    wp_d = [[nc.dram_tensor(f"wpack{l}_{i}", [128, WPC], F32, kind="ExternalInput").ap() for i in range(NWS)]
            for l in range(L)]
    vec_d = nc.dram_tensor("vecs", [128, L * NV], F32, kind="ExternalInput").ap()
    cs_d = nc.dram_tensor("cs", [128, T], F32, kind="ExternalInput").ap()
    misc_d = nc.dram_tensor("misc", [128, 512 + TB], F32, kind="ExternalInput").ap()
    out_d = nc.dram_tensor("outT", [NSEQ, D, T], F32, kind="ExternalOutput").ap()
    import os as _os
    if _os.environ.get("KDUMMY"):
        nc.dram_tensor("dummyin", [128, WTOT], F32, kind="ExternalInput")
    wbf_v = [wp_d[l][0].bitcast(BF16) for l in range(L)]

    with ExitStack() as st:
        S = Sched(nc, st)

        def sb(name, shape, dt):
            return st.enter_context(nc.sbuf_tensor(name, shape, dt))

        xT = sb("xT_sb", [128, KC, T], F32)
        XT = [[Buf(f"xT{c}_{b}") for b in range(NBLK)] for c in range(KC)]
        cs = sb("cs_sb", [128, T], F32); CS = Buf("cs")
        KT = sb("KT", [128, 8, T], BF16)
        KTB = [Buf(f"KT{h}") for h in range(8)]
        Vx = sb("Vx", [128, NT, 4, 192], BF16)
        VXB = [Buf(f"Vx{i}") for i in range(NT)]
        VXONES = Buf("vxones")
        skall = sb("skall", [128, NT * 8], F32)
        SK = [Buf(f"sk{b}") for b in range(NBLK)]
        wslot = [sb(f"wslot{i}", [128, SLOT], BF16) for i in range(NSLOT)]
        WS = [Buf(f"wslot{i}") for i in range(NSLOT)]
        vec = sb("vec_sb", [128, L * NV], F32); VEC = Buf("vec")
        cbf = sb("cbf", [128, 128 * 8], BF16); CBF = Buf("cbf")
        cvec = sb("cvec", [128, 8], F32); CVEC = Buf("cvec")
        lbt = sb("lbt", [128, L * 4], F32); LBT = Buf("lbt")
        lbtmp = sb("lbtmp", [128, 40], F32); LBTMP = Buf("lbtmp")
        gqs = sb("gqs", [128, L], F32); GQS = Buf("gqs")
        Sst = sb("Sst", [128, 4, 2, 128], F32)
        SST = [[Buf(f"S{h}_{i}") for i in range(2)] for h in range(4)]
        Sbf = sb("Sbf", [128, 8, 128], BF16)
        SBF = [Buf(f"Sbf{i}") for i in range(8)]
        hT = sb("hT", [128, KC, TB], BF16)
        HT = [Buf(f"hT{c}") for c in range(KC)]
        NSL = 22
        bfs = sb("bfs", [128, NSL, TB], BF16)
        BFS = [Buf(f"bfs{i}") for i in range(NSL)]
        yT, YT = bfs, BFS
        sqk = bfs[:, 8:13, :]; SQK = BFS[8:13]
        cqn = bfs[:, 0:3, :]; CQN = BFS[0:3]
        ckvn = bfs[:, 3:5, :]; CKVN = BFS[3:5]
        qTh = [bfs[:, 5 + i, :] for i in range(2)]; QTH = BFS[5:7]
        pT = [bfs[:, 7 + i, :] for i in range(3)]; PT = BFS[7:10]
        qp = bfs[:, 10, :]; QP = BFS[10]
        kt_ = bfs[:, 11, :]; KTL = BFS[11]
        kpT = bfs[:, 12, :]; KPT = BFS[12]
        attnT = bfs[:, 13:17, :]; ATT = BFS[13:17]
        ogT = bfs[:, 17:21, :]; OGT = BFS[17:21]
        actT, ACTT = bfs, BFS
        sq = [sb(f"sq{i}", [128, TB], BF16) for i in range(2)]
        SQ = [Buf(f"sq{i}") for i in range(2)]
        rstd = sb("rstd", [128, TB], F32); RSTD = Buf("rstd")
        ft = [sb(f"ft{i}", [128, TB], F32) for i in range(6)]
        FT = [Buf(f"ft{i}") for i in range(6)]
        cqf = [ft[0], ft[1], ft[2]]; CQF = FT[0:3]
        rk, RK = ft[4], FT[4]
        rk2, RK2 = ft[5], FT[5]
        kptok = sb("kptok", [128, 4, 128], BF16); KPTOK = Buf("kptok")
        vtok = sb("vtok", [128, 4, 128], BF16); VTOK = Buf("vtok")
        asb2 = sb("asb2", [128, 4, 64], BF16); ASB2 = [Buf(f"asb2_{i}") for i in range(4)]
        dec = sb("dec", [128, 8], F32); DEC = Buf("dec")
        kst_sb = sb("kst_sb", [128, 32], F32); KSTSB = Buf("kstsb")
        scanm = sb("scanm", [128, TB], F32); SCANM = Buf("scanm")
        trih = sb("trih", [128, 256], BF16); TRIH = Buf("trih")

        NRING = 4
        pring = [st.enter_context(nc.psum_tensor(f"pr{i}", [128, 512], F32)) for i in range(NRING)]
        PR = [Buf(f"pr{i}") for i in range(NRING)]
        pstat = st.enter_context(nc.psum_tensor("pstat", [128, 512], F32)); PSTAT = Buf("pstat")
        pacc = [st.enter_context(nc.psum_tensor(f"pacc{i}", [128, 512], F32)) for i in range(2)]
        PACC = [Buf(f"pacc{i}") for i in range(2)]
        ptr = st.enter_context(nc.psum_tensor("ptr", [128, 1024], BF16)); PTR = Buf("ptr")
        ring_i = [0]

        def ring():
            i = ring_i[0] % NRING
            ring_i[0] += 1
            return pring[i], PR[i]

        def ACT(out, in_, func, reads, writes, scale=None, bias=None):
            kw = {}
            if scale is not None:
                kw["scale"] = scale
            if bias is not None:
                kw["bias"] = bias
            S.op("act", lambda: nc.scalar.activation(out=out, in_=in_, func=func, **kw), reads, writes)

        def TT(eng, out, in0, in1, op, reads, writes):
            e = nc.vector if eng == "dve" else nc.gpsimd
            S.op(eng, lambda: e.tensor_tensor(out=out, in0=in0, in1=in1, op=op), reads, writes)

        def STT(out, in0, scalar, in1, op0, op1, reads, writes):
            S.op("dve", lambda: nc.vector.scalar_tensor_tensor(out=out, in0=in0, scalar=scalar, in1=in1,
                                                               op0=op0, op1=op1), reads, writes)

        def TS(eng, out, in0, s1, s2, op0, op1, reads, writes):
            e = nc.vector if eng == "dve" else nc.gpsimd
            if s2 is None:
                S.op(eng, lambda: e.tensor_scalar(out=out, in0=in0, scalar1=s1, scalar2=None, op0=op0), reads, writes)
            else:
                S.op(eng, lambda: e.tensor_scalar(out=out, in0=in0, scalar1=s1, scalar2=s2, op0=op0, op1=op1),
                     reads, writes)

        def CP(eng, out, in_, reads, writes):
            e = nc.vector if eng == "dve" else nc.gpsimd
            S.op(eng, lambda: e.tensor_copy(out=out, in_=in_), reads, writes)

        def MM(out, lhsT, rhs, start, stop, reads, writes):
            S.op("pe", lambda: nc.tensor.matmul(out, lhsT=lhsT, rhs=rhs, start=start, stop=stop), reads, writes)

        def RECIP(out, in_, reads, writes):
            S.op("dve", lambda: nc.vector.reciprocal(out=out, in_=in_), reads, writes)

        def SCAN(out, d0, d1, reads, writes):
            S.op("dve", lambda: nc.vector.tensor_tensor_scan(out=out, data0=d0, data1=d1, initial=0.0,
                                                             op0=ALU.mult, op1=ALU.add), reads, writes)

        def TRANS(out, in_, ident, reads, writes):
            S.op("pe", lambda: nc.tensor.transpose(out, in_, ident), reads, writes)

        def MSET(ap, val, writes):
            S.op("pool", lambda: nc.gpsimd.memset(ap, val), (), writes)

        S.dma("sp", lambda: nc.sync.dma_start(out=vec[:], in_=vec_d), writes=[VEC], key="vec")
        S.dma("sp", lambda: nc.sync.dma_start(out=cs[:], in_=cs_d), writes=[CS], key="cs")
        S.dma("sp", lambda: nc.sync.dma_start(out=ft[0][:], in_=misc_d[:, 0:512]), writes=[FT[0]], key="misc")
        S.dma("sp", lambda: nc.sync.dma_start(out=scanm[:], in_=misc_d[:, 512:512 + TB]), writes=[SCANM], key="scanm")
        ONES_D, ONES_CQ, ONES_CKV, ONES_Q, ONES_O, ONES_K, TRI, IDN = range(8)

        def cm(i):
            return cbf[:, i * 128:(i + 1) * 128]
        MSET(cm(ONES_D), 1.0 / 1024, [CBF])
        MSET(cm(ONES_CQ), 1.0 / 384, [CBF])
        MSET(cm(ONES_CKV), 1.0 / 256, [CBF])
        MSET(cm(ONES_Q), 1.0 / 96, [CBF])
        MSET(cbf[96:128, ONES_Q * 128:(ONES_Q + 1) * 128], 0.0, [CBF])
        MSET(cm(ONES_O), 1.0 / 128, [CBF])
        MSET(cm(ONES_K), 0.0, [CBF])
        MSET(cbf[64:96, ONES_K * 128:ONES_K * 128 + 8], 1.0 / 96, [CBF])
        MSET(cbf[0:64, ONES_K * 128 + 8:ONES_K * 128 + 9], 1.0 / 96, [CBF])
        MSET(cbf[64:128, ONES_K * 128 + 9:ONES_K * 128 + 10], 1.0 / 96, [CBF])
        for i_ in range(NSL):
            MSET(bfs[:, i_, :], 0.0, [BFS[i_]])
        CP("pool", cm(TRI), ft[0][:, 0:128], [FT[0]], [CBF])
        CP("pool", cm(IDN), ft[0][:, 128:256], [FT[0]], [CBF])
        CP("pool", trih[:], ft[0][:, 256:512], [FT[0]], [TRIH])
        MSET(cvec[:, 0:1], EPS, [CVEC])
        MSET(cvec[:, 1:2], 1.0, [CVEC])
        MSET(cvec[:, 2:3], 0.0, [CVEC])
        C_EPS, C_ONE, C_ZERO = cvec[:, 0:1], cvec[:, 1:2], cvec[:, 2:3]
        MSET(Vx[:, :, :, 64:128], 1.0, [VXONES])
        ll = [vec[:, l * NV + 25: l * NV + 25 + 4 * L] for l in range(1)][0]
        mx, e_all, ssum, rs, cum = lbtmp[:, 0:4], lbtmp[:, 4:4 + 4 * L], lbtmp[:, 16:20], lbtmp[:, 20:24], lbtmp[:, 24:28]
        CP("dve", mx, ll[:, 0:4], [VEC], [LBTMP])
        for l in range(1, L):
            TT("dve", mx, mx, ll[:, 4 * l:4 * l + 4], ALU.max, [VEC, LBTMP], [LBTMP])
        for l in range(L):
            TT("dve", e_all[:, 4 * l:4 * l + 4], ll[:, 4 * l:4 * l + 4], mx, ALU.subtract, [VEC, LBTMP], [LBTMP])
        ACT(e_all, e_all, AF.Exp, [LBTMP], [LBTMP])
        CP("dve", ssum, e_all[:, 0:4], [LBTMP], [LBTMP])
        for l in range(1, L):
            TT("dve", ssum, ssum, e_all[:, 4 * l:4 * l + 4], ALU.add, [LBTMP], [LBTMP])
        RECIP(rs, ssum, [LBTMP], [LBTMP])
        for l in range(L):
            TT("dve", e_all[:, 4 * l:4 * l + 4], e_all[:, 4 * l:4 * l + 4], rs, ALU.mult, [LBTMP], [LBTMP])
        CP("dve", cum, e_all[:, 0:4], [LBTMP], [LBTMP])
        for l in range(L):
            if l > 0:
                TT("dve", cum, cum, e_all[:, 4 * l:4 * l + 4], ALU.add, [LBTMP], [LBTMP])
            TT("dve", lbt[:, 4 * l:4 * l + 4], cum, e_all[:, 0:4], ALU.subtract, [LBTMP], [LBT])
            TS("dve", lbt[:, 4 * l:4 * l + 4], lbt[:, 4 * l:4 * l + 4], 0.0, None, ALU.max, ALU.bypass, [LBT], [LBT])
        for l in range(L):
            TS("dve", gqs[:, l:l + 1], vec[:, l * NV + 21:l * NV + 22], float(96.0 ** -0.5), None, ALU.mult, ALU.bypass,
               [VEC], [GQS])

        RPC = 4
        PCS = RPC * T
        while WTOT % PCS:
            RPC //= 2
            PCS = RPC * T
        NCHK = WTOT // PCS
        WCH = []
        jj = 0
        for l in range(L):
            row = []
            DCH = [Buf(f"dch{l}_{k}") for k in range(NCHK)]
            for k in range(NCHK):
                c0, c1 = k * PCS, (k + 1) * PCS
                hb = jj % (8 // RPC)
                r0 = hb * RPC
                jj += 1
                bq = Buf(f"wch{l}_{c0}")
                xbufs = [XT[c][bb_] for c in range(r0, r0 + RPC) for bb_ in range(NBLK)]
                kbufs = KTB[r0:r0 + RPC]
                xs_ = xT[:, r0:r0 + RPC, :]
                ks_ = KT[:, r0:r0 + RPC, :]

                def ld(l=l, c0=c0, c1=c1, xs_=xs_):
                    return nc.sync.dma_start(out=xs_, in_=wp_d[l][0][:, c0:c1].rearrange("p (r t) -> p r t", t=T))
                S.dma("sp", ld, reads=[DCH[k]], writes=xbufs, key=f"wld{hb}")
                for r in range(RPC):
                    eng = ("act", "dve", "pool")[(jj + r) % 3]
                    c = r0 + r
                    if eng == "act":
                        ACT(KT[:, c, :], xT[:, c, :], AF.Copy, XT[c], [KTB[c]])
                    else:
                        CP(eng, KT[:, c, :], xT[:, c, :], XT[c], [KTB[c]])

                def stf(l=l, c0=c0, c1=c1, ks_=ks_):
                    return nc.sync.dma_start(out=wbf_v[l][:, c0:c1].rearrange("p (r t) -> p r t", t=T), in_=ks_)
                S.dma("sp", stf, reads=kbufs, writes=[bq, DCH[k // 2]], key=f"wst{hb}")
                row.append((c0, c1, bq))
            WCH.append(row)
        MSET(KT[96:128, :, :], 0.0, KTB)

        class WStream:
            def __init__(self):
                self.seq = []
                self.pos = 0
                self.cur = -1
                self.curkey = None

            def plan(self, order):
                self.seq = order

            def _load(self, i):
                l, g = self.seq[i]
                off, size = groups[g]
                slot = i % NSLOT
                rd = [b for (c0, c1, b) in WCH[l] if c0 < off + size and c1 > off]

                def fn(l=l, off=off, size=size, slot=slot):
                    return nc.sync.dma_start(out=wslot[slot][:, 0:size], in_=wbf_v[l][:, off:off + size])
                S.dma("sp", fn, reads=rd, writes=[WS[slot]], key=f"ws{slot}")

            def start(self):
                for i in range(min(NSLOT, len(self.seq))):
                    self._load(i)
                self.pos = min(NSLOT, len(self.seq))
                self.cur = 0

            def get(self, l, name):
                g, off, kc, m = windex[name]
                while self.seq[self.cur] != (l, g):
                    self.cur += 1
                    if self.pos < len(self.seq) and self.pos - self.cur < NSLOT - 1 + 1:
                        pass
                    while self.pos < len(self.seq) and self.pos < self.cur + NSLOT:
                        self._load(self.pos)
                        self.pos += 1
                slot = self.cur % NSLOT
                ap = wslot[slot][:, off:off + kc * m].rearrange("p (k m) -> p k m", m=m)
                return ap, WS[slot]

        WST = WStream()
        order = []
        for s in range(NSEQ):
            for l in range(L):
                for b in range(NBLK):
                    for g in range(NG):
                        order.append((l, g))
        WST.plan(order)
        WST.start()

        def rmsstat_to_rstd(reads_extra=()):
            ACT(rstd[:], pstat[:], AF.Ln, [PSTAT, CVEC], [RSTD], bias=C_EPS)
            ACT(rstd[:], rstd[:], AF.Exp, [RSTD], [RSTD], scale=-0.5)

        def norm_x(l, b, gcol):
            cols = slice(b * TB, (b + 1) * TB)
            for c in range(KC):
                i = c % 2
                ACT(sq[i][:], xT[:, c, cols], AF.Square, [XT[c][b]], [SQ[i]])
                MM(pstat[:], cm(ONES_D), sq[i][:], c == 0, c == KC - 1, [CBF, SQ[i]], [PSTAT])
            rmsstat_to_rstd()
            for c in range(KC):
                STT(hT[:, c, :], xT[:, c, cols], vec[:, l * NV + gcol + c:l * NV + gcol + c + 1], rstd[:],
                    ALU.mult, ALU.mult, [XT[c][b], VEC, RSTD], [HT[c]])

        def proj_fm(l, name, rhs_ap, rhs_bufs, nk, M=128, out_rows=None):
            w, wb = WST.get(l, name)
            ps, psb = ring()
            o = ps[:] if out_rows is None else ps[out_rows[0]:out_rows[1], :]
            for k in range(nk):
                MM(o, w[:, k, :], rhs_ap(k), k == 0, k == nk - 1, [wb] + list(rhs_bufs), [psb])
            return ps, psb

        def mixer_block(l, b):
            cols = slice(b * TB, (b + 1) * TB)
            V0 = l * NV
            norm_x(l, b, 0)
            hrhs = lambda k: hT[:, k, :]
            for j in range(3):
                ps, psb = proj_fm(l, f"cq{j}", hrhs, HT, KC)
                i = j % 2
                ACT(sq[i][:], ps[:], AF.Square, [psb], [SQ[i]])
                ACT(cqf[j][:], ps[:], AF.Copy, [psb], [CQF[j]])
                MM(pstat[:], cm(ONES_CQ), sq[i][:], j == 0, j == 2, [CBF, SQ[i]], [PSTAT])
            rmsstat_to_rstd()
            for j in range(3):
                STT(cqn[:, j, :], cqf[j][:], vec[:, V0 + 16 + j:V0 + 17 + j], rstd[:], ALU.mult, ALU.mult,
                    [CQF[j], VEC, RSTD], [CQN[j]])
            for j in range(2):
                ps, psb = proj_fm(l, f"ckv{j}", hrhs, HT, KC)
                i = j % 2
                ACT(sq[i][:], ps[:], AF.Square, [psb], [SQ[i]])
                ACT(cqf[j][:], ps[:], AF.Copy, [psb], [CQF[j]])
                MM(pstat[:], cm(ONES_CKV), sq[i][:], j == 0, j == 1, [CBF, SQ[i]], [PSTAT])
            rmsstat_to_rstd()
            for j in range(2):
                STT(ckvn[:, j, :], cqf[j][:], vec[:, V0 + 19 + j:V0 + 20 + j], rstd[:], ALU.mult, ALU.mult,
                    [CQF[j], VEC, RSTD], [CKVN[j]])
            ps, psb = proj_fm(l, "kpe", hrhs, HT, KC, M=64, out_rows=(64, 128))
            ACT(sqk[64:128, 4, :], ps[64:128, :], AF.Square, [psb], [SQK[4]])
            STT(rk[64:128, :], ps[64:128, :], vec[64:128, V0 + 23:V0 + 24], cs[64:128, cols], ALU.mult, ALU.mult,
                [psb, VEC, CS], [RK])
            ACT(rk2[64:96, :], rk[96:128, :], AF.Copy, [RK], [RK2])
            TT("pool", rk[64:96, :], rk[64:96, :], rk2[64:96, :], ALU.add, [RK, RK2], [RK])
            for h in range(8):
                CP("pool", KT[64:96, h, cols], rk[64:96, :], [RK], [KTB[h]])
            ckrhs = lambda k: ckvn[:, k, :]
            for j in range(4):
                ps, psb = proj_fm(l, f"uk{j}", ckrhs, CKVN, 2)
                ACT(sqk[:, j, :], ps[:], AF.Square, [psb], [SQK[j]])
                ACT(KT[0:64, 2 * j, cols], ps[0:64, :], AF.Copy, [psb, VEC], [KTB[2 * j]], scale=vec[0:64, V0 + 22:V0 + 23])
                ACT(KT[0:64, 2 * j + 1, cols], ps[64:128, :], AF.Copy, [psb, VEC], [KTB[2 * j + 1]],
                    scale=vec[64:128, V0 + 22:V0 + 23])
            for i in range(4):
                tsl = slice(i * 128, (i + 1) * 128)
                MM(pstat[:, i * 8:(i + 1) * 8], sqk[:, 4, tsl], cbf[:, ONES_K * 128:ONES_K * 128 + 8], True, False,
                   [SQK[4], CBF], [PSTAT])
                for h in range(8):
                    sel = ONES_K * 128 + 8 + (h % 2)
                    MM(pstat[:, i * 8 + h:i * 8 + h + 1], sqk[:, h // 2, tsl],
                       cbf[:, sel:sel + 1], False, h == 7, [SQK[h // 2], CBF], [PSTAT])
            ACT(kst_sb[:, :], pstat[:, 0:32], AF.Ln, [PSTAT, CVEC], [KSTSB], bias=C_EPS)
            ACT(skall[:, b * 32:(b + 1) * 32], kst_sb[:, :], AF.Exp, [KSTSB], [SK[b]], scale=-0.5)
            wv, wvb = WST.get(l, "uv")
            for i in range(4):
                tsl = slice(i * 128, (i + 1) * 128)
                ps, psb = ring()
                for k in range(2):
                    MM(ps[:], ckvn[:, k, tsl], wv[:, k, :], k == 0, k == 1, CKVN + [wvb], [psb])
                pv = ps[:].rearrange("p (j e d) -> p j e d", e=2, d=64)
                ti = b * 4 + i
                CP("dve", Vx[:, ti, :, 0:64], pv[:, :, 0, :], [psb], [VXB[ti]])
                ACT(Vx[:, ti, :, 128:192], pv[:, :, 1, :], AF.Copy, [psb], [VXB[ti]])
            cqrhs = lambda k: cqn[:, k, :]
            def q_prep(h):
                ps, psb = proj_fm(l, f"uq{h}", cqrhs, CQN, 3)
                i = h % 2
                ACT(sq[i][:], ps[:], AF.Square, [psb], [SQ[i]])
                MM(pstat[:], cm(ONES_Q), sq[i][:], True, True, [CBF, SQ[i]], [PSTAT])
                rmsstat_to_rstd()
                STT(ft[0][:], ps[:], gqs[:, l:l + 1], rstd[:], ALU.mult, ALU.mult, [psb, GQS, RSTD], [FT[0]])
                TT("pool", ft[1][:], ft[0][:], cs[:, cols], ALU.mult, [FT[0], CS], [FT[1]])
                qt, QB = qTh[h % 2], QTH[h % 2]
                CP("pool", qt[0:64, :], ft[1][0:64, :], [FT[1]], [QB])
                ACT(ft[2][64:96, :], ft[1][96:128, :], AF.Copy, [FT[1]], [FT[2]])
                TT("pool", qt[64:96, :], ft[1][64:96, :], ft[2][64:96, :], ALU.add, [FT[1], FT[2]], [QB])
                CP("pool", qt[96:128, :], ft[1][96:128, :], [FT[1]], [QB])

            def attn(h):
                qt, QB = qTh[h % 2], QTH[h % 2]
                j, par = h // 2, h % 2
                po, POB = pacc[h % 2], PACC[h % 2]
                nkt = 4 * b + 4
                pend = None
                for kt in range(nkt):
                    r = kt - 4 * b
                    q0 = max(0, r) * 128
                    nq = TB - q0
                    ps2, ps2b = ring()
                    MM(ps2[:, 0:nq], KT[:, h, kt * 128:(kt + 1) * 128], qt[:, q0:TB], True, True,
                       [KTB[h], QB], [ps2b])
                    pt, PTB = pT[kt % 3], PT[kt % 3]
                    ACT(pt[:, 0:nq], ps2[:, 0:nq], AF.Exp, [ps2b, SK[kt // 4]], [PTB],
                        scale=skall[:, kt * 8 + h:kt * 8 + h + 1])
                    if r >= 0:
                        TT("pool", pt[:, 0:128], pt[:, 0:128], cm(TRI), ALU.mult, [PTB, CBF], [PTB])
                    if pend is not None:
                        MM(*pend)
                    pend = (po[:, q0:TB], Vx[:, kt, j, par * 64:par * 64 + 128], pt[:, 0:nq], kt == 0, kt == nkt - 1,
                            [VXB[kt], VXONES, PTB], [POB])
                MM(*pend)
                if par == 0:
                    RECIP(ft[3][64:128, :], po[64:128, :], [POB], [FT[3]])
                    TT("dve", attnT[0:64, j, :], po[0:64, :], ft[3][64:128, :], ALU.mult, [POB, FT[3]], [ATT[j]])
                else:
                    RECIP(ft[3][0:64, :], po[0:64, :], [POB], [FT[3]])
                    TT("dve", attnT[64:128, j, :], po[64:128, :], ft[3][0:64, :], ALU.mult, [POB, FT[3]], [ATT[j]])

            q_prep(0)
            for h in range(8):
                if h < 7:
                    q_prep(h + 1)
                attn(h)
            for h in range(4):
                lbc = lbt[:, 4 * l + h:4 * l + h + 1]
                ps, psb = proj_fm(l, f"hf{h}", hrhs, HT, KC)
                e, l1, l2, bb, kk, tmp = ft[0], ft[1], ft[2], ft[3], ft[4], ft[5]
                E, L1, L2, BB, KK, TMP = FT[0], FT[1], FT[2], FT[3], FT[4], FT[5]
                ACT(e[:], ps[:], AF.Exp, [psb], [E], scale=-1.0)
                ACT(l1[:], e[:], AF.Ln, [E, LBT, CVEC], [L1], scale=lbc, bias=C_ONE)
                ACT(l2[:], e[:], AF.Ln, [E, CVEC], [L2], bias=C_ONE)
                TT("pool", l1[:], l1[:], l2[:], ALU.subtract, [L1, L2], [L1])
                SCAN(bb[:], scanm[:], l1[:], [SCANM, L1], [BB])
                ACT(kk[:], l1[:], AF.Exp, [L1], [KK])
                TS("pool", kk[:], kk[:], -1.0, 1.0, ALU.mult, ALU.add, [KK], [KK])
                b3 = bb[:].rearrange("p (c j) -> p c j", j=64)
                ACT(dec[:, :], b3[:, :, 63], AF.Exp, [BB], [DEC])
                ACT(e[:], bb[:], AF.Exp, [BB], [E])
                ACT(l2[:], bb[:], AF.Exp, [BB], [L2], scale=-1.0)
                TT("pool", tmp[:].rearrange("p (c j) -> p c j", j=64), b3, b3[:, :, 63:64].to_broadcast([128, 8, 64]),
                   ALU.subtract, [BB], [TMP])
                ACT(tmp[:], tmp[:], AF.Exp, [TMP], [TMP], scale=-1.0)
                TT("pool", kt_[:], kk[:], l2[:], ALU.mult, [KK, L2], [KTL])
                TT("pool", kpT[:], kk[:], tmp[:], ALU.mult, [KK, TMP], [KPT])
                ps, psb = proj_fm(l, f"hq{h}", hrhs, HT, KC)
                TT("dve", qp[:], ps[:], e[:], ALU.mult, [psb, E], [QP])
                wv, wvb = WST.get(l, f"hi{h}")
                ps, psb = ring()
                for i in range(4):
                    for k in range(KC):
                        MM(ps[:, i * 128:(i + 1) * 128], hT[:, k, i * 128:(i + 1) * 128], wv[:, k, :], k == 0, k == KC - 1,
                           HT + [wvb], [psb])
                CP("dve", vtok[:].rearrange("p i d -> p (i d)"), ps[:], [psb], [VTOK])
                for i in range(4):
                    TRANS(ptr[:, i * 128:(i + 1) * 128], kpT[:, i * 128:(i + 1) * 128], cm(IDN), [KPT, CBF], [PTR])
                CP("dve", kptok[:].rearrange("p i d -> p (i d)"), ptr[:, 0:512], [PTR], [KPTOK])
                for i in range(4):
                    ps, psb = ring()
                    for par in range(2):
                        c = 2 * i + par
                        csl = slice(c * 64, (c + 1) * 64)
                        MM(ps[par * 64:(par + 1) * 64, 0:64], kt_[:, csl], qp[:, csl], True, True, [KTL, QP], [psb])
                    TT("dve", asb2[:, i, :], ps[:, 0:64], trih[:, 0:64], ALU.mult, [psb, TRIH], [ASB2[i]])
                ubank = [ring(), ring()]
                for c in range(8):
                    i, par = c // 2, c % 2
                    rows = slice(par * 64, (par + 1) * 64)
                    psu, psub = ubank[c % 2]
                    MM(psu[:, (c // 2) * 128:(c // 2 + 1) * 128], kptok[rows, i, :], vtok[rows, i, :], True, True,
                       [KPTOK, VTOK], [psub])
                for c in range(8):
                    gi = b * 8 + c
                    cur, nxt = gi % 2, (gi + 1) % 2
                    psu, psub = ubank[c % 2]
                    ACT(Sbf[:, c, :], Sst[:, h, cur, :], AF.Copy, [SST[h][cur]], [SBF[c]])
                    STT(Sst[:, h, nxt, :], Sst[:, h, cur, :], dec[:, c:c + 1], psu[:, (c // 2) * 128:(c // 2 + 1) * 128],
                        ALU.mult, ALU.add, [SST[h][cur], DEC, psub], [SST[h][nxt]])
                po, POB = pacc[h % 2], PACC[h % 2]
                for c in range(8):
                    i, par = c // 2, c % 2
                    rows = slice(par * 64, (par + 1) * 64)
                    csl = slice(c * 64, (c + 1) * 64)
                    MM(po[:, csl], Sbf[:, c, :], qp[:, csl], True, False, [SBF[c], QP], [POB])
                    MM(po[:, csl], vtok[rows, i, :], asb2[rows, i, :], False, True, [VTOK, ASB2[i]], [POB])
                ACT(sq[0][:], po[:], AF.Square, [POB], [SQ[0]])
                MM(pstat[:], cm(ONES_O), sq[0][:], True, True, [CBF, SQ[0]], [PSTAT])
                rmsstat_to_rstd()
                STT(ft[0][:], po[:], vec[:, V0 + 24:V0 + 25], rstd[:], ALU.mult, ALU.mult, [POB, VEC, RSTD], [FT[0]])
                ps, psb = proj_fm(l, f"hg{h}", hrhs, HT, KC)
                ACT(ft[1][:], ps[:], AF.Silu, [psb], [FT[1]])
                TT("pool", ogT[:, h, :], ft[0][:], ft[1][:], ALU.mult, [FT[0], FT[1]], [OGT[h]])
            for o in range(8):
                psa, psab = proj_fm(l, f"pa{o}", lambda k: attnT[:, k, :], ATT, 4)
                psb_, psbb = proj_fm(l, f"pb{o}", lambda k: ogT[:, k, :], OGT, 4)
                pga, pgab = proj_fm(l, f"ga{o}", hrhs, HT, KC)
                pgb, pgbb = proj_fm(l, f"gb{o}", hrhs, HT, KC)
                ACT(ft[0][:], pga[:], AF.Sigmoid, [pgab], [FT[0]])
                TT("dve", ft[1][:], psa[:], ft[0][:], ALU.mult, [psab, FT[0]], [FT[1]])
                ACT(ft[2][:], pgb[:], AF.Sigmoid, [pgbb], [FT[2]])
                TT("dve", ft[3][:], psb_[:], ft[2][:], ALU.mult, [psbb, FT[2]], [FT[3]])
                TT("pool", yT[:, o, :], ft[1][:], ft[3][:], ALU.add, [FT[1], FT[3]], [YT[o]])
            for o in range(8):
                ps, psb = proj_fm(l, f"wo{o}", lambda k: yT[:, k, :], YT, KC)
                TT("dve", xT[:, o, cols], ps[:], xT[:, o, cols], ALU.add, [psb, XT[o][b]], [XT[o][b]])

        def ffn_block(l, b):
            cols = slice(b * TB, (b + 1) * TB)
            norm_x(l, b, 8)
            hrhs = lambda k: hT[:, k, :]
            for f in range(NF):
                pg, pgb = proj_fm(l, f"gate{f}", hrhs, HT, KC)
                pu, pub = proj_fm(l, f"up{f}", hrhs, HT, KC)
                i = f % 2
                ACT(ft[i][:], pg[:], AF.Silu, [pgb], [FT[i]])
                TT("dve", actT[:, f, :], pu[:], ft[i][:], ALU.mult, [pub, FT[i]], [ACTT[f]])
            for o in range(8):
                wa, wab = WST.get(l, f"dna{o}")
                ps, psb = ring()
                for k in range(11):
                    MM(ps[:], wa[:, k, :], actT[:, k, :], k == 0, False, [wab, ACTT[k]], [psb])
                wb_, wbb = WST.get(l, f"dnb{o}")
                for k in range(11):
                    MM(ps[:], wb_[:, k, :], actT[:, 11 + k, :], False, k == 10, [wbb, ACTT[11 + k]], [psb])
                TT("dve", xT[:, o, cols], ps[:], xT[:, o, cols], ALU.add, [psb, XT[o][b]], [XT[o][b]])

        OUTB = Buf("out")
        for s in range(NSEQ):
            for b in range(NBLK):
                cols = slice(b * TB, (b + 1) * TB)
                def fn(s=s, cols=cols):
                    return nc.sync.dma_start(out=xT[:, :, cols], in_=xT_d[s, :, cols].rearrange("(c p) t -> p c t", p=128))
                S.dma("sp", fn, writes=[XT[c][b] for c in range(KC)], key=f"x_{b}")
            for l in range(L):
                for h in range(4):
                    MSET(Sst[:, h, 0, :], 0.0, [SST[h][0]])
                for b in range(NBLK):
                    mixer_block(l, b)
                    ffn_block(l, b)
                    if l == L - 1:
                        cols = slice(b * TB, (b + 1) * TB)
                        def fn(s=s, cols=cols):
                            return nc.sync.dma_start(out=out_d[s, :, cols].rearrange("(c p) t -> p c t", p=128), in_=xT[:, :, cols])
                        S.dma("sp", fn, reads=[XT[c][b] for c in range(KC)], writes=[OUTB], key=f"x_{b}")
        S.final_wait("sp", [OUTB] + [XT[c][b] for c in range(KC) for b in range(NBLK)])
        S.emit()
    return nc


def _prep_shared(inputs, L, T):
    wpack = np.stack([pack_layer_weights(inputs, l) for l in range(L)], axis=0)
    vecs = np.concatenate([pack_vecs(inputs, l) for l in range(L)], axis=1)
    cs, misc = const_tables(T)
    return wpack, np.ascontiguousarray(vecs), cs, misc


def make_wmap(wpack):
    L, _, WTOT = wpack.shape
    NWS = 1
    WPC = WTOT // NWS
    return {f"wpack{l}_{i}": np.ascontiguousarray(wpack[l, :, i * WPC:(i + 1) * WPC]) for l in range(L) for i in range(NWS)}


def kernel(**inputs):
    inputs = {k: np.asarray(v) for k, v in inputs.items()}
    x = inputs["x"]
    B, T, _ = x.shape
    L = inputs["w_in"].shape[0]
    nseq = B // NCORES
    wpack, vecs, cs, misc = _prep_shared(inputs, L, T)
    wmap = make_wmap(wpack)
    nc = build_program(nseq, T, L)
    in_maps = []
    for c in range(NCORES):
        xs = np.ascontiguousarray(x[c * nseq:(c + 1) * nseq].transpose(0, 2, 1))
        in_maps.append(dict(wmap, xT=xs, vecs=vecs, cs=cs, misc=misc))
    res = run_bass_kernel_spmd(nc, in_maps, core_ids=list(range(NCORES)))
    outs = [np.asarray(r["outT"]).transpose(0, 2, 1) for r in res.results]
    return np.ascontiguousarray(np.concatenate(outs, axis=0)).astype(np.float32)
```

```python
import math
from contextlib import ExitStack

import numpy as np
import concourse.bass as bass
import concourse.mybir as mybir
from concourse.bass_utils import run_bass_kernel_spmd

F32 = mybir.dt.float32
BF16 = mybir.dt.bfloat16
ALU = mybir.AluOpType
AF = mybir.ActivationFunctionType

D = 1024
KC = 8
DFF = 2816
NF = 22
TB = 512
EPS = 1e-6
NCORES = 8
SLOT = 2048
NSLOT = 3
ENGS = ("pe", "act", "dve", "pool", "sp")


class Buf:
    __slots__ = ("name", "w", "r")

    def __init__(self, name):
        self.name = name
        self.w = None
        self.r = {}


class Sched:
    def __init__(self, nc, stack):
        self.nc = nc
        self.stack = stack
        self.ops = {e: [] for e in ENGS}
        self.n = {e: 0 for e in ENGS}
        self.seen = {e: {} for e in ENGS}
        self.prog = {e: stack.enter_context(nc.semaphore("prog_" + e)) for e in ENGS}
        self.dma_sems = {}
        self.dma_cnt = {}

    def dma_sem(self, key):
        if key not in self.dma_sems:
            self.dma_sems[key] = self.stack.enter_context(self.nc.semaphore("d_" + key))
            self.dma_cnt[key] = 0
        return self.dma_sems[key]

    def _need(self, E, dep, waits):
        if dep[0] == "eng":
            _, E2, idx = dep
            if E2 == E and E == "pe":
                return
            key = ("eng", E2)
            if self.seen[E].get(key, 0) >= idx:
                return
            self.seen[E][key] = idx
            waits.append((self.prog[E2], idx))
        else:
            _, skey, cnt = dep
            key = ("dma", skey)
            if self.seen[E].get(key, 0) >= cnt:
                return
            self.seen[E][key] = cnt
            waits.append((self.dma_sems[skey], cnt))

    def _deps(self, E, reads, writes):
        waits = []
        for b in reads:
            if b.w is not None:
                self._need(E, b.w, waits)
        for b in writes:
            if b.w is not None:
                self._need(E, b.w, waits)
            for k, v in b.r.items():
                self._need(E, (k[0], k[1], v), waits)
        return waits

    def op(self, E, fn, reads=(), writes=()):
        waits = self._deps(E, reads, writes)
        self.n[E] += 1
        idx = self.n[E]
        self.ops[E].append((waits, fn, (self.prog[E], 1)))
        me = ("eng", E, idx)
        for b in reads:
            b.r[("eng", E)] = idx
        for b in writes:
            b.w = me
            b.r = {}
        return idx

    def dma(self, Q, fn, reads=(), writes=(), key=None):
        waits = self._deps(Q, reads, writes)
        sem = self.dma_sem(key)
        self.dma_cnt[key] += 16
        cnt = self.dma_cnt[key]
        self.ops[Q].append((waits, fn, (sem, 16)))
        me = ("dma", key, cnt)
        for b in reads:
            b.r[("dma", key)] = cnt
        for b in writes:
            b.w = me
            b.r = {}
        return cnt

    def final_wait(self, E, bufs):
        waits = []
        for b in bufs:
            if b.w is not None:
                self._need(E, b.w, waits)
            for k, v in b.r.items():
                self._need(E, (k[0], k[1], v), waits)
        self.ops[E].append((waits, None, None))

    def emit(self):
        nc = self.nc
        with nc.Block() as block:
            def run(E, eng):
                for waits, fn, inc in self.ops[E]:
                    for sem, val in waits:
                        eng.wait_ge(sem, val)
                    if fn is not None:
                        fn().then_inc(inc[0], inc[1])

            @block.tensor
            def _(e):
                run("pe", e)

            @block.scalar
            def _(e):
                run("act", e)

            @block.vector
            def _(e):
                run("dve", e)

            @block.gpsimd
            def _(e):
                run("pool", e)

            @block.sync
            def _(e):
                run("sp", e)


def weight_items():
    it = []
    for j in range(3):
        it.append((f"cq{j}", 8, 128))
    for j in range(2):
        it.append((f"ckv{j}", 8, 128))
    it.append(("kpe", 8, 64))
    for j in range(4):
        it.append((f"uk{j}", 2, 128))
    it.append(("uv", 2, 512))
    for h in range(8):
        it.append((f"uq{h}", 3, 128))
    for h in range(4):
        it.append((f"hf{h}", 8, 128))
        it.append((f"hq{h}", 8, 128))
        it.append((f"hi{h}", 8, 128))
        it.append((f"hg{h}", 8, 128))
    for o in range(8):
        it.append((f"ga{o}", 8, 128))
        it.append((f"gb{o}", 8, 128))
        it.append((f"pa{o}", 4, 128))
        it.append((f"pb{o}", 4, 128))
    for o in range(8):
        it.append((f"wo{o}", 8, 128))
    for f in range(NF):
        it.append((f"gate{f}", 8, 128))
        it.append((f"up{f}", 8, 128))
    for o in range(8):
        it.append((f"dna{o}", 11, 128))
        it.append((f"dnb{o}", 11, 128))
    return it


def weight_groups():
    groups = []
    index = {}
    cur_off = 0
    cur_size = 0
    start = 0
    for name, kc, m in weight_items():
        n = kc * m
        if cur_size + n > SLOT:
            groups.append((start, cur_size))
            start += cur_size
            cur_size = 0
        index[name] = (len(groups), cur_size, kc, m)
        cur_size += n
    groups.append((start, cur_size))
    total = start + cur_size
    total = ((total + 8191) // 8192) * 8192
    return groups, index, total


IN_OFF = {}
_o = 0
for _n, _s in (("cq", 384), ("ckv", 256), ("kpe", 32), ("hq", 512), ("hf", 512), ("hi", 512), ("hg", 512),
               ("ga", 1024), ("gb", 1024)):
    IN_OFF[_n] = _o
    _o += _s

NV = 40


def pack_layer_weights(inp, l):
    w_in = inp["w_in"][l]
    cols = {}

    def rng(a, n):
        return list(range(a, a + n))
    for j in range(3):
        cols[f"cq{j}"] = (w_in, rng(IN_OFF["cq"] + j * 128, 128))
    for j in range(2):
        cols[f"ckv{j}"] = (w_in, rng(IN_OFF["ckv"] + j * 128, 128))
    kp = IN_OFF["kpe"]
    cols["kpe"] = (w_in, rng(kp, 32) + rng(kp + 16, 16) + rng(kp, 16))
    ukv = inp["mla_w_ukv"][l]
    for j in range(4):
        cols[f"uk{j}"] = (ukv, rng((2 * j) * 128, 64) + rng((2 * j + 1) * 128, 64))
    vc = []
    for h in range(8):
        vc += rng(h * 128 + 64, 64)
    cols["uv"] = (ukv, vc)
    uq = inp["mla_w_uq"][l]
    for h in range(8):
        cols[f"uq{h}"] = (uq, rng(h * 96, 96) + rng(h * 96 + 80, 16) + rng(h * 96 + 64, 16))
    for h in range(4):
        for nm in ("hf", "hq", "hi", "hg"):
            cols[f"{nm}{h}"] = (w_in, rng(IN_OFF[nm] + h * 128, 128))
    for o in range(8):
        cols[f"pa{o}"] = (inp["w_proj_a"][l], rng(o * 128, 128))
        cols[f"pb{o}"] = (inp["w_proj_b"][l], rng(o * 128, 128))
        cols[f"ga{o}"] = (w_in, rng(IN_OFF["ga"] + o * 128, 128))
        cols[f"gb{o}"] = (w_in, rng(IN_OFF["gb"] + o * 128, 128))
        cols[f"wo{o}"] = (inp["w_out"][l], rng(o * 128, 128))
    for f in range(NF):
        cols[f"gate{f}"] = (inp["w_gate"][l], rng(f * 128, 128))
        cols[f"up{f}"] = (inp["w_up"][l], rng(f * 128, 128))
    wd = inp["w_down"][l]
    for o in range(8):
        cols[f"dna{o}"] = (wd[0:11 * 128], rng(o * 128, 128))
        cols[f"dnb{o}"] = (wd[11 * 128:22 * 128], rng(o * 128, 128))
    groups, index, total = weight_groups()
    out = np.zeros((128, total), np.float32)
    for name, kc, m in weight_items():
        W, cl = cols[name]
        g, off, _, _ = index[name]
        base = groups[g][0] + off
        blk = W[:kc * 128][:, cl].reshape(kc, 128, m).transpose(1, 0, 2).reshape(128, kc * m)
        out[:, base:base + kc * m] = blk
    return out


def pack_vecs(inp, l):
    v = np.zeros((128, NV), np.float32)
    v[:, 0:8] = inp["norm_mix"][l].reshape(8, 128).T
    v[:, 8:16] = inp["norm_ffn"][l].reshape(8, 128).T
    v[:, 16:19] = inp["mla_norm_cq"][l].reshape(3, 128).T
    v[:, 19:21] = inp["mla_norm_ckv"][l].reshape(2, 128).T
    qn = inp["mla_q_norm"][l]
    kn = inp["mla_k_norm"][l]
    v[0:96, 21] = qn
    v[96:112, 21] = qn[80:96]
    v[112:128, 21] = qn[64:80]
    v[0:64, 22] = kn[0:64]
    v[64:128, 22] = kn[0:64]
    v[64:96, 23] = kn[64:96]
    v[96:112, 23] = kn[80:96]
    v[112:128, 23] = kn[64:80]
    v[:, 24] = inp["hg_out_norm"][l]
    for ll in range(inp["hg_lb_logits"].shape[0]):
        v[:, 25 + 4 * ll:29 + 4 * ll] = inp["hg_lb_logits"][ll].reshape(4, 128).T
    return v


def const_tables(T):
    pos = np.arange(T, dtype=np.float32)
    inv_freq = (1.0 / (np.float32(10000.0) ** (np.arange(0, 32, 2, dtype=np.float32) / np.float32(32)))).astype(np.float32)
    ang = (pos[:, None] * inv_freq[None, :]).astype(np.float32)
    cos = np.cos(ang).astype(np.float32).T
    sin = np.sin(ang).astype(np.float32).T
    cs = np.ones((128, T), np.float32)
    cs[64:80] = cos
    cs[80:96] = cos
    cs[96:112] = -sin
    cs[112:128] = sin
    k = np.arange(128)
    tri_att = (k[None, :] >= k[:, None]).astype(np.float32)
    ident = np.eye(128, dtype=np.float32)
    s64 = k % 64
    t64 = np.arange(64)
    tri_h = (t64[None, :] >= s64[:, None]).astype(np.float32)
    tri_h4 = np.tile(tri_h, (1, 4))
    scanmask = np.ones((128, TB), np.float32)
    scanmask[:, ::64] = 0.0
    misc = np.concatenate([tri_att, ident, tri_h4, scanmask], axis=1).astype(np.float32)
    return cs, misc


def build_program(NSEQ, T, L, dbg=False):
    NBLK = T // TB
    NT = T // 128
    groups, windex, WTOT = weight_groups()
    NG = len(groups)
    nc = bass.Bass("TRN2", target_bir_lowering=False)
    xT_d = nc.dram_tensor("xT", [NSEQ, D, T], F32, kind="ExternalInput").ap()
    NWS = 1
    WPC = WTOT // NWS
    wp_d = [[nc.dram_tensor(f"wpack{l}_{i}", [128, WPC], F32, kind="ExternalInput").ap() for i in range(NWS)]
            for l in range(L)]
    vec_d = nc.dram_tensor("vecs", [128, L * NV], F32, kind="ExternalInput").ap()
    cs_d = nc.dram_tensor("cs", [128, T], F32, kind="ExternalInput").ap()
    misc_d = nc.dram_tensor("misc", [128, 512 + TB], F32, kind="ExternalInput").ap()
    out_d = nc.dram_tensor("outT", [NSEQ, D, T], F32, kind="ExternalOutput").ap()
    import os as _os
    if _os.environ.get("KDUMMY"):
        nc.dram_tensor("dummyin", [128, WTOT], F32, kind="ExternalInput")
    wbf_v = [wp_d[l][0].bitcast(BF16) for l in range(L)]

    with ExitStack() as st:
        S = Sched(nc, st)

        def sb(name, shape, dt):
            return st.enter_context(nc.sbuf_tensor(name, shape, dt))

        xT = sb("xT_sb", [128, KC, T], F32)
        XT = [[Buf(f"xT{c}_{b}") for b in range(NBLK)] for c in range(KC)]
        cs = sb("cs_sb", [128, T], F32); CS = Buf("cs")
        KT = sb("KT", [128, 8, T], BF16)
        KTB = [Buf(f"KT{h}") for h in range(8)]
        Vx = sb("Vx", [128, NT, 4, 192], BF16)
        VXB = [Buf(f"Vx{i}") for i in range(NT)]
        VXONES = Buf("vxones")
        skall = sb("skall", [128, NT * 8], F32)
        SK = [Buf(f"sk{b}") for b in range(NBLK)]
        wslot = [sb(f"wslot{i}", [128, SLOT], BF16) for i in range(NSLOT)]
        WS = [Buf(f"wslot{i}") for i in range(NSLOT)]
        vec = sb("vec_sb", [128, L * NV], F32); VEC = Buf("vec")
        cbf = sb("cbf", [128, 128 * 8], BF16); CBF = Buf("cbf")
        cvec = sb("cvec", [128, 8], F32); CVEC = Buf("cvec")
        lbt = sb("lbt", [128, L * 4], F32); LBT = Buf("lbt")
        lbtmp = sb("lbtmp", [128, 40], F32); LBTMP = Buf("lbtmp")
        gqs = sb("gqs", [128, L], F32); GQS = Buf("gqs")
        Sst = sb("Sst", [128, 4, 2, 128], F32)
        SST = [[Buf(f"S{h}_{i}") for i in range(2)] for h in range(4)]
        Sbf = sb("Sbf", [128, 8, 128], BF16)
        SBF = [Buf(f"Sbf{i}") for i in range(8)]
        hT = sb("hT", [128, KC, TB], BF16)
        HT = [Buf(f"hT{c}") for c in range(KC)]
        NSL = 22
        bfs = sb("bfs", [128, NSL, TB], BF16)
        BFS = [Buf(f"bfs{i}") for i in range(NSL)]
        yT, YT = bfs, BFS
        sqk = bfs[:, 8:13, :]; SQK = BFS[8:13]
        cqn = bfs[:, 0:3, :]; CQN = BFS[0:3]
        ckvn = bfs[:, 3:5, :]; CKVN = BFS[3:5]
        qTh = [bfs[:, 5 + i, :] for i in range(2)]; QTH = BFS[5:7]
        pT = [bfs[:, 7 + i, :] for i in range(3)]; PT = BFS[7:10]
        qp = bfs[:, 10, :]; QP = BFS[10]
        kt_ = bfs[:, 11, :]; KTL = BFS[11]
        kpT = bfs[:, 12, :]; KPT = BFS[12]
        attnT = bfs[:, 13:17, :]; ATT = BFS[13:17]
        ogT = bfs[:, 17:21, :]; OGT = BFS[17:21]
        actT, ACTT = bfs, BFS
        sq = [sb(f"sq{i}", [128, TB], BF16) for i in range(2)]
        SQ = [Buf(f"sq{i}") for i in range(2)]
        rstd = sb("rstd", [128, TB], F32); RSTD = Buf("rstd")
        ft = [sb(f"ft{i}", [128, TB], F32) for i in range(6)]
        FT = [Buf(f"ft{i}") for i in range(6)]
        cqf = [ft[0], ft[1], ft[2]]; CQF = FT[0:3]
        rk, RK = ft[4], FT[4]
        rk2, RK2 = ft[5], FT[5]
        kptok = sb("kptok", [128, 4, 128], BF16); KPTOK = Buf("kptok")
        vtok = sb("vtok", [128, 4, 128], BF16); VTOK = Buf("vtok")
        asb2 = sb("asb2", [128, 4, 64], BF16); ASB2 = [Buf(f"asb2_{i}") for i in range(4)]
        dec = sb("dec", [128, 8], F32); DEC = Buf("dec")
        kst_sb = sb("kst_sb", [128, 32], F32); KSTSB = Buf("kstsb")
        scanm = sb("scanm", [128, TB], F32); SCANM = Buf("scanm")
        trih = sb("trih", [128, 256], BF16); TRIH = Buf("trih")

        NRING = 4
        pring = [st.enter_context(nc.psum_tensor(f"pr{i}", [128, 512], F32)) for i in range(NRING)]
        PR = [Buf(f"pr{i}") for i in range(NRING)]
        pstat = st.enter_context(nc.psum_tensor("pstat", [128, 512], F32)); PSTAT = Buf("pstat")
        pacc = [st.enter_context(nc.psum_tensor(f"pacc{i}", [128, 512], F32)) for i in range(2)]
        PACC = [Buf(f"pacc{i}") for i in range(2)]
        ptr = st.enter_context(nc.psum_tensor("ptr", [128, 1024], BF16)); PTR = Buf("ptr")
        ring_i = [0]

        def ring():
            i = ring_i[0] % NRING
            ring_i[0] += 1
            return pring[i], PR[i]

        def ACT(out, in_, func, reads, writes, scale=None, bias=None):
            kw = {}
            if scale is not None:
                kw["scale"] = scale
            if bias is not None:
                kw["bias"] = bias
            S.op("act", lambda: nc.scalar.activation(out=out, in_=in_, func=func, **kw), reads, writes)

        def TT(eng, out, in0, in1, op, reads, writes):
            e = nc.vector if eng == "dve" else nc.gpsimd
            S.op(eng, lambda: e.tensor_tensor(out=out, in0=in0, in1=in1, op=op), reads, writes)

        def STT(out, in0, scalar, in1, op0, op1, reads, writes):
            S.op("dve", lambda: nc.vector.scalar_tensor_tensor(out=out, in0=in0, scalar=scalar, in1=in1,
                                                               op0=op0, op1=op1), reads, writes)

        def TS(eng, out, in0, s1, s2, op0, op1, reads, writes):
            e = nc.vector if eng == "dve" else nc.gpsimd
            if s2 is None:
                S.op(eng, lambda: e.tensor_scalar(out=out, in0=in0, scalar1=s1, scalar2=None, op0=op0), reads, writes)
            else:
                S.op(eng, lambda: e.tensor_scalar(out=out, in0=in0, scalar1=s1, scalar2=s2, op0=op0, op1=op1),
                     reads, writes)

        def CP(eng, out, in_, reads, writes):
            e = nc.vector if eng == "dve" else nc.gpsimd
            S.op(eng, lambda: e.tensor_copy(out=out, in_=in_), reads, writes)

        def MM(out, lhsT, rhs, start, stop, reads, writes):
            S.op("pe", lambda: nc.tensor.matmul(out, lhsT=lhsT, rhs=rhs, start=start, stop=stop), reads, writes)

        def RECIP(out, in_, reads, writes):
            S.op("dve", lambda: nc.vector.reciprocal(out=out, in_=in_), reads, writes)

        def SCAN(out, d0, d1, reads, writes):
            S.op("dve", lambda: nc.vector.tensor_tensor_scan(out=out, data0=d0, data1=d1, initial=0.0,
                                                             op0=ALU.mult, op1=ALU.add), reads, writes)

        def TRANS(out, in_, ident, reads, writes):
            S.op("pe", lambda: nc.tensor.transpose(out, in_, ident), reads, writes)

        def MSET(ap, val, writes):
            S.op("pool", lambda: nc.gpsimd.memset(ap, val), (), writes)

        S.dma("sp", lambda: nc.sync.dma_start(out=vec[:], in_=vec_d), writes=[VEC], key="vec")
        S.dma("sp", lambda: nc.sync.dma_start(out=cs[:], in_=cs_d), writes=[CS], key="cs")
        S.dma("sp", lambda: nc.sync.dma_start(out=ft[0][:], in_=misc_d[:, 0:512]), writes=[FT[0]], key="misc")
        S.dma("sp", lambda: nc.sync.dma_start(out=scanm[:], in_=misc_d[:, 512:512 + TB]), writes=[SCANM], key="scanm")
        ONES_D, ONES_CQ, ONES_CKV, ONES_Q, ONES_O, ONES_K, TRI, IDN = range(8)

        def cm(i):
            return cbf[:, i * 128:(i + 1) * 128]
        MSET(cm(ONES_D), 1.0 / 1024, [CBF])
        MSET(cm(ONES_CQ), 1.0 / 384, [CBF])
        MSET(cm(ONES_CKV), 1.0 / 256, [CBF])
        MSET(cm(ONES_Q), 1.0 / 96, [CBF])
        MSET(cbf[96:128, ONES_Q * 128:(ONES_Q + 1) * 128], 0.0, [CBF])
        MSET(cm(ONES_O), 1.0 / 128, [CBF])
        MSET(cm(ONES_K), 0.0, [CBF])
        MSET(cbf[64:96, ONES_K * 128:ONES_K * 128 + 8], 1.0 / 96, [CBF])
        MSET(cbf[0:64, ONES_K * 128 + 8:ONES_K * 128 + 9], 1.0 / 96, [CBF])
        MSET(cbf[64:128, ONES_K * 128 + 9:ONES_K * 128 + 10], 1.0 / 96, [CBF])
        for i_ in range(NSL):
            MSET(bfs[:, i_, :], 0.0, [BFS[i_]])
        CP("pool", cm(TRI), ft[0][:, 0:128], [FT[0]], [CBF])
        CP("pool", cm(IDN), ft[0][:, 128:256], [FT[0]], [CBF])
        CP("pool", trih[:], ft[0][:, 256:512], [FT[0]], [TRIH])
        MSET(cvec[:, 0:1], EPS, [CVEC])
        MSET(cvec[:, 1:2], 1.0, [CVEC])
        MSET(cvec[:, 2:3], 0.0, [CVEC])
        C_EPS, C_ONE, C_ZERO = cvec[:, 0:1], cvec[:, 1:2], cvec[:, 2:3]
        MSET(Vx[:, :, :, 64:128], 1.0, [VXONES])
        ll = [vec[:, l * NV + 25: l * NV + 25 + 4 * L] for l in range(1)][0]
        mx, e_all, ssum, rs, cum = lbtmp[:, 0:4], lbtmp[:, 4:4 + 4 * L], lbtmp[:, 16:20], lbtmp[:, 20:24], lbtmp[:, 24:28]
        CP("dve", mx, ll[:, 0:4], [VEC], [LBTMP])
        for l in range(1, L):
            TT("dve", mx, mx, ll[:, 4 * l:4 * l + 4], ALU.max, [VEC, LBTMP], [LBTMP])
        for l in range(L):
            TT("dve", e_all[:, 4 * l:4 * l + 4], ll[:, 4 * l:4 * l + 4], mx, ALU.subtract, [VEC, LBTMP], [LBTMP])
        ACT(e_all, e_all, AF.Exp, [LBTMP], [LBTMP])
        CP("dve", ssum, e_all[:, 0:4], [LBTMP], [LBTMP])
        for l in range(1, L):
            TT("dve", ssum, ssum, e_all[:, 4 * l:4 * l + 4], ALU.add, [LBTMP], [LBTMP])
        RECIP(rs, ssum, [LBTMP], [LBTMP])
        for l in range(L):
            TT("dve", e_all[:, 4 * l:4 * l + 4], e_all[:, 4 * l:4 * l + 4], rs, ALU.mult, [LBTMP], [LBTMP])
        CP("dve", cum, e_all[:, 0:4], [LBTMP], [LBTMP])
        for l in range(L):
            if l > 0:
                TT("dve", cum, cum, e_all[:, 4 * l:4 * l + 4], ALU.add, [LBTMP], [LBTMP])
            TT("dve", lbt[:, 4 * l:4 * l + 4], cum, e_all[:, 0:4], ALU.subtract, [LBTMP], [LBT])
            TS("dve", lbt[:, 4 * l:4 * l + 4], lbt[:, 4 * l:4 * l + 4], 0.0, None, ALU.max, ALU.bypass, [LBT], [LBT])
        for l in range(L):
            TS("dve", gqs[:, l:l + 1], vec[:, l * NV + 21:l * NV + 22], float(96.0 ** -0.5), None, ALU.mult, ALU.bypass,
               [VEC], [GQS])

        RPC = 4
        PCS = RPC * T
        while WTOT % PCS:
            RPC //= 2
            PCS = RPC * T
        NCHK = WTOT // PCS
        WCH = []
        jj = 0
        for l in range(L):
            row = []
            DCH = [Buf(f"dch{l}_{k}") for k in range(NCHK)]
            for k in range(NCHK):
                c0, c1 = k * PCS, (k + 1) * PCS
                hb = jj % (8 // RPC)
                r0 = hb * RPC
                jj += 1
                bq = Buf(f"wch{l}_{c0}")
                xbufs = [XT[c][bb_] for c in range(r0, r0 + RPC) for bb_ in range(NBLK)]
                kbufs = KTB[r0:r0 + RPC]
                xs_ = xT[:, r0:r0 + RPC, :]
                ks_ = KT[:, r0:r0 + RPC, :]

                def ld(l=l, c0=c0, c1=c1, xs_=xs_):
                    return nc.sync.dma_start(out=xs_, in_=wp_d[l][0][:, c0:c1].rearrange("p (r t) -> p r t", t=T))
                S.dma("sp", ld, reads=[DCH[k]], writes=xbufs, key=f"wld{hb}")
                for r in range(RPC):
                    eng = ("act", "dve", "pool")[(jj + r) % 3]
                    c = r0 + r
                    if eng == "act":
                        ACT(KT[:, c, :], xT[:, c, :], AF.Copy, XT[c], [KTB[c]])
                    else:
                        CP(eng, KT[:, c, :], xT[:, c, :], XT[c], [KTB[c]])

                def stf(l=l, c0=c0, c1=c1, ks_=ks_):
                    return nc.sync.dma_start(out=wbf_v[l][:, c0:c1].rearrange("p (r t) -> p r t", t=T), in_=ks_)
                S.dma("sp", stf, reads=kbufs, writes=[bq, DCH[k // 2]], key=f"wst{hb}")
                row.append((c0, c1, bq))
            WCH.append(row)
        MSET(KT[96:128, :, :], 0.0, KTB)

        class WStream:
            def __init__(self):
                self.seq = []
                self.pos = 0
                self.cur = -1
                self.curkey = None

            def plan(self, order):
                self.seq = order

            def _load(self, i):
                l, g = self.seq[i]
                off, size = groups[g]
                slot = i % NSLOT
                rd = [b for (c0, c1, b) in WCH[l] if c0 < off + size and c1 > off]

                def fn(l=l, off=off, size=size, slot=slot):
                    return nc.sync.dma_start(out=wslot[slot][:, 0:size], in_=wbf_v[l][:, off:off + size])
                S.dma("sp", fn, reads=rd, writes=[WS[slot]], key=f"ws{slot}")

            def start(self):
                for i in range(min(NSLOT, len(self.seq))):
                    self._load(i)
                self.pos = min(NSLOT, len(self.seq))
                self.cur = 0

            def get(self, l, name):
                g, off, kc, m = windex[name]
                while self.seq[self.cur] != (l, g):
                    self.cur += 1
                    if self.pos < len(self.seq) and self.pos - self.cur < NSLOT - 1 + 1:
                        pass
                    while self.pos < len(self.seq) and self.pos < self.cur + NSLOT:
                        self._load(self.pos)
                        self.pos += 1
                slot = self.cur % NSLOT
                ap = wslot[slot][:, off:off + kc * m].rearrange("p (k m) -> p k m", m=m)
                return ap, WS[slot]

        WST = WStream()
        order = []
        for s in range(NSEQ):
            for l in range(L):
                for b in range(NBLK):
                    for g in range(NG):
                        order.append((l, g))
        WST.plan(order)
        WST.start()

        def rmsstat_to_rstd(reads_extra=()):
            ACT(rstd[:], pstat[:], AF.Ln, [PSTAT, CVEC], [RSTD], bias=C_EPS)
            ACT(rstd[:], rstd[:], AF.Exp, [RSTD], [RSTD], scale=-0.5)

        def norm_x(l, b, gcol):
            cols = slice(b * TB, (b + 1) * TB)
            for c in range(KC):
                i = c % 2
                ACT(sq[i][:], xT[:, c, cols], AF.Square, [XT[c][b]], [SQ[i]])
                MM(pstat[:], cm(ONES_D), sq[i][:], c == 0, c == KC - 1, [CBF, SQ[i]], [PSTAT])
            rmsstat_to_rstd()
            for c in range(KC):
                STT(hT[:, c, :], xT[:, c, cols], vec[:, l * NV + gcol + c:l * NV + gcol + c + 1], rstd[:],
                    ALU.mult, ALU.mult, [XT[c][b], VEC, RSTD], [HT[c]])

        def proj_fm(l, name, rhs_ap, rhs_bufs, nk, M=128, out_rows=None):
            w, wb = WST.get(l, name)
            ps, psb = ring()
            o = ps[:] if out_rows is None else ps[out_rows[0]:out_rows[1], :]
            for k in range(nk):
                MM(o, w[:, k, :], rhs_ap(k), k == 0, k == nk - 1, [wb] + list(rhs_bufs), [psb])
            return ps, psb

        def mixer_block(l, b):
            cols = slice(b * TB, (b + 1) * TB)
            V0 = l * NV
            norm_x(l, b, 0)
            hrhs = lambda k: hT[:, k, :]
            for j in range(3):
                ps, psb = proj_fm(l, f"cq{j}", hrhs, HT, KC)
                i = j % 2
                ACT(sq[i][:], ps[:], AF.Square, [psb], [SQ[i]])
                ACT(cqf[j][:], ps[:], AF.Copy, [psb], [CQF[j]])
                MM(pstat[:], cm(ONES_CQ), sq[i][:], j == 0, j == 2, [CBF, SQ[i]], [PSTAT])
            rmsstat_to_rstd()
            for j in range(3):
                STT(cqn[:, j, :], cqf[j][:], vec[:, V0 + 16 + j:V0 + 17 + j], rstd[:], ALU.mult, ALU.mult,
                    [CQF[j], VEC, RSTD], [CQN[j]])
            for j in range(2):
                ps, psb = proj_fm(l, f"ckv{j}", hrhs, HT, KC)
                i = j % 2
                ACT(sq[i][:], ps[:], AF.Square, [psb], [SQ[i]])
                ACT(cqf[j][:], ps[:], AF.Copy, [psb], [CQF[j]])
                MM(pstat[:], cm(ONES_CKV), sq[i][:], j == 0, j == 1, [CBF, SQ[i]], [PSTAT])
            rmsstat_to_rstd()
            for j in range(2):
                STT(ckvn[:, j, :], cqf[j][:], vec[:, V0 + 19 + j:V0 + 20 + j], rstd[:], ALU.mult, ALU.mult,
                    [CQF[j], VEC, RSTD], [CKVN[j]])
            ps, psb = proj_fm(l, "kpe", hrhs, HT, KC, M=64, out_rows=(64, 128))
            ACT(sqk[64:128, 4, :], ps[64:128, :], AF.Square, [psb], [SQK[4]])
            STT(rk[64:128, :], ps[64:128, :], vec[64:128, V0 + 23:V0 + 24], cs[64:128, cols], ALU.mult, ALU.mult,
                [psb, VEC, CS], [RK])
            ACT(rk2[64:96, :], rk[96:128, :], AF.Copy, [RK], [RK2])
            TT("pool", rk[64:96, :], rk[64:96, :], rk2[64:96, :], ALU.add, [RK, RK2], [RK])
            for h in range(8):
                CP("pool", KT[64:96, h, cols], rk[64:96, :], [RK], [KTB[h]])
            ckrhs = lambda k: ckvn[:, k, :]
            for j in range(4):
                ps, psb = proj_fm(l, f"uk{j}", ckrhs, CKVN, 2)
                ACT(sqk[:, j, :], ps[:], AF.Square, [psb], [SQK[j]])
                ACT(KT[0:64, 2 * j, cols], ps[0:64, :], AF.Copy, [psb, VEC], [KTB[2 * j]], scale=vec[0:64, V0 + 22:V0 + 23])
                ACT(KT[0:64, 2 * j + 1, cols], ps[64:128, :], AF.Copy, [psb, VEC], [KTB[2 * j + 1]],
                    scale=vec[64:128, V0 + 22:V0 + 23])
            for i in range(4):
                tsl = slice(i * 128, (i + 1) * 128)
                MM(pstat[:, i * 8:(i + 1) * 8], sqk[:, 4, tsl], cbf[:, ONES_K * 128:ONES_K * 128 + 8], True, False,
                   [SQK[4], CBF], [PSTAT])
                for h in range(8):
                    sel = ONES_K * 128 + 8 + (h % 2)
                    MM(pstat[:, i * 8 + h:i * 8 + h + 1], sqk[:, h // 2, tsl],
                       cbf[:, sel:sel + 1], False, h == 7, [SQK[h // 2], CBF], [PSTAT])
            ACT(kst_sb[:, :], pstat[:, 0:32], AF.Ln, [PSTAT, CVEC], [KSTSB], bias=C_EPS)
            ACT(skall[:, b * 32:(b + 1) * 32], kst_sb[:, :], AF.Exp, [KSTSB], [SK[b]], scale=-0.5)
            wv, wvb = WST.get(l, "uv")
            for i in range(4):
                tsl = slice(i * 128, (i + 1) * 128)
                ps, psb = ring()
                for k in range(2):
                    MM(ps[:], ckvn[:, k, tsl], wv[:, k, :], k == 0, k == 1, CKVN + [wvb], [psb])
                pv = ps[:].rearrange("p (j e d) -> p j e d", e=2, d=64)
                ti = b * 4 + i
                CP("dve", Vx[:, ti, :, 0:64], pv[:, :, 0, :], [psb], [VXB[ti]])
                ACT(Vx[:, ti, :, 128:192], pv[:, :, 1, :], AF.Copy, [psb], [VXB[ti]])
            cqrhs = lambda k: cqn[:, k, :]
            def q_prep(h):
                ps, psb = proj_fm(l, f"uq{h}", cqrhs, CQN, 3)
                i = h % 2
                ACT(sq[i][:], ps[:], AF.Square, [psb], [SQ[i]])
                MM(pstat[:], cm(ONES_Q), sq[i][:], True, True, [CBF, SQ[i]], [PSTAT])
                rmsstat_to_rstd()
                STT(ft[0][:], ps[:], gqs[:, l:l + 1], rstd[:], ALU.mult, ALU.mult, [psb, GQS, RSTD], [FT[0]])
                TT("pool", ft[1][:], ft[0][:], cs[:, cols], ALU.mult, [FT[0], CS], [FT[1]])
                qt, QB = qTh[h % 2], QTH[h % 2]
                CP("pool", qt[0:64, :], ft[1][0:64, :], [FT[1]], [QB])
                ACT(ft[2][64:96, :], ft[1][96:128, :], AF.Copy, [FT[1]], [FT[2]])
                TT("pool", qt[64:96, :], ft[1][64:96, :], ft[2][64:96, :], ALU.add, [FT[1], FT[2]], [QB])
                CP("pool", qt[96:128, :], ft[1][96:128, :], [FT[1]], [QB])

            def attn(h):
                qt, QB = qTh[h % 2], QTH[h % 2]
                j, par = h // 2, h % 2
                po, POB = pacc[h % 2], PACC[h % 2]
                nkt = 4 * b + 4
                pend = None
                for kt in range(nkt):
                    r = kt - 4 * b
                    q0 = max(0, r) * 128
                    nq = TB - q0
                    ps2, ps2b = ring()
                    MM(ps2[:, 0:nq], KT[:, h, kt * 128:(kt + 1) * 128], qt[:, q0:TB], True, True,
                       [KTB[h], QB], [ps2b])
                    pt, PTB = pT[kt % 3], PT[kt % 3]
                    ACT(pt[:, 0:nq], ps2[:, 0:nq], AF.Exp, [ps2b, SK[kt // 4]], [PTB],
                        scale=skall[:, kt * 8 + h:kt * 8 + h + 1])
                    if r >= 0:
                        TT("pool", pt[:, 0:128], pt[:, 0:128], cm(TRI), ALU.mult, [PTB, CBF], [PTB])
                    if pend is not None:
                        MM(*pend)
                    pend = (po[:, q0:TB], Vx[:, kt, j, par * 64:par * 64 + 128], pt[:, 0:nq], kt == 0, kt == nkt - 1,
                            [VXB[kt], VXONES, PTB], [POB])
                MM(*pend)
                if par == 0:
                    RECIP(ft[3][64:128, :], po[64:128, :], [POB], [FT[3]])
                    TT("dve", attnT[0:64, j, :], po[0:64, :], ft[3][64:128, :], ALU.mult, [POB, FT[3]], [ATT[j]])
                else:
                    RECIP(ft[3][0:64, :], po[0:64, :], [POB], [FT[3]])
                    TT("dve", attnT[64:128, j, :], po[64:128, :], ft[3][0:64, :], ALU.mult, [POB, FT[3]], [ATT[j]])

            q_prep(0)
            for h in range(8):
                if h < 7:
                    q_prep(h + 1)
                attn(h)
            for h in range(4):
                lbc = lbt[:, 4 * l + h:4 * l + h + 1]
                ps, psb = proj_fm(l, f"hf{h}", hrhs, HT, KC)
                e, l1, l2, bb, kk, tmp = ft[0], ft[1], ft[2], ft[3], ft[4], ft[5]
                E, L1, L2, BB, KK, TMP = FT[0], FT[1], FT[2], FT[3], FT[4], FT[5]
                ACT(e[:], ps[:], AF.Exp, [psb], [E], scale=-1.0)
                ACT(l1[:], e[:], AF.Ln, [E, LBT, CVEC], [L1], scale=lbc, bias=C_ONE)
                ACT(l2[:], e[:], AF.Ln, [E, CVEC], [L2], bias=C_ONE)
                TT("pool", l1[:], l1[:], l2[:], ALU.subtract, [L1, L2], [L1])
                SCAN(bb[:], scanm[:], l1[:], [SCANM, L1], [BB])
                ACT(kk[:], l1[:], AF.Exp, [L1], [KK])
                TS("pool", kk[:], kk[:], -1.0, 1.0, ALU.mult, ALU.add, [KK], [KK])
                b3 = bb[:].rearrange("p (c j) -> p c j", j=64)
                ACT(dec[:, :], b3[:, :, 63], AF.Exp, [BB], [DEC])
                ACT(e[:], bb[:], AF.Exp, [BB], [E])
                ACT(l2[:], bb[:], AF.Exp, [BB], [L2], scale=-1.0)
                TT("pool", tmp[:].rearrange("p (c j) -> p c j", j=64), b3, b3[:, :, 63:64].to_broadcast([128, 8, 64]),
                   ALU.subtract, [BB], [TMP])
                ACT(tmp[:], tmp[:], AF.Exp, [TMP], [TMP], scale=-1.0)
                TT("pool", kt_[:], kk[:], l2[:], ALU.mult, [KK, L2], [KTL])
                TT("pool", kpT[:], kk[:], tmp[:], ALU.mult, [KK, TMP], [KPT])
                ps, psb = proj_fm(l, f"hq{h}", hrhs, HT, KC)
                TT("dve", qp[:], ps[:], e[:], ALU.mult, [psb, E], [QP])
                wv, wvb = WST.get(l, f"hi{h}")
                ps, psb = ring()
                for i in range(4):
                    for k in range(KC):
                        MM(ps[:, i * 128:(i + 1) * 128], hT[:, k, i * 128:(i + 1) * 128], wv[:, k, :], k == 0, k == KC - 1,
                           HT + [wvb], [psb])
                CP("dve", vtok[:].rearrange("p i d -> p (i d)"), ps[:], [psb], [VTOK])
                for i in range(4):
                    TRANS(ptr[:, i * 128:(i + 1) * 128], kpT[:, i * 128:(i + 1) * 128], cm(IDN), [KPT, CBF], [PTR])
                CP("dve", kptok[:].rearrange("p i d -> p (i d)"), ptr[:, 0:512], [PTR], [KPTOK])
                for i in range(4):
                    ps, psb = ring()
                    for par in range(2):
                        c = 2 * i + par
                        csl = slice(c * 64, (c + 1) * 64)
                        MM(ps[par * 64:(par + 1) * 64, 0:64], kt_[:, csl], qp[:, csl], True, True, [KTL, QP], [psb])
                    TT("dve", asb2[:, i, :], ps[:, 0:64], trih[:, 0:64], ALU.mult, [psb, TRIH], [ASB2[i]])
                ubank = [ring(), ring()]
                for c in range(8):
                    i, par = c // 2, c % 2
                    rows = slice(par * 64, (par + 1) * 64)
                    psu, psub = ubank[c % 2]
                    MM(psu[:, (c // 2) * 128:(c // 2 + 1) * 128], kptok[rows, i, :], vtok[rows, i, :], True, True,
                       [KPTOK, VTOK], [psub])
                for c in range(8):
                    gi = b * 8 + c
                    cur, nxt = gi % 2, (gi + 1) % 2
                    psu, psub = ubank[c % 2]
                    ACT(Sbf[:, c, :], Sst[:, h, cur, :], AF.Copy, [SST[h][cur]], [SBF[c]])
                    STT(Sst[:, h, nxt, :], Sst[:, h, cur, :], dec[:, c:c + 1], psu[:, (c // 2) * 128:(c // 2 + 1) * 128],
                        ALU.mult, ALU.add, [SST[h][cur], DEC, psub], [SST[h][nxt]])
                ps, psb = proj_fm(l, f"hg{h}", hrhs, HT, KC)
                ACT(ft[1][:], ps[:], AF.Silu, [psb], [FT[1]])
                po, POB = pacc[h % 2], PACC[h % 2]
                for c in range(8):
                    i, par = c // 2, c % 2
                    rows = slice(par * 64, (par + 1) * 64)
                    csl = slice(c * 64, (c + 1) * 64)
                    MM(po[:, csl], Sbf[:, c, :], qp[:, csl], True, False, [SBF[c], QP], [POB])
                    MM(po[:, csl], vtok[rows, i, :], asb2[rows, i, :], False, True, [VTOK, ASB2[i]], [POB])
                ACT(sq[0][:], po[:], AF.Square, [POB], [SQ[0]])
                MM(pstat[:], cm(ONES_O), sq[0][:], True, True, [CBF, SQ[0]], [PSTAT])
                rmsstat_to_rstd()
                STT(ft[0][:], po[:], vec[:, V0 + 24:V0 + 25], rstd[:], ALU.mult, ALU.mult, [POB, VEC, RSTD], [FT[0]])
                TT("pool", ogT[:, h, :], ft[0][:], ft[1][:], ALU.mult, [FT[0], FT[1]], [OGT[h]])
            for o in range(8):
                pga, pgab = proj_fm(l, f"ga{o}", hrhs, HT, KC)
                ACT(ft[0][:], pga[:], AF.Sigmoid, [pgab], [FT[0]])
                pgb, pgbb = proj_fm(l, f"gb{o}", hrhs, HT, KC)
                ACT(ft[2][:], pgb[:], AF.Sigmoid, [pgbb], [FT[2]])
                psa, psab = proj_fm(l, f"pa{o}", lambda k: attnT[:, k, :], ATT, 4)
                TT("dve", ft[1][:], psa[:], ft[0][:], ALU.mult, [psab, FT[0]], [FT[1]])
                psb_, psbb = proj_fm(l, f"pb{o}", lambda k: ogT[:, k, :], OGT, 4)
                TT("dve", ft[3][:], psb_[:], ft[2][:], ALU.mult, [psbb, FT[2]], [FT[3]])
                TT("pool", yT[:, o, :], ft[1][:], ft[3][:], ALU.add, [FT[1], FT[3]], [YT[o]])
            for o in range(8):
                ps, psb = proj_fm(l, f"wo{o}", lambda k: yT[:, k, :], YT, KC)
                TT("dve", xT[:, o, cols], ps[:], xT[:, o, cols], ALU.add, [psb, XT[o][b]], [XT[o][b]])

        def ffn_block(l, b):
            cols = slice(b * TB, (b + 1) * TB)
            norm_x(l, b, 8)
            hrhs = lambda k: hT[:, k, :]
            for f in range(NF):
                pg, pgb = proj_fm(l, f"gate{f}", hrhs, HT, KC)
                pu, pub = proj_fm(l, f"up{f}", hrhs, HT, KC)
                i = f % 2
                ACT(ft[i][:], pg[:], AF.Silu, [pgb], [FT[i]])
                TT("dve", actT[:, f, :], pu[:], ft[i][:], ALU.mult, [pub, FT[i]], [ACTT[f]])
            for o in range(8):
                wa, wab = WST.get(l, f"dna{o}")
                ps, psb = ring()
                for k in range(11):
                    MM(ps[:], wa[:, k, :], actT[:, k, :], k == 0, False, [wab, ACTT[k]], [psb])
                wb_, wbb = WST.get(l, f"dnb{o}")
                for k in range(11):
                    MM(ps[:], wb_[:, k, :], actT[:, 11 + k, :], False, k == 10, [wbb, ACTT[11 + k]], [psb])
                TT("dve", xT[:, o, cols], ps[:], xT[:, o, cols], ALU.add, [psb, XT[o][b]], [XT[o][b]])

        OUTB = Buf("out")
        for s in range(NSEQ):
            for b in range(NBLK):
                cols = slice(b * TB, (b + 1) * TB)
                def fn(s=s, cols=cols):
                    return nc.sync.dma_start(out=xT[:, :, cols], in_=xT_d[s, :, cols].rearrange("(c p) t -> p c t", p=128))
                S.dma("sp", fn, writes=[XT[c][b] for c in range(KC)], key=f"x_{b}")
            for l in range(L):
                for h in range(4):
                    MSET(Sst[:, h, 0, :], 0.0, [SST[h][0]])
                for b in range(NBLK):
                    mixer_block(l, b)
                    ffn_block(l, b)
                    if l == L - 1:
                        cols = slice(b * TB, (b + 1) * TB)
                        def fn(s=s, cols=cols):
                            return nc.sync.dma_start(out=out_d[s, :, cols].rearrange("(c p) t -> p c t", p=128), in_=xT[:, :, cols])
                        S.dma("sp", fn, reads=[XT[c][b] for c in range(KC)], writes=[OUTB], key=f"x_{b}")
        S.final_wait("sp", [OUTB] + [XT[c][b] for c in range(KC) for b in range(NBLK)])
        S.emit()
    return nc


def _prep_shared(inputs, L, T):
    wpack = np.stack([pack_layer_weights(inputs, l) for l in range(L)], axis=0)
    vecs = np.concatenate([pack_vecs(inputs, l) for l in range(L)], axis=1)
    cs, misc = const_tables(T)
    return wpack, np.ascontiguousarray(vecs), cs, misc


def make_wmap(wpack):
    L, _, WTOT = wpack.shape
    NWS = 1
    WPC = WTOT // NWS
    return {f"wpack{l}_{i}": np.ascontiguousarray(wpack[l, :, i * WPC:(i + 1) * WPC]) for l in range(L) for i in range(NWS)}


def kernel(**inputs):
    inputs = {k: np.asarray(v) for k, v in inputs.items()}
    x = inputs["x"]
    B, T, _ = x.shape
    L = inputs["w_in"].shape[0]
    nseq = B // NCORES
    wpack, vecs, cs, misc = _prep_shared(inputs, L, T)
    wmap = make_wmap(wpack)
    nc = build_program(nseq, T, L)
    in_maps = []
    for c in range(NCORES):
        xs = np.ascontiguousarray(x[c * nseq:(c + 1) * nseq].transpose(0, 2, 1))
        in_maps.append(dict(wmap, xT=xs, vecs=vecs, cs=cs, misc=misc))
    res = run_bass_kernel_spmd(nc, in_maps, core_ids=list(range(NCORES)))
    outs = [np.asarray(r["outT"]).transpose(0, 2, 1) for r in res.results]
    return np.ascontiguousarray(np.concatenate(outs, axis=0)).astype(np.float32)
```

```python
import math
from contextlib import ExitStack

import numpy as np
import concourse.bass as bass
import concourse.mybir as mybir
from concourse.bass_utils import run_bass_kernel_spmd

F32 = mybir.dt.float32
BF16 = mybir.dt.bfloat16
ALU = mybir.AluOpType
AF = mybir.ActivationFunctionType

D = 1024
KC = 8
DFF = 2816
NF = 22
TB = 512
EPS = 1e-6
NCORES = 8
SLOT = 2048
NSLOT = 3
ENGS = ("pe", "act", "dve", "pool", "sp")


class Buf:
    __slots__ = ("name", "w", "r")

    def __init__(self, name):
        self.name = name
        self.w = None
        self.r = {}


class Sched:
    def __init__(self, nc, stack):
        self.nc = nc
        self.stack = stack
        self.ops = {e: [] for e in ENGS}
        self.n = {e: 0 for e in ENGS}
        self.seen = {e: {} for e in ENGS}
        self.prog = {e: stack.enter_context(nc.semaphore("prog_" + e)) for e in ENGS}
        self.dma_sems = {}
        self.dma_cnt = {}

    def dma_sem(self, key):
        if key not in self.dma_sems:
            self.dma_sems[key] = self.stack.enter_context(self.nc.semaphore("d_" + key))
            self.dma_cnt[key] = 0
        return self.dma_sems[key]

    def _need(self, E, dep, waits):
        if dep[0] == "eng":
            _, E2, idx = dep
            if E2 == E and E == "pe":
                return
            key = ("eng", E2)
            if self.seen[E].get(key, 0) >= idx:
                return
            self.seen[E][key] = idx
            waits.append((self.prog[E2], idx))
        else:
            _, skey, cnt = dep
            key = ("dma", skey)
            if self.seen[E].get(key, 0) >= cnt:
                return
            self.seen[E][key] = cnt
            waits.append((self.dma_sems[skey], cnt))

    def _deps(self, E, reads, writes):
        waits = []
        for b in reads:
            if b.w is not None:
                self._need(E, b.w, waits)
        for b in writes:
            if b.w is not None:
                self._need(E, b.w, waits)
            for k, v in b.r.items():
                self._need(E, (k[0], k[1], v), waits)
        return waits

    def op(self, E, fn, reads=(), writes=()):
        waits = self._deps(E, reads, writes)
        self.n[E] += 1
        idx = self.n[E]
        self.ops[E].append((waits, fn, (self.prog[E], 1)))
        me = ("eng", E, idx)
        for b in reads:
            b.r[("eng", E)] = idx
        for b in writes:
            b.w = me
            b.r = {}
        return idx

    def dma(self, Q, fn, reads=(), writes=(), key=None):
        waits = self._deps(Q, reads, writes)
        sem = self.dma_sem(key)
        self.dma_cnt[key] += 16
        cnt = self.dma_cnt[key]
        self.ops[Q].append((waits, fn, (sem, 16)))
        me = ("dma", key, cnt)
        for b in reads:
            b.r[("dma", key)] = cnt
        for b in writes:
            b.w = me
            b.r = {}
        return cnt

    def final_wait(self, E, bufs):
        waits = []
        for b in bufs:
            if b.w is not None:
                self._need(E, b.w, waits)
            for k, v in b.r.items():
                self._need(E, (k[0], k[1], v), waits)
        self.ops[E].append((waits, None, None))

    def emit(self):
        nc = self.nc
        with nc.Block() as block:
            def run(E, eng):
                for waits, fn, inc in self.ops[E]:
                    for sem, val in waits:
                        eng.wait_ge(sem, val)
                    if fn is not None:
                        fn().then_inc(inc[0], inc[1])

            @block.tensor
            def _(e):
                run("pe", e)

            @block.scalar
            def _(e):
                run("act", e)

            @block.vector
            def _(e):
                run("dve", e)

            @block.gpsimd
            def _(e):
                run("pool", e)

            @block.sync
            def _(e):
                run("sp", e)


def weight_items():
    it = []
    for j in range(3):
        it.append((f"cq{j}", 8, 128))
    for j in range(2):
        it.append((f"ckv{j}", 8, 128))
    it.append(("kpe", 8, 64))
    for j in range(4):
        it.append((f"uk{j}", 2, 128))
    it.append(("uv", 2, 512))
    for h in range(8):
        it.append((f"uq{h}", 3, 128))
    for h in range(4):
        it.append((f"hf{h}", 8, 128))
        it.append((f"hq{h}", 8, 128))
        it.append((f"hi{h}", 8, 128))
        it.append((f"hg{h}", 8, 128))
    for o in range(8):
        it.append((f"ga{o}", 8, 128))
        it.append((f"gb{o}", 8, 128))
        it.append((f"pa{o}", 4, 128))
        it.append((f"pb{o}", 4, 128))
    for o in range(8):
        it.append((f"wo{o}", 8, 128))
    for f in range(NF):
        it.append((f"gate{f}", 8, 128))
        it.append((f"up{f}", 8, 128))
    for o in range(8):
        it.append((f"dna{o}", 11, 128))
        it.append((f"dnb{o}", 11, 128))
    return it


def weight_groups():
    groups = []
    index = {}
    cur_off = 0
    cur_size = 0
    start = 0
    for name, kc, m in weight_items():
        n = kc * m
        if cur_size + n > SLOT:
            groups.append((start, cur_size))
            start += cur_size
            cur_size = 0
        index[name] = (len(groups), cur_size, kc, m)
        cur_size += n
    groups.append((start, cur_size))
    total = start + cur_size
    total = ((total + 8191) // 8192) * 8192
    return groups, index, total


IN_OFF = {}
_o = 0
for _n, _s in (("cq", 384), ("ckv", 256), ("kpe", 32), ("hq", 512), ("hf", 512), ("hi", 512), ("hg", 512),
               ("ga", 1024), ("gb", 1024)):
    IN_OFF[_n] = _o
    _o += _s

NV = 40


def pack_layer_weights(inp, l):
    w_in = inp["w_in"][l]
    cols = {}

    def rng(a, n):
        return list(range(a, a + n))
    for j in range(3):
        cols[f"cq{j}"] = (w_in, rng(IN_OFF["cq"] + j * 128, 128))
    for j in range(2):
        cols[f"ckv{j}"] = (w_in, rng(IN_OFF["ckv"] + j * 128, 128))
    kp = IN_OFF["kpe"]
    cols["kpe"] = (w_in, rng(kp, 32) + rng(kp + 16, 16) + rng(kp, 16))
    ukv = inp["mla_w_ukv"][l]
    for j in range(4):
        cols[f"uk{j}"] = (ukv, rng((2 * j) * 128, 64) + rng((2 * j + 1) * 128, 64))
    vc = []
    for h in range(8):
        vc += rng(h * 128 + 64, 64)
    cols["uv"] = (ukv, vc)
    uq = inp["mla_w_uq"][l]
    for h in range(8):
        cols[f"uq{h}"] = (uq, rng(h * 96, 96) + rng(h * 96 + 80, 16) + rng(h * 96 + 64, 16))
    for h in range(4):
        for nm in ("hf", "hq", "hi", "hg"):
            cols[f"{nm}{h}"] = (w_in, rng(IN_OFF[nm] + h * 128, 128))
    for o in range(8):
        cols[f"pa{o}"] = (inp["w_proj_a"][l], rng(o * 128, 128))
        cols[f"pb{o}"] = (inp["w_proj_b"][l], rng(o * 128, 128))
        cols[f"ga{o}"] = (w_in, rng(IN_OFF["ga"] + o * 128, 128))
        cols[f"gb{o}"] = (w_in, rng(IN_OFF["gb"] + o * 128, 128))
        cols[f"wo{o}"] = (inp["w_out"][l], rng(o * 128, 128))
    for f in range(NF):
        cols[f"gate{f}"] = (inp["w_gate"][l], rng(f * 128, 128))
        cols[f"up{f}"] = (inp["w_up"][l], rng(f * 128, 128))
    wd = inp["w_down"][l]
    for o in range(8):
        cols[f"dna{o}"] = (wd[0:11 * 128], rng(o * 128, 128))
        cols[f"dnb{o}"] = (wd[11 * 128:22 * 128], rng(o * 128, 128))
    groups, index, total = weight_groups()
    out = np.zeros((128, total), np.float32)
    for name, kc, m in weight_items():
        W, cl = cols[name]
        g, off, _, _ = index[name]
        base = groups[g][0] + off
        blk = W[:kc * 128][:, cl].reshape(kc, 128, m).transpose(1, 0, 2).reshape(128, kc * m)
        out[:, base:base + kc * m] = blk
    return out


def pack_vecs(inp, l):
    v = np.zeros((128, NV), np.float32)
    v[:, 0:8] = inp["norm_mix"][l].reshape(8, 128).T
    v[:, 8:16] = inp["norm_ffn"][l].reshape(8, 128).T
    v[:, 16:19] = inp["mla_norm_cq"][l].reshape(3, 128).T
    v[:, 19:21] = inp["mla_norm_ckv"][l].reshape(2, 128).T
    qn = inp["mla_q_norm"][l]
    kn = inp["mla_k_norm"][l]
    v[0:96, 21] = qn
    v[96:112, 21] = qn[80:96]
    v[112:128, 21] = qn[64:80]
    v[0:64, 22] = kn[0:64]
    v[64:128, 22] = kn[0:64]
    v[64:96, 23] = kn[64:96]
    v[96:112, 23] = kn[80:96]
    v[112:128, 23] = kn[64:80]
    v[:, 24] = inp["hg_out_norm"][l]
    for ll in range(inp["hg_lb_logits"].shape[0]):
        v[:, 25 + 4 * ll:29 + 4 * ll] = inp["hg_lb_logits"][ll].reshape(4, 128).T
    return v


def const_tables(T):
    pos = np.arange(T, dtype=np.float32)
    inv_freq = (1.0 / (np.float32(10000.0) ** (np.arange(0, 32, 2, dtype=np.float32) / np.float32(32)))).astype(np.float32)
    ang = (pos[:, None] * inv_freq[None, :]).astype(np.float32)
    cos = np.cos(ang).astype(np.float32).T
    sin = np.sin(ang).astype(np.float32).T
    cs = np.ones((128, T), np.float32)
    cs[64:80] = cos
    cs[80:96] = cos
    cs[96:112] = -sin
    cs[112:128] = sin
    k = np.arange(128)
    tri_att = (k[None, :] >= k[:, None]).astype(np.float32)
    ident = np.eye(128, dtype=np.float32)
    s64 = k % 64
    t64 = np.arange(64)
    tri_h = (t64[None, :] >= s64[:, None]).astype(np.float32)
    tri_h4 = np.tile(tri_h, (1, 4))
    scanmask = np.ones((128, TB), np.float32)
    scanmask[:, ::64] = 0.0
    misc = np.concatenate([tri_att, ident, tri_h4, scanmask], axis=1).astype(np.float32)
    return cs, misc


def build_program(NSEQ, T, L, dbg=False):
    NBLK = T // TB
    NT = T // 128
    groups, windex, WTOT = weight_groups()
    NG = len(groups)
    nc = bass.Bass("TRN2", target_bir_lowering=False)
    xT_d = nc.dram_tensor("xT", [NSEQ, D, T], F32, kind="ExternalInput").ap()
    NWS = 1
    WPC = WTOT // NWS
    wp_d = [[nc.dram_tensor(f"wpack{l}_{i}", [128, WPC], F32, kind="ExternalInput").ap() for i in range(NWS)]
            for l in range(L)]
    vec_d = nc.dram_tensor("vecs", [128, L * NV], F32, kind="ExternalInput").ap()
    cs_d = nc.dram_tensor("cs", [128, T], F32, kind="ExternalInput").ap()
    misc_d = nc.dram_tensor("misc", [128, 512 + TB], F32, kind="ExternalInput").ap()
    out_d = nc.dram_tensor("outT", [NSEQ, D, T], F32, kind="ExternalOutput").ap()
    import os as _os
    if _os.environ.get("KDUMMY"):
        nc.dram_tensor("dummyin", [128, WTOT], F32, kind="ExternalInput")
    wbf_v = [wp_d[l][0].bitcast(BF16) for l in range(L)]

    with ExitStack() as st:
        S = Sched(nc, st)

        def sb(name, shape, dt):
            return st.enter_context(nc.sbuf_tensor(name, shape, dt))

        xT = sb("xT_sb", [128, KC, T], F32)
        XT = [[Buf(f"xT{c}_{b}") for b in range(NBLK)] for c in range(KC)]
        cs = sb("cs_sb", [128, T], F32); CS = Buf("cs")
        KT = sb("KT", [128, 8, T], BF16)
        KTB = [Buf(f"KT{h}") for h in range(8)]
        Vx = sb("Vx", [128, NT, 4, 192], BF16)
        VXB = [Buf(f"Vx{i}") for i in range(NT)]
        VXONES = Buf("vxones")
        skall = sb("skall", [128, NT * 8], F32)
        SK = [Buf(f"sk{b}") for b in range(NBLK)]
        wslot = [sb(f"wslot{i}", [128, SLOT], BF16) for i in range(NSLOT)]
        WS = [Buf(f"wslot{i}") for i in range(NSLOT)]
        vec = sb("vec_sb", [128, L * NV], F32); VEC = Buf("vec")
        cbf = sb("cbf", [128, 128 * 8], BF16); CBF = Buf("cbf")
        cvec = sb("cvec", [128, 8], F32); CVEC = Buf("cvec")
        lbt = sb("lbt", [128, L * 4], F32); LBT = Buf("lbt")
        lbtmp = sb("lbtmp", [128, 40], F32); LBTMP = Buf("lbtmp")
        gqs = sb("gqs", [128, L], F32); GQS = Buf("gqs")
        Sst = sb("Sst", [128, 4, 2, 128], F32)
        SST = [[Buf(f"S{h}_{i}") for i in range(2)] for h in range(4)]
        Sbf = sb("Sbf", [128, 8, 128], BF16)
        SBF = [Buf(f"Sbf{i}") for i in range(8)]
        hT = sb("hT", [128, KC, TB], BF16)
        HT = [Buf(f"hT{c}") for c in range(KC)]
        NSL = 22
        bfs = sb("bfs", [128, NSL, TB], BF16)
        BFS = [Buf(f"bfs{i}") for i in range(NSL)]
        yT, YT = bfs, BFS
        sqk = bfs[:, 8:13, :]; SQK = BFS[8:13]
        cqn = bfs[:, 0:3, :]; CQN = BFS[0:3]
        ckvn = bfs[:, 3:5, :]; CKVN = BFS[3:5]
        qTh = [bfs[:, 5 + i, :] for i in range(2)]; QTH = BFS[5:7]
        pT = [bfs[:, 7 + i, :] for i in range(3)]; PT = BFS[7:10]
        qp = bfs[:, 10, :]; QP = BFS[10]
        kt_ = bfs[:, 11, :]; KTL = BFS[11]
        kpT = bfs[:, 12, :]; KPT = BFS[12]
        attnT = bfs[:, 13:17, :]; ATT = BFS[13:17]
        ogT = bfs[:, 17:21, :]; OGT = BFS[17:21]
        actT, ACTT = bfs, BFS
        sq = [sb(f"sq{i}", [128, TB], BF16) for i in range(2)]
        SQ = [Buf(f"sq{i}") for i in range(2)]
        rstd = sb("rstd", [128, TB], F32); RSTD = Buf("rstd")
        ft = [sb(f"ft{i}", [128, TB], F32) for i in range(6)]
        FT = [Buf(f"ft{i}") for i in range(6)]
        cqf = [ft[0], ft[1], ft[2]]; CQF = FT[0:3]
        rk, RK = ft[4], FT[4]
        rk2, RK2 = ft[5], FT[5]
        kptok = sb("kptok", [128, 4, 128], BF16); KPTOK = Buf("kptok")
        vtok = sb("vtok", [128, 4, 128], BF16); VTOK = Buf("vtok")
        asb2 = sb("asb2", [128, 4, 64], BF16); ASB2 = [Buf(f"asb2_{i}") for i in range(4)]
        dec = sb("dec", [128, 8], F32); DEC = Buf("dec")
        kst_sb = sb("kst_sb", [128, 32], F32); KSTSB = Buf("kstsb")
        scanm = sb("scanm", [128, TB], F32); SCANM = Buf("scanm")
        trih = sb("trih", [128, 256], BF16); TRIH = Buf("trih")

        NRING = 4
        pring = [st.enter_context(nc.psum_tensor(f"pr{i}", [128, 512], F32)) for i in range(NRING)]
        PR = [Buf(f"pr{i}") for i in range(NRING)]
        pstat = st.enter_context(nc.psum_tensor("pstat", [128, 512], F32)); PSTAT = Buf("pstat")
        pacc = [st.enter_context(nc.psum_tensor(f"pacc{i}", [128, 512], F32)) for i in range(2)]
        PACC = [Buf(f"pacc{i}") for i in range(2)]
        ptr = st.enter_context(nc.psum_tensor("ptr", [128, 1024], BF16)); PTR = Buf("ptr")
        ring_i = [0]

        def ring():
            i = ring_i[0] % NRING
            ring_i[0] += 1
            return pring[i], PR[i]

        def ACT(out, in_, func, reads, writes, scale=None, bias=None):
            kw = {}
            if scale is not None:
                kw["scale"] = scale
            if bias is not None:
                kw["bias"] = bias
            S.op("act", lambda: nc.scalar.activation(out=out, in_=in_, func=func, **kw), reads, writes)

        def TT(eng, out, in0, in1, op, reads, writes):
            e = nc.vector if eng == "dve" else nc.gpsimd
            S.op(eng, lambda: e.tensor_tensor(out=out, in0=in0, in1=in1, op=op), reads, writes)

        def STT(out, in0, scalar, in1, op0, op1, reads, writes):
            S.op("dve", lambda: nc.vector.scalar_tensor_tensor(out=out, in0=in0, scalar=scalar, in1=in1,
                                                               op0=op0, op1=op1), reads, writes)

        def TS(eng, out, in0, s1, s2, op0, op1, reads, writes):
            e = nc.vector if eng == "dve" else nc.gpsimd
            if s2 is None:
                S.op(eng, lambda: e.tensor_scalar(out=out, in0=in0, scalar1=s1, scalar2=None, op0=op0), reads, writes)
            else:
                S.op(eng, lambda: e.tensor_scalar(out=out, in0=in0, scalar1=s1, scalar2=s2, op0=op0, op1=op1),
                     reads, writes)

        def CP(eng, out, in_, reads, writes):
            e = nc.vector if eng == "dve" else nc.gpsimd
            S.op(eng, lambda: e.tensor_copy(out=out, in_=in_), reads, writes)

        def MM(out, lhsT, rhs, start, stop, reads, writes):
            S.op("pe", lambda: nc.tensor.matmul(out, lhsT=lhsT, rhs=rhs, start=start, stop=stop), reads, writes)

        def RECIP(out, in_, reads, writes):
            S.op("dve", lambda: nc.vector.reciprocal(out=out, in_=in_), reads, writes)

        def SCAN(out, d0, d1, reads, writes):
            S.op("dve", lambda: nc.vector.tensor_tensor_scan(out=out, data0=d0, data1=d1, initial=0.0,
                                                             op0=ALU.mult, op1=ALU.add), reads, writes)

        def TRANS(out, in_, ident, reads, writes):
            S.op("pe", lambda: nc.tensor.transpose(out, in_, ident), reads, writes)

        def MSET(ap, val, writes):
            S.op("pool", lambda: nc.gpsimd.memset(ap, val), (), writes)

        S.dma("sp", lambda: nc.sync.dma_start(out=vec[:], in_=vec_d), writes=[VEC], key="vec")
        S.dma("sp", lambda: nc.sync.dma_start(out=cs[:], in_=cs_d), writes=[CS], key="cs")
        S.dma("sp", lambda: nc.sync.dma_start(out=ft[0][:], in_=misc_d[:, 0:512]), writes=[FT[0]], key="misc")
        S.dma("sp", lambda: nc.sync.dma_start(out=scanm[:], in_=misc_d[:, 512:512 + TB]), writes=[SCANM], key="scanm")
        ONES_D, ONES_CQ, ONES_CKV, ONES_Q, ONES_O, ONES_K, TRI, IDN = range(8)

        def cm(i):
            return cbf[:, i * 128:(i + 1) * 128]
        MSET(cm(ONES_D), 1.0 / 1024, [CBF])
        MSET(cm(ONES_CQ), 1.0 / 384, [CBF])
        MSET(cm(ONES_CKV), 1.0 / 256, [CBF])
        MSET(cm(ONES_Q), 1.0 / 96, [CBF])
        MSET(cbf[96:128, ONES_Q * 128:(ONES_Q + 1) * 128], 0.0, [CBF])
        MSET(cm(ONES_O), 1.0 / 128, [CBF])
        MSET(cm(ONES_K), 0.0, [CBF])
        MSET(cbf[64:96, ONES_K * 128:ONES_K * 128 + 8], 1.0 / 96, [CBF])
        MSET(cbf[0:64, ONES_K * 128 + 8:ONES_K * 128 + 9], 1.0 / 96, [CBF])
        MSET(cbf[64:128, ONES_K * 128 + 9:ONES_K * 128 + 10], 1.0 / 96, [CBF])
        for i_ in range(NSL):
            MSET(bfs[:, i_, :], 0.0, [BFS[i_]])
        CP("pool", cm(TRI), ft[0][:, 0:128], [FT[0]], [CBF])
        CP("pool", cm(IDN), ft[0][:, 128:256], [FT[0]], [CBF])
        CP("pool", trih[:], ft[0][:, 256:512], [FT[0]], [TRIH])
        MSET(cvec[:, 0:1], EPS, [CVEC])
        MSET(cvec[:, 1:2], 1.0, [CVEC])
        MSET(cvec[:, 2:3], 0.0, [CVEC])
        C_EPS, C_ONE, C_ZERO = cvec[:, 0:1], cvec[:, 1:2], cvec[:, 2:3]
        MSET(Vx[:, :, :, 64:128], 1.0, [VXONES])
        ll = [vec[:, l * NV + 25: l * NV + 25 + 4 * L] for l in range(1)][0]
        mx, e_all, ssum, rs, cum = lbtmp[:, 0:4], lbtmp[:, 4:4 + 4 * L], lbtmp[:, 16:20], lbtmp[:, 20:24], lbtmp[:, 24:28]
        CP("dve", mx, ll[:, 0:4], [VEC], [LBTMP])
        for l in range(1, L):
            TT("dve", mx, mx, ll[:, 4 * l:4 * l + 4], ALU.max, [VEC, LBTMP], [LBTMP])
        for l in range(L):
            TT("dve", e_all[:, 4 * l:4 * l + 4], ll[:, 4 * l:4 * l + 4], mx, ALU.subtract, [VEC, LBTMP], [LBTMP])
        ACT(e_all, e_all, AF.Exp, [LBTMP], [LBTMP])
        CP("dve", ssum, e_all[:, 0:4], [LBTMP], [LBTMP])
        for l in range(1, L):
            TT("dve", ssum, ssum, e_all[:, 4 * l:4 * l + 4], ALU.add, [LBTMP], [LBTMP])
        RECIP(rs, ssum, [LBTMP], [LBTMP])
        for l in range(L):
            TT("dve", e_all[:, 4 * l:4 * l + 4], e_all[:, 4 * l:4 * l + 4], rs, ALU.mult, [LBTMP], [LBTMP])
        CP("dve", cum, e_all[:, 0:4], [LBTMP], [LBTMP])
        for l in range(L):
            if l > 0:
                TT("dve", cum, cum, e_all[:, 4 * l:4 * l + 4], ALU.add, [LBTMP], [LBTMP])
            TT("dve", lbt[:, 4 * l:4 * l + 4], cum, e_all[:, 0:4], ALU.subtract, [LBTMP], [LBT])
            TS("dve", lbt[:, 4 * l:4 * l + 4], lbt[:, 4 * l:4 * l + 4], 0.0, None, ALU.max, ALU.bypass, [LBT], [LBT])
        for l in range(L):
            TS("dve", gqs[:, l:l + 1], vec[:, l * NV + 21:l * NV + 22], float(96.0 ** -0.5), None, ALU.mult, ALU.bypass,
               [VEC], [GQS])

        RPC = 4
        PCS = RPC * T
        while WTOT % PCS:
            RPC //= 2
            PCS = RPC * T
        NCHK = WTOT // PCS
        WCH = []
        jj = 0
        for l in range(L):
            row = []
            DCH = [Buf(f"dch{l}_{k}") for k in range(NCHK)]
            for k in range(NCHK):
                c0, c1 = k * PCS, (k + 1) * PCS
                hb = jj % (8 // RPC)
                r0 = hb * RPC
                jj += 1
                bq = Buf(f"wch{l}_{c0}")
                xbufs = [XT[c][bb_] for c in range(r0, r0 + RPC) for bb_ in range(NBLK)]
                kbufs = KTB[r0:r0 + RPC]
                xs_ = xT[:, r0:r0 + RPC, :]
                ks_ = KT[:, r0:r0 + RPC, :]

                def ld(l=l, c0=c0, c1=c1, xs_=xs_):
                    return nc.sync.dma_start(out=xs_, in_=wp_d[l][0][:, c0:c1].rearrange("p (r t) -> p r t", t=T))
                S.dma("sp", ld, reads=[DCH[k]], writes=xbufs, key=f"wld{hb}")
                for r in range(RPC):
                    eng = ("act", "dve", "pool")[(jj + r) % 3]
                    c = r0 + r
                    if eng == "act":
                        ACT(KT[:, c, :], xT[:, c, :], AF.Copy, XT[c], [KTB[c]])
                    else:
                        CP(eng, KT[:, c, :], xT[:, c, :], XT[c], [KTB[c]])

                def stf(l=l, c0=c0, c1=c1, ks_=ks_):
                    return nc.sync.dma_start(out=wbf_v[l][:, c0:c1].rearrange("p (r t) -> p r t", t=T), in_=ks_)
                S.dma("sp", stf, reads=kbufs, writes=[bq, DCH[k // 2]], key=f"wst{hb}")
                row.append((c0, c1, bq))
            WCH.append(row)
        MSET(KT[96:128, :, :], 0.0, KTB)

        class WStream:
            def __init__(self):
                self.seq = []
                self.pos = 0
                self.cur = -1
                self.curkey = None

            def plan(self, order):
                self.seq = order

            def _load(self, i):
                l, g = self.seq[i]
                off, size = groups[g]
                slot = i % NSLOT
                rd = [b for (c0, c1, b) in WCH[l] if c0 < off + size and c1 > off]

                def fn(l=l, off=off, size=size, slot=slot):
                    return nc.sync.dma_start(out=wslot[slot][:, 0:size], in_=wbf_v[l][:, off:off + size])
                S.dma("sp", fn, reads=rd, writes=[WS[slot]], key=f"ws{slot}")

            def start(self):
                for i in range(min(NSLOT, len(self.seq))):
                    self._load(i)
                self.pos = min(NSLOT, len(self.seq))
                self.cur = 0

            def get(self, l, name):
                g, off, kc, m = windex[name]
                while self.seq[self.cur] != (l, g):
                    self.cur += 1
                    if self.pos < len(self.seq) and self.pos - self.cur < NSLOT - 1 + 1:
                        pass
                    while self.pos < len(self.seq) and self.pos < self.cur + NSLOT:
                        self._load(self.pos)
                        self.pos += 1
                slot = self.cur % NSLOT
                ap = wslot[slot][:, off:off + kc * m].rearrange("p (k m) -> p k m", m=m)
                return ap, WS[slot]

        WST = WStream()
        order = []
        for s in range(NSEQ):
            for l in range(L):
                for b in range(NBLK):
                    for g in range(NG):
                        order.append((l, g))
        WST.plan(order)
        WST.start()

        def rmsstat_to_rstd(reads_extra=()):
            ACT(rstd[:], pstat[:], AF.Ln, [PSTAT, CVEC], [RSTD], bias=C_EPS)
            ACT(rstd[:], rstd[:], AF.Exp, [RSTD], [RSTD], scale=-0.5)

        def norm_x(l, b, gcol):
            cols = slice(b * TB, (b + 1) * TB)
            for c in range(KC):
                i = c % 2
                ACT(sq[i][:], xT[:, c, cols], AF.Square, [XT[c][b]], [SQ[i]])
                MM(pstat[:], cm(ONES_D), sq[i][:], c == 0, c == KC - 1, [CBF, SQ[i]], [PSTAT])
            rmsstat_to_rstd()
            for c in range(KC):
                STT(hT[:, c, :], xT[:, c, cols], vec[:, l * NV + gcol + c:l * NV + gcol + c + 1], rstd[:],
                    ALU.mult, ALU.mult, [XT[c][b], VEC, RSTD], [HT[c]])

        def proj_fm(l, name, rhs_ap, rhs_bufs, nk, M=128, out_rows=None):
            w, wb = WST.get(l, name)
            ps, psb = ring()
            o = ps[:] if out_rows is None else ps[out_rows[0]:out_rows[1], :]
            for k in range(nk):
                MM(o, w[:, k, :], rhs_ap(k), k == 0, k == nk - 1, [wb] + list(rhs_bufs), [psb])
            return ps, psb

        def mixer_block(l, b):
            cols = slice(b * TB, (b + 1) * TB)
            V0 = l * NV
            norm_x(l, b, 0)
            hrhs = lambda k: hT[:, k, :]
            pend = None
            for j in range(3):
                ps, psb = proj_fm(l, f"cq{j}", hrhs, HT, KC)
                if pend is not None:
                    MM(*pend)
                i = j % 2
                ACT(sq[i][:], ps[:], AF.Square, [psb], [SQ[i]])
                ACT(cqf[j][:], ps[:], AF.Copy, [psb], [CQF[j]])
                pend = (pstat[:], cm(ONES_CQ), sq[i][:], j == 0, j == 2, [CBF, SQ[i]], [PSTAT])
            MM(*pend)
            rmsstat_to_rstd()
            for j in range(3):
                STT(cqn[:, j, :], cqf[j][:], vec[:, V0 + 16 + j:V0 + 17 + j], rstd[:], ALU.mult, ALU.mult,
                    [CQF[j], VEC, RSTD], [CQN[j]])
            pend = None
            for j in range(2):
                ps, psb = proj_fm(l, f"ckv{j}", hrhs, HT, KC)
                if pend is not None:
                    MM(*pend)
                i = j % 2
                ACT(sq[i][:], ps[:], AF.Square, [psb], [SQ[i]])
                ACT(cqf[j][:], ps[:], AF.Copy, [psb], [CQF[j]])
                pend = (pstat[:], cm(ONES_CKV), sq[i][:], j == 0, j == 1, [CBF, SQ[i]], [PSTAT])
            MM(*pend)
            rmsstat_to_rstd()
            for j in range(2):
                STT(ckvn[:, j, :], cqf[j][:], vec[:, V0 + 19 + j:V0 + 20 + j], rstd[:], ALU.mult, ALU.mult,
                    [CQF[j], VEC, RSTD], [CKVN[j]])
            ps, psb = proj_fm(l, "kpe", hrhs, HT, KC, M=64, out_rows=(64, 128))
            ACT(sqk[64:128, 4, :], ps[64:128, :], AF.Square, [psb], [SQK[4]])
            STT(rk[64:128, :], ps[64:128, :], vec[64:128, V0 + 23:V0 + 24], cs[64:128, cols], ALU.mult, ALU.mult,
                [psb, VEC, CS], [RK])
            ACT(rk2[64:96, :], rk[96:128, :], AF.Copy, [RK], [RK2])
            TT("pool", rk[64:96, :], rk[64:96, :], rk2[64:96, :], ALU.add, [RK, RK2], [RK])
            for h in range(8):
                CP("pool", KT[64:96, h, cols], rk[64:96, :], [RK], [KTB[h]])
            ckrhs = lambda k: ckvn[:, k, :]
            for j in range(4):
                ps, psb = proj_fm(l, f"uk{j}", ckrhs, CKVN, 2)
                ACT(sqk[:, j, :], ps[:], AF.Square, [psb], [SQK[j]])
                ACT(KT[0:64, 2 * j, cols], ps[0:64, :], AF.Copy, [psb, VEC], [KTB[2 * j]], scale=vec[0:64, V0 + 22:V0 + 23])
                ACT(KT[0:64, 2 * j + 1, cols], ps[64:128, :], AF.Copy, [psb, VEC], [KTB[2 * j + 1]],
                    scale=vec[64:128, V0 + 22:V0 + 23])
            for i in range(4):
                tsl = slice(i * 128, (i + 1) * 128)
                MM(pstat[:, i * 8:(i + 1) * 8], sqk[:, 4, tsl], cbf[:, ONES_K * 128:ONES_K * 128 + 8], True, False,
                   [SQK[4], CBF], [PSTAT])
                for h in range(8):
                    sel = ONES_K * 128 + 8 + (h % 2)
                    MM(pstat[:, i * 8 + h:i * 8 + h + 1], sqk[:, h // 2, tsl],
                       cbf[:, sel:sel + 1], False, h == 7, [SQK[h // 2], CBF], [PSTAT])
            ACT(kst_sb[:, :], pstat[:, 0:32], AF.Ln, [PSTAT, CVEC], [KSTSB], bias=C_EPS)
            ACT(skall[:, b * 32:(b + 1) * 32], kst_sb[:, :], AF.Exp, [KSTSB], [SK[b]], scale=-0.5)
            wv, wvb = WST.get(l, "uv")
            for i in range(4):
                tsl = slice(i * 128, (i + 1) * 128)
                ps, psb = ring()
                for k in range(2):
                    MM(ps[:], ckvn[:, k, tsl], wv[:, k, :], k == 0, k == 1, CKVN + [wvb], [psb])
                pv = ps[:].rearrange("p (j e d) -> p j e d", e=2, d=64)
                ti = b * 4 + i
                CP("dve", Vx[:, ti, :, 0:64], pv[:, :, 0, :], [psb], [VXB[ti]])
                ACT(Vx[:, ti, :, 128:192], pv[:, :, 1, :], AF.Copy, [psb], [VXB[ti]])
            cqrhs = lambda k: cqn[:, k, :]
            def q_prep(h):
                ps, psb = proj_fm(l, f"uq{h}", cqrhs, CQN, 3)
                i = h % 2
                ACT(sq[i][:], ps[:], AF.Square, [psb], [SQ[i]])
                MM(pstat[:], cm(ONES_Q), sq[i][:], True, True, [CBF, SQ[i]], [PSTAT])
                rmsstat_to_rstd()
                STT(ft[0][:], ps[:], gqs[:, l:l + 1], rstd[:], ALU.mult, ALU.mult, [psb, GQS, RSTD], [FT[0]])
                TT("pool", ft[1][:], ft[0][:], cs[:, cols], ALU.mult, [FT[0], CS], [FT[1]])
                qt, QB = qTh[h % 2], QTH[h % 2]
                CP("pool", qt[0:64, :], ft[1][0:64, :], [FT[1]], [QB])
                ACT(ft[2][64:96, :], ft[1][96:128, :], AF.Copy, [FT[1]], [FT[2]])
                TT("pool", qt[64:96, :], ft[1][64:96, :], ft[2][64:96, :], ALU.add, [FT[1], FT[2]], [QB])
                CP("pool", qt[96:128, :], ft[1][96:128, :], [FT[1]], [QB])

            def attn(h):
                qt, QB = qTh[h % 2], QTH[h % 2]
                j, par = h // 2, h % 2
                po, POB = pacc[h % 2], PACC[h % 2]
                nkt = 4 * b + 4
                pend = None
                for kt in range(nkt):
                    r = kt - 4 * b
                    q0 = max(0, r) * 128
                    nq = TB - q0
                    ps2, ps2b = ring()
                    MM(ps2[:, 0:nq], KT[:, h, kt * 128:(kt + 1) * 128], qt[:, q0:TB], True, True,
                       [KTB[h], QB], [ps2b])
                    pt, PTB = pT[kt % 3], PT[kt % 3]
                    ACT(pt[:, 0:nq], ps2[:, 0:nq], AF.Exp, [ps2b, SK[kt // 4]], [PTB],
                        scale=skall[:, kt * 8 + h:kt * 8 + h + 1])
                    if r >= 0:
                        TT("pool", pt[:, 0:128], pt[:, 0:128], cm(TRI), ALU.mult, [PTB, CBF], [PTB])
                    if pend is not None:
                        MM(*pend)
                    pend = (po[:, q0:TB], Vx[:, kt, j, par * 64:par * 64 + 128], pt[:, 0:nq], kt == 0, kt == nkt - 1,
                            [VXB[kt], VXONES, PTB], [POB])
                MM(*pend)
                if par == 0:
                    RECIP(ft[3][64:128, :], po[64:128, :], [POB], [FT[3]])
                    TT("dve", attnT[0:64, j, :], po[0:64, :], ft[3][64:128, :], ALU.mult, [POB, FT[3]], [ATT[j]])
                else:
                    RECIP(ft[3][0:64, :], po[0:64, :], [POB], [FT[3]])
                    TT("dve", attnT[64:128, j, :], po[64:128, :], ft[3][0:64, :], ALU.mult, [POB, FT[3]], [ATT[j]])

            q_prep(0)
            for h in range(8):
                if h < 7:
                    q_prep(h + 1)
                attn(h)
            for h in range(4):
                lbc = lbt[:, 4 * l + h:4 * l + h + 1]
                ps, psb = proj_fm(l, f"hf{h}", hrhs, HT, KC)
                e, l1, l2, bb, kk, tmp = ft[0], ft[1], ft[2], ft[3], ft[4], ft[5]
                E, L1, L2, BB, KK, TMP = FT[0], FT[1], FT[2], FT[3], FT[4], FT[5]
                ACT(e[:], ps[:], AF.Exp, [psb], [E], scale=-1.0)
                ACT(l1[:], e[:], AF.Ln, [E, LBT, CVEC], [L1], scale=lbc, bias=C_ONE)
                ACT(l2[:], e[:], AF.Ln, [E, CVEC], [L2], bias=C_ONE)
                TT("pool", l1[:], l1[:], l2[:], ALU.subtract, [L1, L2], [L1])
                SCAN(bb[:], scanm[:], l1[:], [SCANM, L1], [BB])
                ACT(kk[:], l1[:], AF.Exp, [L1], [KK])
                TS("pool", kk[:], kk[:], -1.0, 1.0, ALU.mult, ALU.add, [KK], [KK])
                b3 = bb[:].rearrange("p (c j) -> p c j", j=64)
                ACT(dec[:, :], b3[:, :, 63], AF.Exp, [BB], [DEC])
                ACT(e[:], bb[:], AF.Exp, [BB], [E])
                ACT(l2[:], bb[:], AF.Exp, [BB], [L2], scale=-1.0)
                TT("pool", tmp[:].rearrange("p (c j) -> p c j", j=64), b3, b3[:, :, 63:64].to_broadcast([128, 8, 64]),
                   ALU.subtract, [BB], [TMP])
                ACT(tmp[:], tmp[:], AF.Exp, [TMP], [TMP], scale=-1.0)
                TT("pool", kt_[:], kk[:], l2[:], ALU.mult, [KK, L2], [KTL])
                TT("pool", kpT[:], kk[:], tmp[:], ALU.mult, [KK, TMP], [KPT])
                ps, psb = proj_fm(l, f"hq{h}", hrhs, HT, KC)
                TT("dve", qp[:], ps[:], e[:], ALU.mult, [psb, E], [QP])
                wv, wvb = WST.get(l, f"hi{h}")
                ps, psb = ring()
                for i in range(4):
                    for k in range(KC):
                        MM(ps[:, i * 128:(i + 1) * 128], hT[:, k, i * 128:(i + 1) * 128], wv[:, k, :], k == 0, k == KC - 1,
                           HT + [wvb], [psb])
                CP("dve", vtok[:].rearrange("p i d -> p (i d)"), ps[:], [psb], [VTOK])
                for i in range(4):
                    TRANS(ptr[:, i * 128:(i + 1) * 128], kpT[:, i * 128:(i + 1) * 128], cm(IDN), [KPT, CBF], [PTR])
                CP("dve", kptok[:].rearrange("p i d -> p (i d)"), ptr[:, 0:512], [PTR], [KPTOK])
                for i in range(4):
                    ps, psb = ring()
                    for par in range(2):
                        c = 2 * i + par
                        csl = slice(c * 64, (c + 1) * 64)
                        MM(ps[par * 64:(par + 1) * 64, 0:64], kt_[:, csl], qp[:, csl], True, True, [KTL, QP], [psb])
                    TT("dve", asb2[:, i, :], ps[:, 0:64], trih[:, 0:64], ALU.mult, [psb, TRIH], [ASB2[i]])
                ubank = [ring(), ring()]
                for c in range(8):
                    i, par = c // 2, c % 2
                    rows = slice(par * 64, (par + 1) * 64)
                    psu, psub = ubank[c % 2]
                    MM(psu[:, (c // 2) * 128:(c // 2 + 1) * 128], kptok[rows, i, :], vtok[rows, i, :], True, True,
                       [KPTOK, VTOK], [psub])
                for c in range(8):
                    gi = b * 8 + c
                    cur, nxt = gi % 2, (gi + 1) % 2
                    psu, psub = ubank[c % 2]
                    ACT(Sbf[:, c, :], Sst[:, h, cur, :], AF.Copy, [SST[h][cur]], [SBF[c]])
                    STT(Sst[:, h, nxt, :], Sst[:, h, cur, :], dec[:, c:c + 1], psu[:, (c // 2) * 128:(c // 2 + 1) * 128],
                        ALU.mult, ALU.add, [SST[h][cur], DEC, psub], [SST[h][nxt]])
                ps, psb = proj_fm(l, f"hg{h}", hrhs, HT, KC)
                ACT(ft[1][:], ps[:], AF.Silu, [psb], [FT[1]])
                po, POB = pacc[h % 2], PACC[h % 2]
                for c in range(8):
                    i, par = c // 2, c % 2
                    rows = slice(par * 64, (par + 1) * 64)
                    csl = slice(c * 64, (c + 1) * 64)
                    MM(po[:, csl], Sbf[:, c, :], qp[:, csl], True, False, [SBF[c], QP], [POB])
                    MM(po[:, csl], vtok[rows, i, :], asb2[rows, i, :], False, True, [VTOK, ASB2[i]], [POB])
                ACT(sq[0][:], po[:], AF.Square, [POB], [SQ[0]])
                MM(pstat[:], cm(ONES_O), sq[0][:], True, True, [CBF, SQ[0]], [PSTAT])
                rmsstat_to_rstd()
                STT(ft[0][:], po[:], vec[:, V0 + 24:V0 + 25], rstd[:], ALU.mult, ALU.mult, [POB, VEC, RSTD], [FT[0]])
                TT("pool", ogT[:, h, :], ft[0][:], ft[1][:], ALU.mult, [FT[0], FT[1]], [OGT[h]])
            for o in range(8):
                pga, pgab = proj_fm(l, f"ga{o}", hrhs, HT, KC)
                ACT(ft[0][:], pga[:], AF.Sigmoid, [pgab], [FT[0]])
                pgb, pgbb = proj_fm(l, f"gb{o}", hrhs, HT, KC)
                ACT(ft[2][:], pgb[:], AF.Sigmoid, [pgbb], [FT[2]])
                psa, psab = proj_fm(l, f"pa{o}", lambda k: attnT[:, k, :], ATT, 4)
                TT("dve", ft[1][:], psa[:], ft[0][:], ALU.mult, [psab, FT[0]], [FT[1]])
                psb_, psbb = proj_fm(l, f"pb{o}", lambda k: ogT[:, k, :], OGT, 4)
                TT("dve", ft[3][:], psb_[:], ft[2][:], ALU.mult, [psbb, FT[2]], [FT[3]])
                TT("pool", yT[:, o, :], ft[1][:], ft[3][:], ALU.add, [FT[1], FT[3]], [YT[o]])
            for o in range(8):
                ps, psb = proj_fm(l, f"wo{o}", lambda k: yT[:, k, :], YT, KC)
                TT("dve", xT[:, o, cols], ps[:], xT[:, o, cols], ALU.add, [psb, XT[o][b]], [XT[o][b]])

        def ffn_block(l, b):
            cols = slice(b * TB, (b + 1) * TB)
            norm_x(l, b, 8)
            hrhs = lambda k: hT[:, k, :]
            for f in range(NF):
                pg, pgb = proj_fm(l, f"gate{f}", hrhs, HT, KC)
                pu, pub = proj_fm(l, f"up{f}", hrhs, HT, KC)
                i = f % 2
                ACT(ft[i][:], pg[:], AF.Silu, [pgb], [FT[i]])
                TT("dve", actT[:, f, :], pu[:], ft[i][:], ALU.mult, [pub, FT[i]], [ACTT[f]])
            for o in range(8):
                wa, wab = WST.get(l, f"dna{o}")
                ps, psb = ring()
                for k in range(11):
                    MM(ps[:], wa[:, k, :], actT[:, k, :], k == 0, False, [wab, ACTT[k]], [psb])
                wb_, wbb = WST.get(l, f"dnb{o}")
                for k in range(11):
                    MM(ps[:], wb_[:, k, :], actT[:, 11 + k, :], False, k == 10, [wbb, ACTT[11 + k]], [psb])
                TT("dve", xT[:, o, cols], ps[:], xT[:, o, cols], ALU.add, [psb, XT[o][b]], [XT[o][b]])

        OUTB = Buf("out")
        for s in range(NSEQ):
            for b in range(NBLK):
                cols = slice(b * TB, (b + 1) * TB)
                def fn(s=s, cols=cols):
                    return nc.sync.dma_start(out=xT[:, :, cols], in_=xT_d[s, :, cols].rearrange("(c p) t -> p c t", p=128))
                S.dma("sp", fn, writes=[XT[c][b] for c in range(KC)], key=f"x_{b}")
            for l in range(L):
                for h in range(4):
                    MSET(Sst[:, h, 0, :], 0.0, [SST[h][0]])
                for b in range(NBLK):
                    mixer_block(l, b)
                    ffn_block(l, b)
                    if l == L - 1:
                        cols = slice(b * TB, (b + 1) * TB)
                        def fn(s=s, cols=cols):
                            return nc.sync.dma_start(out=out_d[s, :, cols].rearrange("(c p) t -> p c t", p=128), in_=xT[:, :, cols])
                        S.dma("sp", fn, reads=[XT[c][b] for c in range(KC)], writes=[OUTB], key=f"x_{b}")
        S.final_wait("sp", [OUTB] + [XT[c][b] for c in range(KC) for b in range(NBLK)])
        S.emit()
    return nc


def _prep_shared(inputs, L, T):
    wpack = np.stack([pack_layer_weights(inputs, l) for l in range(L)], axis=0)
    vecs = np.concatenate([pack_vecs(inputs, l) for l in range(L)], axis=1)
    cs, misc = const_tables(T)
    return wpack, np.ascontiguousarray(vecs), cs, misc


def make_wmap(wpack):
    L, _, WTOT = wpack.shape
    NWS = 1
    WPC = WTOT // NWS
    return {f"wpack{l}_{i}": np.ascontiguousarray(wpack[l, :, i * WPC:(i + 1) * WPC]) for l in range(L) for i in range(NWS)}


def kernel(**inputs):
    inputs = {k: np.asarray(v) for k, v in inputs.items()}
    x = inputs["x"]
    B, T, _ = x.shape
    L = inputs["w_in"].shape[0]
    nseq = B // NCORES
    wpack, vecs, cs, misc = _prep_shared(inputs, L, T)
    wmap = make_wmap(wpack)
    nc = build_program(nseq, T, L)
    in_maps = []
    for c in range(NCORES):
        xs = np.ascontiguousarray(x[c * nseq:(c + 1) * nseq].transpose(0, 2, 1))
        in_maps.append(dict(wmap, xT=xs, vecs=vecs, cs=cs, misc=misc))
    res = run_bass_kernel_spmd(nc, in_maps, core_ids=list(range(NCORES)))
    outs = [np.asarray(r["outT"]).transpose(0, 2, 1) for r in res.results]
    return np.ascontiguousarray(np.concatenate(outs, axis=0)).astype(np.float32)
```
